# Optimizing a Trainium2 kernel written in Bass

```python
import jax, jax.numpy as jnp
from jax import lax
import numpy as np

D_MODEL = 1024
BATCH = 8
SEQ = 4096
DEPTH = 4

N_BRANCH = 4
BRANCH_WIDTH = 512
EPS = 1e-6
Q_BLOCK = 128
GM_GROUPS = 4
GM_CHUNK = 128
GM_GROUP_DIM = BRANCH_WIDTH // GM_GROUPS
DSA_HEADS = 4
DSA_HEAD_DIM = BRANCH_WIDTH // DSA_HEADS
DSA_LATENT = 128
IDX_HEADS = 4
IDX_DIM = 64
TOPK_MAX = 256
CONV_WIDTH = 3
FOX_HEADS = 4
FOX_HEAD_DIM = BRANCH_WIDTH // FOX_HEADS

W = BRANCH_WIDTH
IN_SPLITS = (
    W, W, W,
    W, DSA_LATENT, IDX_HEADS * IDX_DIM, IDX_DIM, IDX_HEADS, W,
    W, W, W, W,
    W, W, W, FOX_HEADS, W,
    N_BRANCH * D_MODEL,
)
IN_WIDTH = sum(IN_SPLITS)

kernel_name = "hybrid_parallel_gated_mixers"


def rms_norm(x, g):
    xf = x.astype(jnp.float32)
    y = xf * lax.rsqrt(jnp.mean(xf * xf, axis=-1, keepdims=True) + EPS)
    return (y * g.astype(jnp.float32)).astype(x.dtype)


def layer_norm(x, g, b):
    xf = x.astype(jnp.float32)
    mu = jnp.mean(xf, axis=-1, keepdims=True)
    var = jnp.mean(jnp.square(xf - mu), axis=-1, keepdims=True)
    y = (xf - mu) * lax.rsqrt(var + EPS)
    return (y * g.astype(jnp.float32) + b.astype(jnp.float32)).astype(x.dtype)


def to_blocks(a):
    b, s = a.shape[0], a.shape[1]
    return jnp.moveaxis(a.reshape(b, s // Q_BLOCK, Q_BLOCK, *a.shape[2:]), 1, 0)


def from_blocks(a):
    nb, b = a.shape[0], a.shape[1]
    return jnp.moveaxis(a, 0, 1).reshape(b, nb * Q_BLOCK, a.shape[-1])


def chunked_spatial_gating(u, v, ln_g, ln_b, w_s, b_s):
    bsz, s, _ = v.shape
    v = layer_norm(v, ln_g, ln_b)
    vc = v.reshape(bsz, s // GM_CHUNK, GM_CHUNK, GM_GROUPS, GM_GROUP_DIM)
    mask = jnp.tril(jnp.ones((GM_CHUNK, GM_CHUNK), dtype=bool))
    w = jnp.where(mask[None], w_s, jnp.zeros_like(w_s))
    mixed = jnp.einsum('gts,bcsge->bctge', w, vc) + jnp.transpose(b_s)[None, None, :, :, None]
    return u * mixed.reshape(bsz, s, W)


def dsa_attention(q, c_kv, q_idx, k_idx, w_idx, kv_g, w_uk, w_uv):
    bsz, s, _ = q.shape
    k_sel = min(TOPK_MAX, s // 4)
    c = rms_norm(c_kv, kv_g)
    qh = q.reshape(bsz, s, DSA_HEADS, DSA_HEAD_DIM)
    q_lat = jnp.einsum('bshd,hld->bshl', qh, w_uk)
    qi = q_idx.reshape(bsz, s, IDX_HEADS, IDX_DIM)
    wi = w_idx * (IDX_HEADS ** -0.5)
    key_pos = jnp.arange(s)
    gather = jax.vmap(lambda cb, ib: cb[ib])

    def block(args):
        qlb, qib, wib, blk = args
        pos_q = blk * Q_BLOCK + jnp.arange(Q_BLOCK)
        dots = jnp.einsum('bqhd,bsd->bqhs', qib, k_idx).astype(jnp.float32) * (IDX_DIM ** -0.5)
        score = jnp.einsum('bqh,bqhs->bqs', wib.astype(jnp.float32), jax.nn.relu(dots))
        causal = key_pos[None, :] <= pos_q[:, None]
        score = jnp.where(causal[None], score, -jnp.inf)
        _, idx = lax.top_k(score, k_sel)
        valid = idx <= pos_q[None, :, None]
        c_sel = gather(c, idx)
        logits = jnp.einsum('bqhl,bqkl->bqhk', qlb, c_sel).astype(jnp.float32) * (DSA_HEAD_DIM ** -0.5)
        logits = jnp.where(valid[:, :, None, :], logits, -jnp.inf)
        p = jax.nn.softmax(logits, axis=-1).astype(c.dtype)
        o_lat = jnp.einsum('bqhk,bqkl->bqhl', p, c_sel)
        o = jnp.einsum('bqhl,hld->bqhd', o_lat, w_uv)
        return o.reshape(bsz, Q_BLOCK, W)

    nb = s // Q_BLOCK
    out = lax.map(block, (to_blocks(q_lat), to_blocks(qi), to_blocks(wi), jnp.arange(nb)))
    return from_blocks(out)


def short_gated_conv(b_gate, c_gate, x_in, conv_w):
    s = x_in.shape[1]
    y = c_gate * x_in
    yp = jnp.pad(y, ((0, 0), (CONV_WIDTH - 1, 0), (0, 0)))
    conv = conv_w[0] * yp[:, 0:s]
    for j in range(1, CONV_WIDTH):
        conv = conv + conv_w[j] * yp[:, j:j + s]
    return b_gate * conv


def forgetting_attention(q, k, v, f_logit, b_f):
    bsz, s, _ = q.shape
    qh = q.reshape(bsz, s, FOX_HEADS, FOX_HEAD_DIM)
    kh = k.reshape(bsz, s, FOX_HEADS, FOX_HEAD_DIM)
    vh = v.reshape(bsz, s, FOX_HEADS, FOX_HEAD_DIM)
    cum = jnp.cumsum(jax.nn.log_sigmoid((f_logit + b_f).astype(jnp.float32)), axis=1)
    cum_k = jnp.transpose(cum, (0, 2, 1))
    key_pos = jnp.arange(s)

    def block(args):
        qb, cq, blk = args
        pos_q = blk * Q_BLOCK + jnp.arange(Q_BLOCK)
        logits = jnp.einsum('bqhd,bshd->bhqs', qb, kh).astype(jnp.float32) * (FOX_HEAD_DIM ** -0.5)
        logits = logits + jnp.transpose(cq, (0, 2, 1))[..., None] - cum_k[:, :, None, :]
        logits = jnp.where((key_pos[None, :] <= pos_q[:, None])[None, None], logits, -jnp.inf)
        p = jax.nn.softmax(logits, axis=-1).astype(vh.dtype)
        o = jnp.einsum('bhqs,bshd->bqhd', p, vh)
        return o.reshape(bsz, Q_BLOCK, W)

    nb = s // Q_BLOCK
    out = lax.map(block, (to_blocks(qh), to_blocks(cum), jnp.arange(nb)))
    return from_blocks(out)


def hybrid_layer(x, norm_g, w_in, gm_ln_g, gm_ln_b, gm_w_s, gm_b_s, dsa_kv_g, dsa_w_uk, dsa_w_uv,
                 conv_w, fox_b_f, w_branch, w_out):
    bsz, s, d = x.shape
    h = rms_norm(x, norm_g)
    proj = h @ w_in
    points = [int(p) for p in np.cumsum(np.array(IN_SPLITS))[:-1]]
    (a_u, a_v, a_z,
     b_q, b_c, b_qi, b_ki, b_wi, b_z,
     c_b, c_c, c_x, c_z,
     d_q, d_k, d_v, d_f, d_z,
     gates) = jnp.split(proj, points, axis=-1)
    y_a = chunked_spatial_gating(a_u, a_v, gm_ln_g, gm_ln_b, gm_w_s, gm_b_s) * jax.nn.silu(a_z)
    y_b = dsa_attention(b_q, b_c, b_qi, b_ki, b_wi, dsa_kv_g, dsa_w_uk, dsa_w_uv) * jax.nn.silu(b_z)
    y_c = short_gated_conv(c_b, c_c, c_x, conv_w) * jax.nn.silu(c_z)
    y_d = forgetting_attention(d_q, d_k, d_v, d_f, fox_b_f) * jax.nn.silu(d_z)
    ys = jnp.stack([y_a, y_b, y_c, y_d], axis=0)
    branch_d = jnp.einsum('nbsw,nwd->bsnd', ys, w_branch)
    g = jax.nn.sigmoid(gates.reshape(bsz, s, N_BRANCH, d))
    merged = jnp.sum(g * branch_d, axis=2)
    return x + merged @ w_out


def setup_inputs(seed: int = 0) -> dict:
    key = jax.random.key(seed)
    ks = jax.random.split(key, 16)
    f32 = jnp.float32
    nrm = lambda k, shp, sc: jax.random.normal(k, shp, f32) * sc
    return {
        'x': nrm(ks[0], (BATCH, SEQ, D_MODEL), 1.0),
        'norm_g': 1.0 + nrm(ks[1], (DEPTH, D_MODEL), 0.05),
        'w_in': nrm(ks[2], (DEPTH, D_MODEL, IN_WIDTH), D_MODEL ** -0.5),
        'gm_ln_g': 1.0 + nrm(ks[3], (DEPTH, W), 0.05),
        'gm_ln_b': nrm(ks[4], (DEPTH, W), 0.02),
        'gm_w_s': nrm(ks[5], (DEPTH, GM_GROUPS, GM_CHUNK, GM_CHUNK), GM_CHUNK ** -0.5),
        'gm_b_s': 1.0 + nrm(ks[6], (DEPTH, GM_GROUPS, GM_CHUNK), 0.1),
        'dsa_kv_g': 1.0 + nrm(ks[7], (DEPTH, DSA_LATENT), 0.05),
        'dsa_w_uk': nrm(ks[8], (DEPTH, DSA_HEADS, DSA_LATENT, DSA_HEAD_DIM), DSA_LATENT ** -0.5),
        'dsa_w_uv': nrm(ks[9], (DEPTH, DSA_HEADS, DSA_LATENT, DSA_HEAD_DIM), DSA_LATENT ** -0.5),
        'conv_w': nrm(ks[10], (DEPTH, CONV_WIDTH, W), CONV_WIDTH ** -0.5),
        'fox_b_f': jax.random.uniform(ks[11], (DEPTH, FOX_HEADS), f32, minval=1.0, maxval=4.0),
        'w_branch': nrm(ks[12], (DEPTH, N_BRANCH, W, D_MODEL), W ** -0.5),
        'w_out': nrm(ks[13], (DEPTH, D_MODEL, D_MODEL), 0.5 * D_MODEL ** -0.5),
        'final_g': 1.0 + nrm(ks[14], (D_MODEL,), 0.05),
    }


def reference(x, norm_g, w_in, gm_ln_g, gm_ln_b, gm_w_s, gm_b_s, dsa_kv_g, dsa_w_uk, dsa_w_uv,
              conv_w, fox_b_f, w_branch, w_out, final_g):
    h = x
    for l in range(DEPTH):
        h = hybrid_layer(h, norm_g[l], w_in[l], gm_ln_g[l], gm_ln_b[l], gm_w_s[l], gm_b_s[l],
                         dsa_kv_g[l], dsa_w_uk[l], dsa_w_uv[l], conv_w[l], fox_b_f[l],
                         w_branch[l], w_out[l])
    return rms_norm(h, final_g)
```

```python
import numpy as np
import concourse.bass as bass
import concourse.mybir as mybir
from concourse.bass_utils import run_bass_kernel_spmd

F32 = mybir.dt.float32
BF16 = mybir.dt.bfloat16
U8 = mybir.dt.uint8
ALU = mybir.AluOpType
AF = mybir.ActivationFunctionType

D = 1024
W = 512
INW = 11208
OFF = dict(a_u=0, a_v=512, a_z=1024, b_q=1536, b_misc=2048, b_z=2500,
           c_b=3012, c_c=3524, c_x=4036, c_z=4548,
           d_q=5060, d_k=5572, d_vf=6084, d_z=6600, gates=7112)
EPS = 1e-6
TOPK = 256
NITER = 13
NEG = -1.0e30
NP32 = 672
NPBF = 1536
NC32 = 648
NCBF = 512
SLOT = 4160
NWSLOT = 3
GRAN = 1024


class Buf:
    def __init__(self, ap, toks):
        self.ap = ap
        self.toks = toks


class Sched:
    COMPUTE = ('pe', 'act', 'dve', 'pool')

    def __init__(self):
        self.ops = []
        self.tok = {}
        self.ndma = {}

    def _flat(self, lst):
        out = []
        for x in lst:
            if x is None:
                continue
            if isinstance(x, Buf):
                out.extend(x.toks)
            elif isinstance(x, (list, tuple)) and len(x) > 0 and isinstance(x[0], (Buf, list)):
                out.extend(self._flat(x))
            else:
                out.append(x)
        return out

    def op(self, eng, fn, reads=(), writes=(), dma=False):
        idx = len(self.ops)
        reads = self._flat(reads)
        writes = self._flat(writes)
        key = ('dma', idx) if dma else eng
        deps = set()
        for t in reads:
            st = self.tok.setdefault(t, [dict(), dict()])
            for e, i in st[0].items():
                deps.add(i)
        for t in writes:
            st = self.tok.setdefault(t, [dict(), dict()])
            for e, i in st[0].items():
                if e == key and not dma:
                    continue
                deps.add(i)
            for e, i in st[1].items():
                if e == key and not dma:
                    continue
                deps.add(i)
        for t in reads:
            self.tok[t][1][key] = idx
        for t in writes:
            self.tok[t][0] = {key: idx}
            self.tok[t][1] = {}
        deps.discard(idx)
        self.ops.append(dict(eng=eng, fn=fn, deps=deps, dma=dma, sig=False))
        return idx

    def emit(self, nc):
        ops = self.ops
        for o in ops:
            for d in o['deps']:
                ops[d]['sig'] = True
        tick = {e: 0 for e in self.COMPUTE}
        sems = {e: nc.alloc_semaphore(name="s_" + e) for e in self.COMPUTE}
        NR = 20
        rings = {q: [nc.alloc_semaphore(name="d_%s_%d" % (q, i)) for i in range(NR)] for q in ('sp', 'pool')}
        dcount = {'sp': 0, 'pool': 0}
        for o in ops:
            if o['dma']:
                q = o['eng']
                n = dcount[q]
                dcount[q] += 1
                o['dsem'] = rings[q][n % NR]
                o['dval'] = 16 * (n // NR + 1)
                o['dprev'] = 16 * (n // NR)
            elif o['sig']:
                tick[o['eng']] += 1
                o['tick'] = tick[o['eng']]
        by_eng = {e: [] for e in ('pe', 'act', 'dve', 'pool', 'sp')}
        for i, o in enumerate(ops):
            by_eng[o['eng']].append(i)

        def run(ename, e):
            seen = {}
            for i in by_eng[ename]:
                o = ops[i]
                waits = {}
                for d in o['deps']:
                    p = ops[d]
                    if p['dma']:
                        s, v = p['dsem'], p['dval']
                    else:
                        s, v = sems[p['eng']], p['tick']
                    k = id(s)
                    if seen.get(k, 0) >= v:
                        continue
                    if k not in waits or waits[k][1] < v:
                        waits[k] = (s, v)
                if o['dma'] and o['dprev'] > 0:
                    s, v = o['dsem'], o['dprev']
                    k = id(s)
                    if seen.get(k, 0) < v and (k not in waits or waits[k][1] < v):
                        waits[k] = (s, v)
                for k, (s, v) in waits.items():
                    e.wait_ge(s, v)
                    seen[k] = v
                if o['fn'] is None:
                    continue
                ins = o['fn'](e)
                if o['dma']:
                    ins.then_inc(o['dsem'], 16)
                elif o['sig']:
                    ins.then_inc(sems[ename], 1)

        with nc.Block() as block:
            @block.tensor
            def _(e):
                run('pe', e)

            @block.scalar
            def _(e):
                run('act', e)

            @block.vector
            def _(e):
                run('dve', e)

            @block.gpsimd
            def _(e):
                run('pool', e)

            @block.sync
            def _(e):
                run('sp', e)


class Builder:
    def __init__(self, S, NL, final_norm=True, dbg=False):
        self.S, self.NL, self.final_norm, self.dbg = S, NL, final_norm, dbg
        self.NT = S // 512
        self.NCH = S // 128
        self.nc = bass.Bass("TRN2", target_bir_lowering=False)
        self.s = Sched()
        self.wplan = None
        self.wpos = 0
        self.ps_rr = 0
        self.ps_ring = [0, 1, 2, 3]
        self.ps_priv = {}
        self._alloc()

    def _alloc(self):
        nc, S, NL = self.nc, self.S, self.NL
        dt = nc.dram_tensor
        self.x_in = dt("x", [S, D], F32, kind="ExternalInput").ap()
        self.w_in = dt("w_in", [NL, D, INW], F32, kind="ExternalInput").ap()
        self.w_br = dt("w_br", [NL, 4, W, D], F32, kind="ExternalInput").ap()
        self.w_out = dt("w_out", [NL, D, D], F32, kind="ExternalInput").ap()
        self.pf32_d = dt("pf32", [NL, 128, NP32], F32, kind="ExternalInput").ap()
        self.pbf_d = dt("pbf", [NL, 128, NPBF], F32, kind="ExternalInput").ap()
        self.cf32_d = dt("cf32", [128, NC32], F32, kind="ExternalInput").ap()
        self.cbf_d = dt("cbf", [128, NCBF], F32, kind="ExternalInput").ap()
        self.fg_d = dt("fg", [128, D], F32, kind="ExternalInput").ap()
        self.out_d = dt("out", [S, D], F32, kind="ExternalOutput").ap()
        self.xs_d = [dt("xs%d" % i, [S, D], F32).ap() for i in range(2)]
        if self.dbg:
            self.dbg_d = dt("dbg", [7, S, D], F32, kind="ExternalOutput").ap()

        def sb(name, shape, dtype):
            t = nc.alloc_sbuf_tensor(name, shape, dtype)
            return Buf(t.ap() if hasattr(t, 'ap') else t[:], [name])
        self.sb = sb
        NCH = self.NCH
        self.kT = sb("kT", [128, 4, S], BF16)
        self.vC = sb("vC", [128, NCH, W], BF16)
        self.cTok = sb("cTok", [128, NCH, 128], BF16)
        self.cT = sb("cT", [128, S], BF16)
        self.kiT = sb("kiT", [128, S], BF16)
        self.cumneg = sb("cumneg", [128, NCH, 4], F32)
        self.xc = [sb("xc0", [128, D], F32)] * 2
        self.hT = sb("hT", [128, 8, 512], BF16)
        self.mg = sb("mg", [128, 8, 512], F32)
        self.yT = sb("yT", [128, 4, 512], BF16)
        self.wslot = [sb("wslot%d" % i, [128, SLOT], BF16) for i in range(NWSLOT)]
        self.pf32 = sb("pf32s", [128, NP32], F32)
        self.pbf = sb("pbfs", [128, NPBF], BF16)
        self.cf32 = sb("cf32s", [128, NC32], F32)
        self.cbf = sb("cbfs", [128, NCBF], BF16)
        self.biasm = sb("biasm", [128, 4, 128], F32)
        self.wTm = sb("wTm", [128, 4, 128], BF16)
        self.halo = sb("halo", [128, 4, 2], F32)
        self.small = sb("small", [128, 64], F32)
        self.bis = sb("bis", [128, 32], F32)
        self.wstat = sb("wstat", [128, 4, 8], F32)
        self.NSCR = 49 * 1024 + (16 * 1024 if self.dbg else 0)
        self.scr_t = nc.alloc_sbuf_tensor("scr", [128, self.NSCR // 2], BF16)
        self.scr_off = 0
        self.ps = []
        for i in range(8):
            t = nc.alloc_psum_tensor("ps%d" % i, [128, 512], F32)
            self.ps.append(Buf(t.ap() if hasattr(t, 'ap') else t[:], ["ps%d" % i]))

    def scr_reset(self):
        self.scr_off = getattr(self, 'scr_base', 0)

    def scr(self, shape, dtype, align=64):
        n = 1
        for d_ in shape[1:]:
            n *= d_
        nbytes = n * (4 if dtype == F32 else (1 if dtype == U8 else 2))
        off = (self.scr_off + align - 1) // align * align
        assert off + nbytes <= self.NSCR, ("scratch overflow", off, nbytes)
        self.scr_off = off + nbytes
        full = self.scr_t.ap() if hasattr(self.scr_t, 'ap') else self.scr_t[:]
        ap = full[:, off // 2:(off + nbytes) // 2]
        if dtype == F32:
            ap = ap.bitcast(F32)
        elif dtype == U8:
            ap = ap.bitcast(U8)
        if len(shape) == 3:
            ap = ap.rearrange("p (a b) -> p a b", a=shape[1])
        toks = [("scr", g) for g in range(off // GRAN, (off + nbytes - 1) // GRAN + 1)]
        return Buf(ap, toks)

    def psum(self, ring=None):
        if ring is not None:
            key = tuple(ring)
            n = self.ps_priv.get(key, 0)
            self.ps_priv[key] = n + 1
            return self.ps[ring[n % len(ring)]]
        ring = self.ps_ring
        b = self.ps[ring[self.ps_rr % len(ring)]]
        self.ps_rr += 1
        return b

    def interleave(self, *gens):
        gens = list(gens)
        while gens:
            for g in list(gens):
                try:
                    next(g)
                except StopIteration:
                    gens.remove(g)

    def wnext(self, desc):
        if self.wplan is None:
            self.wrec.append(desc)
            return self.wslot[(len(self.wrec) - 1) % NWSLOT]
        j = self.wpos
        assert self.wplan[j] == desc, (self.wplan[j], desc)
        self.wpos += 1
        nxt = j + NWSLOT - 2
        if nxt < len(self.wplan):
            self._wload(nxt)
        return self.wslot[j % NWSLOT]

    def _wload(self, j):
        kind, l, a, b = self.wplan[j]
        slot = self.wslot[j % NWSLOT]
        if kind == 'in':
            src = self.w_in[l, :, a:a + b].rearrange("(kc p) n -> p kc n", p=128)
            dst = slot.ap[:, 0:8 * b].rearrange("p (kc n) -> p kc n", kc=8)
        elif kind == 'br':
            src = self.w_br[l, a][:, b * 512:(b + 1) * 512].rearrange("(wc p) n -> p wc n", p=128)
            dst = slot.ap[:, 0:2048].rearrange("p (wc n) -> p wc n", wc=4)
        else:
            src = self.w_out[l, :, a:a + b].rearrange("(kc p) n -> p kc n", p=128)
            dst = slot.ap[:, 0:8 * b].rearrange("p (kc n) -> p kc n", kc=8)
        self.s.op('pool', lambda e, dst=dst, src=src: e.dma_start(out=dst, in_=src),
                  reads=[], writes=[slot], dma=True)

    def mm(self, out, lhsT, rhs, start, stop, reads, writes):
        self.s.op('pe', lambda e: e.matmul(out, lhsT, rhs, start=start, stop=stop, skip_group_check=True),
                  reads=reads, writes=writes)

    def act(self, out, in_, func, reads, writes, bias=None, scale=None, accum_out=None):
        kw = {}
        if bias is not None:
            kw['bias'] = bias
        if scale is not None:
            kw['scale'] = scale
        if accum_out is not None:
            kw['accum_out'] = accum_out
        self.s.op('act', lambda e: e.activation(out, in_, func, **kw), reads=reads, writes=writes)

    def tt(self, eng, out, in0, in1, op, reads, writes):
        self.s.op(eng, lambda e: e.tensor_tensor(out, in0, in1, op), reads=reads, writes=writes)

    def ts(self, eng, out, in0, s1, s2, op0, op1, reads, writes, accum_out=None):
        if op1 is None:
            self.s.op(eng, lambda e: e.tensor_scalar(out, in0, s1, None, op0), reads=reads, writes=writes)
        elif accum_out is None:
            self.s.op(eng, lambda e: e.tensor_scalar(out, in0, s1, s2, op0, op1), reads=reads, writes=writes)
        else:
            self.s.op(eng, lambda e: e.tensor_scalar(out, in0, s1, s2, op0, op1, accum_out=accum_out),
                      reads=reads, writes=writes)

    def stt(self, eng, out, in0, scalar, in1, op0, op1, reads, writes, accum_out=None):
        if accum_out is None:
            self.s.op(eng, lambda e: e.scalar_tensor_tensor(out, in0, scalar, in1, op0, op1),
                      reads=reads, writes=writes)
        else:
            self.s.op(eng, lambda e: e.scalar_tensor_tensor(out, in0, scalar, in1, op0, op1, accum_out=accum_out),
                      reads=reads, writes=writes)

    def rsqrt(self, out, in_, scale, eps, reads, writes):
        self.act(out, in_, AF.Sqrt, reads=reads, writes=writes, bias=self.epsc, scale=scale)
        self.s.op('dve', lambda e: e.reciprocal(out, out), reads=writes, writes=writes)

    def dma(self, out, in_, reads, writes, q='sp'):
        self.s.op(q, lambda e: e.dma_start(out=out, in_=in_), reads=reads, writes=writes, dma=True)

    def proj_fm_sub(self, wb, sub, ncols=512):
        ps = self.psum()
        wv = wb.ap[:, 0:8 * ncols].rearrange("p (kc n) -> p kc n", kc=8)
        for kc in range(8):
            self.mm(ps.ap, wv[:, kc, sub * 128:(sub + 1) * 128], self.hT.ap[:, kc, :], kc == 0, kc == 7,
                    reads=[wb, self.hT], writes=[ps])
        return ps

    def proj_tm(self, wb, c, col0, ncols, blkcols, ps=None, first=True):
        if ps is None:
            ps = self.psum()
        wv = wb.ap[:, 0:8 * blkcols].rearrange("p (kc n) -> p kc n", kc=8)
        for kc in range(8):
            self.mm(ps.ap[:, 0:ncols], self.hT.ap[:, kc, c * 128:(c + 1) * 128], wv[:, kc, col0:col0 + ncols],
                    first and kc == 0, kc == 7, reads=[wb, self.hT], writes=[ps])
        return ps

    def build(self):
        self.wplan = None
        self.wrec = []
        saved = self.s
        self.s = Sched()
        self.program()
        plan = self.wrec
        self.s = saved
        self.wplan = plan
        self.wpos = 0
        self.ps_rr = 0
        self.ps_priv = {}
        for j in range(min(NWSLOT - 2, len(plan))):
            self._wload(j)
        self.program()
        assert self.wpos == len(plan)
        self.s.emit(self.nc)
        return self.nc

    def program(self):
        S, NL = self.S, self.NL
        self.dma(self.cf32.ap, self.cf32_d, [], [self.cf32])
        self.dma(self.cbf.ap, self.cbf_d, [], [self.cbf], q='pool')
        self.identf = self.cf32.ap[:, 0:128]
        self.Uf = self.cf32.ap[:, 128:256]
        self.E64 = self.cf32.ap[:, 256:384]
        self.E127 = self.cf32.ap[:, 384:512]
        self.onesf = self.cf32.ap[:, 512:640]
        self.epsc = self.cf32.ap[:, 640:641]
        self.onec = self.cf32.ap[:, 641:642]
        self.ident = self.cbf.ap[:, 0:128]
        self.tri01 = self.cbf.ap[:, 128:256]
        self.maskneg = self.cbf.ap[:, 256:384]
        self.ones = self.cbf.ap[:, 384:512]
        out_tokens = []
        for l in range(NL):
            src = self.x_in if l == 0 else self.xs_d[(l - 1) % 2]
            srct = "x_in" if l == 0 else "xs%d" % ((l - 1) % 2)
            last = (l == NL - 1)
            dst = self.out_d if last else self.xs_d[l % 2]
            dstt = "out" if last else "xs%d" % (l % 2)
            self.layer(l, src, srct, dst, dstt, last and self.final_norm)
        toks = [("out", t, c) for t in range(self.NT) for c in range(4)]
        if self.dbg:
            toks.append("dbg")
        self.s.op('sp', None, reads=toks, writes=[])

    def layer(self, l, src, srct, dst, dstt, do_final):
        s = self.s
        self.dma(self.pf32.ap, self.pf32_d[l], [], [self.pf32])
        self.dma(self.pbf.ap, self.pbf_d[l], [], [self.pbf], q='pool')
        pf = self.pf32.ap
        self.gT = pf[:, 0:8]
        self.GT = pf[:, 8:12]
        self.BT = pf[:, 12:16]
        self.bsbc = pf[:, 16:528].rearrange("p (g t) -> p g t", g=4)
        self.kvg = pf[:, 528:656]
        self.cw = pf[:, 656:668].rearrange("p (a b) -> p a b", a=4)
        self.bfbc = pf[:, 668:672]
        pb = self.pbf.ap
        wT = pb[:, 0:512].rearrange("p (g t) -> p g t", g=4)
        self.wuk = pb[:, 512:1024].rearrange("p (h l) -> p h l", h=4)
        self.wuv = pb[:, 1024:1536].rearrange("p (h d) -> p h d", h=4)
        self.tt('dve', self.wTm.ap, wT, self.tri01.unsqueeze(1).to_broadcast([128, 4, 128]), ALU.mult,
                reads=[self.pbf, self.cbf], writes=[self.wTm])
        ps = self.psum()
        self.mm(ps.ap, self.ones, self.wTm.ap.rearrange("p g t -> p (g t)"), True, True,
                reads=[self.cbf, self.wTm], writes=[ps])
        self.tt('dve', self.biasm.ap, ps.ap.rearrange("p (g t) -> p g t", g=4),
                self.BT.unsqueeze(2).to_broadcast([128, 4, 128]), ALU.mult,
                reads=[ps, self.pf32], writes=[self.biasm])
        self.tt('dve', self.biasm.ap, self.biasm.ap, self.bsbc, ALU.add,
                reads=[self.biasm, self.pf32], writes=[self.biasm])
        s.op('dve', lambda e: e.memset(self.halo.ap, 0.0), reads=[], writes=[self.halo])
        for t in range(self.NT):
            self.tile(l, t, src, srct, dst, dstt, do_final)

    def tile(self, l, t, src, srct, dst, dstt, do_final):
        s = self.s
        self.scr_reset()
        sm = self.small
        xc2 = [self.xc[0], self.scr([128, D], F32)]
        xsbs = [self.scr([128, D], BF16) for _ in range(2)]
        junk = self.scr([128, D], BF16)
        for c in range(4):
            gc = 4 * t + c
            xb = xc2[c % 2]
            r0 = gc * 128
            self.dma(xb.ap, src[r0:r0 + 128, :], [(srct, t, c)], [xb])
            xsb = xsbs[c % 2]
            ss = sm.ap[:, c:c + 1]
            s.op('dve', lambda e, ss=ss: e.memset(ss, 0.0), reads=[], writes=[("sm", c)])
            self.act(junk.ap, xb.ap, AF.Square, reads=[xb], writes=[junk, ("sm", c)], accum_out=ss)
            rs = sm.ap[:, 8 + c:9 + c]
            self.rsqrt(rs, ss, 1.0 / D, EPS, reads=[("sm", c)], writes=[("sm", 8 + c)])
            self.ts('dve', xsb.ap, xb.ap, rs, None, ALU.mult, None, reads=[xb, ("sm", 8 + c)], writes=[xsb])
            for half in range(2):
                ps = self.psum()
                for k4 in range(4):
                    kc = half * 4 + k4
                    self.mm(ps.ap[:, k4 * 128:(k4 + 1) * 128], xsb.ap[:, kc * 128:(kc + 1) * 128], self.ident,
                            k4 == 0, k4 == 3, reads=[xsb, self.cbf], writes=[ps])
                self.tt('dve', self.hT.ap[:, half * 4:half * 4 + 4, c * 128:(c + 1) * 128],
                        ps.ap.rearrange("p (k n) -> p k n", k=4),
                        self.gT[:, half * 4:half * 4 + 4].unsqueeze(2).to_broadcast([128, 4, 128]), ALU.mult,
                        reads=[ps, self.pf32], writes=[self.hT])
            self.scr_off -= 0
        self.scr_base = 0
        self.scr_reset()
        gD = self.branch_D(l, t)
        next(gD)
        self.scr_base = self.scr_off

        def rest():
            self.scr_reset()
            yield from self.branch_A(l, t)
            self.scr_reset()
            yield from self.lift(l, t, 0)
            self.scr_reset()
            yield from self.branch_C(l, t)
            self.scr_reset()
            yield from self.lift(l, t, 2)
        self.ps_ring = [0, 1, 2]
        self.interleave(gD, rest())
        self.ps_ring = [0, 1, 2, 3]
        self.scr_reset()
        for _ in self.lift(l, t, 3, ysrc=self.yTD):
            pass
        self.scr_base = 0
        self.scr_reset()
        self.branch_B(l, t)
        self.scr_reset()
        for _ in self.lift(l, t, 1):
            pass
        self.scr_reset()
        self.outproj(l, t, src, srct, dst, dstt, do_final)

    def dbg_dump(self, idx, t, ysrc=None):
        if not self.dbg:
            return
        ysrc = ysrc if ysrc is not None else self.yT
        tmp = self.scr([128, 4, 512], F32)
        self.s.op('dve', lambda e: e.tensor_copy(tmp.ap, ysrc.ap), reads=[ysrc], writes=[tmp])
        for wc in range(4):
            dstap = self.dbg_d[idx, t * 512:(t + 1) * 512, wc * 128:(wc + 1) * 128].rearrange("t w -> w t")
            self.s.op('sp', lambda e, dstap=dstap, wc=wc: e.dma_start(out=dstap, in_=tmp.ap[:, wc, :],
                                                                         allow_slow_non_contiguous=True),
                      reads=[tmp], writes=["dbg"], dma=True)

    def branch_A(self, l, t):
        s = self.s
        vn = self.scr([128, 4, W], BF16)
        t1 = self.scr([128, 4, 512], F32)
        st = self.scr([128, 4, 8], F32)
        mv = self.scr([128, 4, 2], F32)
        wb = self.wnext(('in', l, OFF['a_v'], 512))
        for c in range(4):
            ps = self.proj_tm(wb, c, 0, 512, 512)
            s.op('dve', lambda e, c=c, ps=ps: e.bn_stats(st.ap[:, c, 0:6], ps.ap), reads=[ps], writes=[("Ast", c)])
            s.op('dve', lambda e, c=c: e.bn_aggr(mv.ap[:, c, :], st.ap[:, c, 0:6]), reads=[("Ast", c)],
                 writes=[("Amv", c)])
            rstd = st.ap[:, c, 6:7]
            self.rsqrt(rstd, mv.ap[:, c, 1:2], 1.0, EPS, reads=[("Amv", c)], writes=[("Ars", c)])
            self.ts('dve', vn.ap[:, c, :], ps.ap, mv.ap[:, c, 0:1], rstd, ALU.subtract, ALU.mult,
                    reads=[ps, ("Amv", c), ("Ars", c)], writes=[vn])
            yield
        for g in range(4):
            ps = self.psum()
            for c in range(4):
                self.mm(ps.ap[:, c * 128:(c + 1) * 128], vn.ap[:, c, g * 128:(g + 1) * 128], self.wTm.ap[:, g, :],
                        c == 0, c == 3, reads=[vn, self.wTm], writes=[ps])
            self.stt('dve', t1.ap[:, g, :].rearrange("p (c n) -> p c n", c=4),
                     ps.ap.rearrange("p (c n) -> p c n", c=4), self.GT[:, g:g + 1],
                     self.biasm.ap[:, g, :].unsqueeze(1).to_broadcast([128, 4, 128]), ALU.mult, ALU.add,
                     reads=[ps, self.pf32, self.biasm], writes=[t1])
        wb = self.wnext(('in', l, OFF['a_u'], 512))
        for g in range(4):
            ps = self.proj_fm_sub(wb, g)
            self.tt('dve', t1.ap[:, g, :], ps.ap, t1.ap[:, g, :], ALU.mult, reads=[ps, t1], writes=[t1])
            yield
        wb = self.wnext(('in', l, OFF['a_z'], 512))
        szs = [self.scr([128, 512], F32) for _ in range(2)]
        for g in range(4):
            ps = self.proj_fm_sub(wb, g)
            sz = szs[g % 2]
            self.act(sz.ap, ps.ap, AF.Silu, reads=[ps], writes=[sz])
            self.tt('dve', self.yT.ap[:, g, :], t1.ap[:, g, :], sz.ap, ALU.mult, reads=[t1, sz], writes=[self.yT])
            yield
        self.dbg_dump(0, t)

    def lift(self, l, t, b, ysrc=None):
        yT = ysrc if ysrc is not None else self.yT
        first = (b == 0)
        sgs = [self.scr([128, 512], F32) for _ in range(2)]
        tmps = [self.scr([128, 512], F32) for _ in range(2)]
        for half in range(2):
            wg = self.wnext(('in', l, OFF['gates'] + b * D + half * 512, 512))
            wbr = self.wnext(('br', l, b, half))
            wbv = wbr.ap[:, 0:2048].rearrange("p (wc n) -> p wc n", wc=4)
            for sub in range(4):
                ncn = half * 4 + sub
                psg = self.proj_fm_sub(wg, sub)
                sg = sgs[ncn % 2]
                self.act(sg.ap, psg.ap, AF.Sigmoid, reads=[psg], writes=[sg])
                psb = self.psum()
                for wc in range(4):
                    self.mm(psb.ap, wbv[:, wc, sub * 128:(sub + 1) * 128], yT.ap[:, wc, :], wc == 0, wc == 3,
                            reads=[wbr, yT], writes=[psb])
                if first:
                    self.tt('dve', self.mg.ap[:, ncn, :], psb.ap, sg.ap, ALU.mult, reads=[psb, sg],
                            writes=[("mg", ncn)])
                else:
                    tmp = tmps[ncn % 2]
                    self.tt('dve', tmp.ap, psb.ap, sg.ap, ALU.mult, reads=[psb, sg], writes=[tmp])
                    self.tt('pool', self.mg.ap[:, ncn, :], self.mg.ap[:, ncn, :], tmp.ap, ALU.add,
                            reads=[tmp, ("mg", ncn)], writes=[("mg", ncn)])
                yield

    def branch_C(self, l, t):
        s = self.s
        yc = self.scr([128, 4, 514], F32)
        cv = self.scr([128, 4, 512], F32)
        ccs = [self.scr([128, 512], F32) for _ in range(2)]
        wbc = self.wnext(('in', l, OFF['c_c'], 512))
        wbx = self.wnext(('in', l, OFF['c_x'], 512))
        s.op('dve', lambda e: e.tensor_copy(yc.ap[:, :, 0:2], self.halo.ap), reads=[self.halo], writes=[yc])
        for g in range(4):
            ps = self.proj_fm_sub(wbc, g)
            cc = ccs[g % 2]
            self.act(cc.ap, ps.ap, AF.Copy, reads=[ps], writes=[cc])
            ps2 = self.proj_fm_sub(wbx, g)
            self.tt('dve', yc.ap[:, g, 2:514], ps2.ap, cc.ap, ALU.mult, reads=[ps2, cc], writes=[yc])
            self.ts('dve', cv.ap[:, g, :], yc.ap[:, g, 0:512], self.cw[:, g, 0:1], None, ALU.mult, None,
                    reads=[yc, self.pf32], writes=[cv])
            self.stt('dve', cv.ap[:, g, :], yc.ap[:, g, 1:513], self.cw[:, g, 1:2], cv.ap[:, g, :], ALU.mult, ALU.add,
                     reads=[yc, self.pf32, cv], writes=[cv])
            self.stt('dve', cv.ap[:, g, :], yc.ap[:, g, 2:514], self.cw[:, g, 2:3], cv.ap[:, g, :], ALU.mult, ALU.add,
                     reads=[yc, self.pf32, cv], writes=[cv])
            yield
        s.op('dve', lambda e: e.tensor_copy(self.halo.ap, yc.ap[:, :, 512:514]), reads=[yc], writes=[self.halo])
        wbb = self.wnext(('in', l, OFF['c_b'], 512))
        for g in range(4):
            ps = self.proj_fm_sub(wbb, g)
            self.tt('dve', cv.ap[:, g, :], ps.ap, cv.ap[:, g, :], ALU.mult, reads=[ps, cv], writes=[cv])
            yield
        wbz = self.wnext(('in', l, OFF['c_z'], 512))
        for g in range(4):
            ps = self.proj_fm_sub(wbz, g)
            sz = ccs[g % 2]
            self.act(sz.ap, ps.ap, AF.Silu, reads=[ps], writes=[sz])
            self.tt('dve', self.yT.ap[:, g, :], cv.ap[:, g, :], sz.ap, ALU.mult, reads=[cv, sz], writes=[self.yT])
            yield
        self.dbg_dump(2, t)

    def branch_D(self, l, t):
        s = self.s
        NCH = self.NCH
        qTf = self.scr([128, 4, 512], BF16)
        szD = self.scr([128, 4, 512], BF16)
        biasall = self.scr([128, 4, NCH * 4], F32)
        pTs = [self.scr([128, 512], BF16) for _ in range(3)]
        rden = self.scr([128, 512], F32)
        tmpo = self.scr([128, 512], F32)
        lsb = self.scr([128, 4, 4], F32)
        crefs = self.scr([128, 4, 4], F32)
        self.yTD = self.scr([128, 4, 512], BF16)
        wb = self.wnext(('in', l, OFF['d_q'], 512))
        for h in range(4):
            ps = self.proj_fm_sub(wb, h)
            self.act(qTf.ap[:, h, :], ps.ap, AF.Copy, reads=[ps], writes=[qTf], scale=float(128 ** -0.5))
        wb = self.wnext(('in', l, OFF['d_k'], 512))
        for h in range(4):
            ps = self.proj_fm_sub(wb, h)
            s.op('dve', lambda e, h=h, ps=ps: e.tensor_copy(self.kT.ap[:, h, t * 512:(t + 1) * 512], ps.ap),
                 reads=[ps], writes=[("kT", t)])
        wb = self.wnext(('in', l, OFF['d_vf'], 516))
        for c in range(4):
            gc = 4 * t + c
            ps = self.proj_tm(wb, c, 0, 512, 516)
            self.act(self.vC.ap[:, gc, :], ps.ap, AF.Copy, reads=[ps], writes=[("vC", gc)])
            psf = self.proj_tm(wb, c, 512, 4, 516)
            xf = lsb.ap[:, c, :]
            self.tt('dve', xf, psf.ap[:, 0:4], self.bfbc, ALU.add, reads=[psf, self.pf32], writes=[("lsb", c)])
            self.act(xf, xf, AF.Exp, reads=[("lsb", c)], writes=[("lsb", c)], scale=-1.0)
            self.act(xf, xf, AF.Ln, reads=[("lsb", c), self.cf32], writes=[("lsb", c)], bias=self.onec)
            psc = self.psum()
            self.mm(psc.ap[:, 0:4], self.Uf, xf, True, gc == 0, reads=[self.cf32, ("lsb", c)], writes=[psc])
            if gc > 0:
                self.mm(psc.ap[:, 0:4], self.E127, self.cumneg.ap[:, gc - 1, :], False, True,
                        reads=[self.cf32, ("cum", gc - 1)], writes=[psc])
            s.op('dve', lambda e, gc=gc, psc=psc: e.tensor_copy(self.cumneg.ap[:, gc, :], psc.ap[:, 0:4]),
                 reads=[psc], writes=[("cum", gc)])
            if c % 2 == 0:
                psr = self.psum()
                self.mm(psr.ap[:, 0:4], self.E127, self.cumneg.ap[:, gc, :], True, True,
                        reads=[self.cf32, ("cum", gc)], writes=[psr])
                s.op('dve', lambda e, c=c, psr=psr: e.tensor_copy(crefs.ap[:, c // 2, :], psr.ap[:, 0:4]),
                     reads=[psr], writes=[("cref", c // 2)])
            else:
                nk = gc + 1
                m = c // 2
                self.tt('dve', biasall.ap[:, m, 0:nk * 4].rearrange("p (i h) -> p i h", h=4),
                        self.cumneg.ap[:, 0:nk, :], crefs.ap[:, m, :].unsqueeze(1).to_broadcast([128, nk, 4]),
                        ALU.subtract, reads=[("cum", i) for i in range(nk)] + [("cref", m)], writes=[("bias", m)])
        wb = self.wnext(('in', l, OFF['d_z'], 512))
        for h in range(4):
            ps = self.proj_fm_sub(wb, h)
            self.act(szD.ap[:, h, :], ps.ap, AF.Silu, reads=[ps], writes=[szD])
        nkc = 4 * t + 4
        items = [(h, i) for h in range(4) for i in range(nkc)]
        pss_of = {}
        LA = 2

        def S1(k):
            h, i = items[k]
            c0 = max(i - 4 * t, 0) * 128
            pss = self.psum(ring=[3, 6, 7])
            pss_of[k] = pss
            self.mm(pss.ap[:, c0:512], self.kT.ap[:, h, i * 128:(i + 1) * 128], qTf.ap[:, h, c0:512], True, True,
                    reads=[("kT", i // 4), qTf], writes=[pss])

        def S23(k):
            h, i = items[k]
            ci = i - 4 * t
            j0 = max(ci, 0)
            c0 = j0 * 128
            acc_o = self.ps[4]
            acc_d = self.ps[5]
            pss = pss_of.pop(k)
            pT = pTs[k % 3]
            for m in range(j0 // 2, 2):
                a0 = max(c0, m * 256)
                self.act(pT.ap[:, a0:(m + 1) * 256], pss.ap[:, a0:(m + 1) * 256], AF.Exp,
                         reads=[pss, ("bias", m)], writes=[pT],
                         bias=biasall.ap[:, m, i * 4 + h:i * 4 + h + 1])
            if ci >= 0:
                self.tt('pool', pT.ap[:, c0:c0 + 128], pT.ap[:, c0:c0 + 128], self.tri01, ALU.mult,
                        reads=[pT, self.cbf], writes=[pT])
            self.mm(acc_o.ap[:, c0:512], self.vC.ap[:, i, h * 128:(h + 1) * 128], pT.ap[:, c0:512],
                    i == 0, i == nkc - 1, reads=[("vC", i), pT], writes=[acc_o])
            self.mm(acc_d.ap[:, c0:512], self.ones, pT.ap[:, c0:512], i == 0, i == nkc - 1,
                    reads=[self.cbf, pT], writes=[acc_d])
            if i == nkc - 1:
                s.op('dve', lambda e, acc_d=acc_d: e.reciprocal(rden.ap, acc_d.ap), reads=[acc_d], writes=[rden])
                self.tt('dve', tmpo.ap, acc_o.ap, rden.ap, ALU.mult, reads=[acc_o, rden], writes=[tmpo])
                self.tt('dve', self.yTD.ap[:, h, :], tmpo.ap, szD.ap[:, h, :], ALU.mult, reads=[tmpo, szD],
                        writes=[self.yTD])

        yield
        for k in range(min(LA, len(items))):
            S1(k)
        for k in range(len(items)):
            if k + LA < len(items):
                S1(k + LA)
            S23(k)
            yield
        self.dbg_dump(3, t, ysrc=self.yTD)

    def branch_B(self, l, t):
        s = self.s
        S = self.S
        sm = self.small
        K1 = 1024
        qlatT = self.scr([128, 4, 512], BF16)
        szB = self.scr([128, 4, 512], BF16)
        qiT = self.scr([128, 2, 512], BF16)
        scores = [self.scr([128, S], BF16, align=K1) for _ in range(2)]
        junk = self.scr([128, S], U8)
        rhflat = self.scr([128, 2048], BF16, align=K1)
        rh = [Buf(rhflat.ap[:, h * 512:(h + 1) * 512], [rhflat.toks[h]]) for h in range(4)]
        st8 = Buf(rhflat.ap.bitcast(U8), rhflat.toks)
        pmall = self.scr([128, 4, 512], BF16, align=K1)
        pm = [Buf(pmall.ap[:, h, :], [pmall.toks[h]]) for h in range(4)]
        qTb = pmall
        pmT = [self.scr([128, 512], BF16, align=K1) for _ in range(2)]
        Dsg = self.scr([128, 4, 128], BF16, align=K1)
        Dg4 = self.scr([128, 4, 128], F32, align=K1)
        rdbc = self.scr([128, 512], F32, align=K1)
        olat = self.scr([128, 512], BF16, align=K1)
        ctmp = self.scr([128, 128], F32, align=K1)
        kid = self.scr([128, 128], BF16)
        qib = self.scr([128, 256], BF16)
        denp = self.scr([128, 4, 8], F32, align=K1)
        den = self.scr([128, 4], F32)
        wb = self.wnext(('in', l, OFF['b_q'], 512))
        for h in range(4):
            ps = self.proj_fm_sub(wb, h)
            self.act(qTb.ap[:, h, :], ps.ap, AF.Copy, reads=[ps], writes=[qTb])
        for h in range(4):
            ps = self.psum()
            self.mm(ps.ap, self.wuk[:, h, :], qTb.ap[:, h, :], True, True, reads=[self.pbf, qTb], writes=[ps])
            self.act(qlatT.ap[:, h, :], ps.ap, AF.Copy, reads=[ps], writes=[qlatT], scale=float(128 ** -0.5))
        wb = self.wnext(('in', l, OFF['b_misc'], 452))
        for c in range(4):
            gc = 4 * t + c
            ps = self.proj_tm(wb, c, 0, 452, 452)
            ss = sm.ap[:, 16 + c:17 + c]
            s.op('dve', lambda e, ss=ss: e.memset(ss, 0.0), reads=[], writes=[("sm", 16 + c)])
            self.act(ctmp.ap, ps.ap[:, 0:128], AF.Square, reads=[ps], writes=[ctmp, ("sm", 16 + c)], accum_out=ss)
            rs = sm.ap[:, 24 + c:25 + c]
            self.rsqrt(rs, ss, 1.0 / 128, EPS, reads=[("sm", 16 + c)], writes=[("sm", 24 + c)])
            self.stt('dve', self.cTok.ap[:, gc, :], ps.ap[:, 0:128], rs, self.kvg, ALU.mult, ALU.mult,
                     reads=[ps, ("sm", 24 + c), self.pf32], writes=[("cTok", gc)])
            pst = self.psum()
            self.mm(pst.ap[:, 0:128], self.cTok.ap[:, gc, :], self.ident, True, True,
                    reads=[("cTok", gc), self.cbf], writes=[pst])
            self.act(self.cT.ap[:, gc * 128:(gc + 1) * 128], pst.ap[:, 0:128], AF.Copy, reads=[pst],
                     writes=[("cT", gc)])
            s.op('dve', lambda e, ps=ps: e.tensor_copy(kid.ap.rearrange("p (a b) -> p a b", a=2),
                                                       ps.ap[:, 384:448].unsqueeze(1).to_broadcast([128, 2, 64])),
                 reads=[ps], writes=[kid])
            pst = self.psum()
            self.mm(pst.ap[:, 0:128], kid.ap, self.ident, True, True, reads=[kid, self.cbf], writes=[pst])
            self.act(self.kiT.ap[:, gc * 128:(gc + 1) * 128], pst.ap[:, 0:128], AF.Copy, reads=[pst],
                     writes=[("kiT", gc)])
            s.op('dve', lambda e, ps=ps: e.tensor_copy(qib.ap, ps.ap[:, 128:384]), reads=[ps], writes=[qib])
            pst = self.psum()
            for pr in range(2):
                self.mm(pst.ap[:, pr * 128:(pr + 1) * 128], qib.ap[:, pr * 128:(pr + 1) * 128], self.ident,
                        pr == 0, pr == 1, reads=[qib, self.cbf], writes=[pst])
            self.act(qiT.ap[:, :, c * 128:(c + 1) * 128], pst.ap[:, 0:256].rearrange("p (a b) -> p a b", a=2),
                     AF.Copy, reads=[pst], writes=[qiT])
            aw = self.wstat.ap[:, c, 0:4]
            sg = self.wstat.ap[:, c, 4:8]
            self.act(aw, ps.ap[:, 448:452], AF.Abs, reads=[ps], writes=[("aw", c)], scale=0.5 / 8.0)
            self.act(sg, ps.ap[:, 448:452], AF.Sign, reads=[ps], writes=[("sg", c)])
        wb = self.wnext(('in', l, OFF['b_z'], 512))
        for h in range(4):
            ps = self.proj_fm_sub(wb, h)
            self.act(szB.ap[:, h, :], ps.ap, AF.Silu, reads=[ps], writes=[szB])
        acc_o = self.ps[4]
        acc_r = self.ps[5]
        pso = self.ps[5]
        self.ps_ring = [0, 1, 2, 3, 6, 7]
        bis = self.bis.ap
        th = bis[:, 0:1]

        def SC(c):
            gc = 4 * t + c
            nk = gc + 1
            nkt = (nk + 3) // 4
            score = scores[c % 2]
            aw = self.wstat.ap[:, c, 0:4]
            sg = self.wstat.ap[:, c, 4:8]
            self.tt('dve', Dsg.ap, self.ident.unsqueeze(1).to_broadcast([128, 4, 128]),
                    sg.unsqueeze(2).to_broadcast([128, 4, 128]), ALU.mult, reads=[self.cbf, ("sg", c)], writes=[Dsg])
            for kt in range(nkt):
                wdt = min(512, nk * 128 - kt * 512)
                k0 = kt * 512
                for h in range(4):
                    ps = self.psum()
                    r0 = (h % 2) * 64
                    self.mm(ps.ap[:, 0:wdt], qiT.ap[r0:r0 + 64, h // 2, c * 128:(c + 1) * 128],
                            self.kiT.ap[r0:r0 + 64, k0:k0 + wdt], True, True,
                            reads=[qiT] + [("kiT", k0 // 128 + i) for i in range(wdt // 128)], writes=[ps])
                    self.act(rh[h].ap[:, 0:wdt], ps.ap[:, 0:wdt], AF.Relu, reads=[ps, ("aw", c)], writes=[rh[h]],
                             scale=aw[:, h:h + 1])
                pssc = self.psum()
                diag = (gc // 4 == kt)
                for h in range(4):
                    self.mm(pssc.ap[:, 0:wdt], Dsg.ap[:, h, :], rh[h].ap[:, 0:wdt], h == 0, (h == 3 and not diag),
                            reads=[Dsg, rh[h]], writes=[pssc])
                if diag:
                    jd = gc % 4
                    self.mm(pssc.ap[:, jd * 128:(jd + 1) * 128], self.ident, self.maskneg, False, True,
                            reads=[self.cbf], writes=[pssc])
                s.op('dve', lambda e, score=score, pssc=pssc, k0=k0, wdt=wdt:
                     e.tensor_copy(score.ap[:, k0:k0 + wdt], pssc.ap[:, 0:wdt]), reads=[pssc], writes=[score])
                yield

        def BI(c):
            gc = 4 * t + c
            nk = gc + 1
            n = nk * 128
            score = scores[c % 2]
            if gc < 2:
                s.op('dve', lambda e: e.memset(th, -1.0e29), reads=[], writes=[self.bis])
            else:
                s.op('dve', lambda e: e.memset(bis[:, 0:32], 0.0), reads=[], writes=[self.bis])
                s.op('dve', lambda e: e.memset(th, 1.0e-30), reads=[], writes=[self.bis])
                delta = 4.0
                for it in range(NITER):
                    cnt = bis[:, 2 + it:3 + it]
                    self.ts('dve', junk.ap[:, 0:n], score.ap[:, 0:n], th, 0.0, ALU.is_ge, ALU.add,
                            reads=[score, self.bis], writes=[self.bis], accum_out=cnt)
                    sel = bis[:, 1:2]
                    self.ts('dve', sel, cnt, TOPK - 0.5, 2.0 * delta, ALU.is_ge, ALU.mult, reads=[self.bis],
                            writes=[self.bis])
                    self.stt('dve', th, th, -delta, sel, ALU.add, ALU.add, reads=[self.bis], writes=[self.bis])
                    delta *= 0.5
                    yield
                self.ts('dve', th, th, -2.0 * delta, None, ALU.add, None, reads=[self.bis], writes=[self.bis])
                cpos = bis[:, 2:3]
                nz = bis[:, 16:17]
                t1 = bis[:, 17:18]
                t2 = bis[:, 18:19]
                tie = bis[:, 19:20]
                aa = bis[:, 20:21]
                ini = bis[:, 21:22]
                dd = bis[:, 22:23]
                B_ = [self.bis]
                J_ = [("junk",)]
                self.ts('dve', junk.ap[:, 0:n], score.ap[:, 0:n], 0.0, 0.0, ALU.is_equal, ALU.add,
                        reads=[score], writes=J_ + B_, accum_out=nz)
                self.ts('dve', t1, cpos, TOPK - 0.5, None, ALU.is_le, None, reads=B_, writes=B_)
                self.tt('dve', t2, cpos, nz, ALU.add, reads=B_, writes=B_)
                self.stt('dve', tie, t2, TOPK - 0.5, t1, ALU.is_ge, ALU.mult, reads=B_, writes=B_)
                self.ts('dve', aa, cpos, 2.0, -(2.0 * TOPK + 2.0), ALU.mult, ALU.add, reads=B_, writes=B_)
                self.ts('dve', ini, aa, tie, 4.0, ALU.mult, ALU.add, reads=B_, writes=B_)
                self.ts('dve', dd, tie, -1.0, 1.0, ALU.mult, ALU.add, reads=B_, writes=B_)
                self.tt('dve', th, th, dd, ALU.mult, reads=B_, writes=B_)
                self.stt('dve', th, tie, 1.0e-30, th, ALU.mult, ALU.add, reads=B_, writes=B_)
                s.op('dve', lambda e, n=n: e.tensor_tensor_scan(st8.ap[:, 0:n], junk.ap[:, 0:n], junk.ap[:, 0:n], ini,
                                                                ALU.add, ALU.add), reads=J_ + B_, writes=[st8])
                self.stt('dve', junk.ap[:, 0:n], st8.ap[:, 0:n], 2.5, junk.ap[:, 0:n], ALU.is_le, ALU.mult,
                         reads=[st8] + J_, writes=J_)
                self.stt('dve', score.ap[:, 0:n], junk.ap[:, 0:n], 1.0, score.ap[:, 0:n], ALU.mult, ALU.add,
                         reads=J_ + [score], writes=[score])
                yield
            if self.dbg:
                self.dma(self.dbg_d[5, gc * 128:(gc + 1) * 128, 0:32], self.bis.ap, [self.bis], ["dbg"])
                self.dma(self.dbg_d[6, gc * 128:(gc + 1) * 128, 0:nk * 128], score.ap[:, 0:nk * 128], [score],
                         ["dbg"], q='pool')
            self.ts('dve', score.ap[:, 0:n], score.ap[:, 0:n], th, -30000.0, ALU.is_lt, ALU.mult,
                    reads=[score, self.bis], writes=[score])
            yield

        def AT(c):
            gc = 4 * t + c
            nk = gc + 1
            nkt = (nk + 3) // 4
            score = scores[c % 2]
            s.op('dve', lambda e: e.memset(denp.ap, 0.0), reads=[], writes=[denp])
            for kt in range(nkt):
                wdt = min(512, nk * 128 - kt * 512)
                k0 = kt * 512
                for h in range(4):
                    ps = self.psum()
                    self.mm(ps.ap[:, 0:wdt], qlatT.ap[:, h, c * 128:(c + 1) * 128], self.cT.ap[:, k0:k0 + wdt],
                            True, False, reads=[qlatT] + [("cT", k0 // 128 + i) for i in range(wdt // 128)],
                            writes=[ps])
                    self.mm(ps.ap[:, 0:wdt], self.ident, score.ap[:, k0:k0 + wdt], False, True,
                            reads=[self.cbf, score], writes=[ps])
                    self.act(pm[h].ap[:, 0:wdt], ps.ap[:, 0:wdt], AF.Exp, reads=[ps], writes=[pm[h], denp],
                             accum_out=denp.ap[:, h, kt:kt + 1])
                yield
                njj = wdt // 128
                psts = {}

                def T(jj):
                    pst = self.psum()
                    psts[jj] = pst
                    for h in range(4):
                        self.mm(pst.ap[:, h * 128:(h + 1) * 128], pm[h].ap[:, jj * 128:(jj + 1) * 128], self.ident,
                                h == 0, h == 3, reads=[pm[h], self.cbf], writes=[pst])
                T(0)
                for jj in range(njj):
                    if jj + 1 < njj:
                        T(jj + 1)
                    kc = k0 // 128 + jj
                    pst = psts.pop(jj)
                    pT = pmT[kc % 2]
                    self.act(pT.ap, pst.ap, AF.Copy, reads=[pst], writes=[pT])
                    self.mm(acc_o.ap, self.cTok.ap[:, kc, :], pT.ap, kc == 0, kc == nk - 1,
                            reads=[("cTok", kc), pT], writes=[acc_o])
                    yield
            s.op('dve', lambda e: e.tensor_reduce(den.ap, denp.ap, mybir.AxisListType.X, ALU.add),
                 reads=[denp], writes=[den])
            s.op('dve', lambda e: e.reciprocal(den.ap, den.ap), reads=[den], writes=[den])
            self.tt('dve', Dg4.ap, self.identf.unsqueeze(1).to_broadcast([128, 4, 128]),
                    den.ap.unsqueeze(2).to_broadcast([128, 4, 128]), ALU.mult, reads=[self.cf32, den], writes=[Dg4])
            self.mm(acc_r.ap, self.onesf, Dg4.ap.rearrange("p h n -> p (h n)"), True, True,
                    reads=[self.cf32, Dg4], writes=[acc_r])
            self.act(rdbc.ap, acc_r.ap, AF.Copy, reads=[acc_r], writes=[rdbc])
            self.tt('dve', olat.ap, acc_o.ap, rdbc.ap, ALU.mult, reads=[acc_o, rdbc], writes=[olat])
            yield
            for h in range(4):
                self.mm(pso.ap[:, h * 128:(h + 1) * 128], self.wuv[:, h, :], olat.ap[:, h * 128:(h + 1) * 128],
                        h == 0, h == 3, reads=[self.pbf, olat], writes=[pso])
            self.tt('dve', self.yT.ap[:, :, c * 128:(c + 1) * 128], pso.ap.rearrange("p (h n) -> p h n", h=4),
                    szB.ap[:, :, c * 128:(c + 1) * 128], ALU.mult, reads=[pso, szB], writes=[self.yT])
            yield

        def chain(*gens):
            for g in gens:
                for _ in g:
                    yield

        self.interleave(SC(0))
        self.interleave(BI(0), SC(1))
        for c in range(4):
            first = chain(AT(c), SC(c + 2)) if c + 2 < 4 else AT(c)
            if c + 1 < 4:
                self.interleave(first, BI(c + 1))
            else:
                self.interleave(first)
        self.ps_ring = [0, 1, 2, 3]
        self.dbg_dump(1, t)

    def outproj(self, l, t, src, srct, dst, dstt, do_final):
        s = self.s
        sm = self.small
        mgb = self.scr([128, 8, 512], BF16)
        for ncn in range(8):
            self.act(mgb.ap[:, ncn, :], self.mg.ap[:, ncn, :], AF.Copy, reads=[("mg", ncn)], writes=[mgb])
        if self.dbg and l == 0:
            tmp = self.scr([128, 8, 512], F32)
            self.s.op('dve', lambda e: e.tensor_copy(tmp.ap, self.mg.ap), reads=[("mg", i) for i in range(8)],
                      writes=[tmp])
            for ncn in range(8):
                dstap = self.dbg_d[4, t * 512:(t + 1) * 512, ncn * 128:(ncn + 1) * 128].rearrange("t w -> w t")
                self.s.op('sp', lambda e, dstap=dstap, ncn=ncn: e.dma_start(out=dstap, in_=tmp.ap[:, ncn, :],
                                                                               allow_slow_non_contiguous=True),
                          reads=[tmp], writes=["dbg"], dma=True)
        wos = [self.wnext(('out', l, half * 512, 512)) for half in range(2)]
        if do_final:
            fgb = self.scr([128, D], F32)
            self.dma(fgb.ap, self.fg_d, [], [fgb])
            junk = self.scr([128, D], BF16)
        xc2 = [self.xc[0], self.scr([128, D], F32)]
        for c in range(4):
            gc = 4 * t + c
            r0 = gc * 128
            xb = xc2[c % 2]
            self.dma(xb.ap, src[r0:r0 + 128, :], [(srct, t, c)], [xb])
            for half in range(2):
                wv = wos[half].ap[:, 0:4096].rearrange("p (kc n) -> p kc n", kc=8)
                ps = self.psum()
                for ncn in range(8):
                    self.mm(ps.ap, mgb.ap[:, ncn, c * 128:(c + 1) * 128], wv[:, ncn, :], ncn == 0, ncn == 7,
                            reads=[mgb, wos[half]], writes=[ps])
                self.tt('dve', xb.ap[:, half * 512:(half + 1) * 512], ps.ap, xb.ap[:, half * 512:(half + 1) * 512],
                        ALU.add, reads=[ps, xb], writes=[xb])
            if do_final:
                ss = sm.ap[:, 32 + c:33 + c]
                s.op('dve', lambda e, ss=ss: e.memset(ss, 0.0), reads=[], writes=[("sm", 32 + c)])
                self.act(junk.ap, xb.ap, AF.Square, reads=[xb], writes=[junk, ("sm", 32 + c)], accum_out=ss)
                rs = sm.ap[:, 40 + c:41 + c]
                self.rsqrt(rs, ss, 1.0 / D, EPS, reads=[("sm", 32 + c)], writes=[("sm", 40 + c)])
                self.stt('dve', xb.ap, xb.ap, rs, fgb.ap, ALU.mult, ALU.mult, reads=[xb, ("sm", 40 + c), fgb],
                         writes=[xb])
            self.dma(dst[r0:r0 + 128, :], xb.ap, [xb], [(dstt, t, c)])


def _consts():
    i = np.arange(128)
    ident = np.eye(128, dtype=np.float32)
    tri = (i[:, None] <= i[None, :]).astype(np.float32)
    e64 = np.zeros((128, 128), np.float32)
    e64[64, :] = 1.0
    e127 = np.zeros((128, 128), np.float32)
    e127[127, :] = 1.0
    ones = np.ones((128, 128), np.float32)
    maskneg = np.where(i[None, :] <= i[:, None], 0.0, NEG).astype(np.float32)
    extra = np.zeros((128, 8), np.float32)
    extra[:, 0] = EPS
    extra[:, 1] = 1.0
    cf32 = np.concatenate([ident, tri, e64, e127, ones, extra], axis=1)
    cbf = np.concatenate([ident, tri, maskneg, ones], axis=1)
    return np.ascontiguousarray(cf32), np.ascontiguousarray(cbf)


def _pack_params(NL, norm_g, gm_ln_g, gm_ln_b, gm_w_s, gm_b_s, dsa_kv_g, dsa_w_uk, dsa_w_uv, conv_w, fox_b_f):
    pf = np.zeros((NL, 128, NP32), np.float32)
    pb = np.zeros((NL, 128, NPBF), np.float32)
    for l in range(NL):
        pf[l, :, 0:8] = norm_g[l].reshape(8, 128).T
        pf[l, :, 8:12] = gm_ln_g[l].reshape(4, 128).T
        pf[l, :, 12:16] = gm_ln_b[l].reshape(4, 128).T
        pf[l, :, 16:528] = np.broadcast_to(gm_b_s[l].reshape(1, 512), (128, 512))
        pf[l, :, 528:656] = np.broadcast_to(dsa_kv_g[l].reshape(1, 128), (128, 128))
        pf[l, :, 656:668] = conv_w[l].reshape(3, 4, 128).transpose(2, 1, 0).reshape(128, 12)
        pf[l, :, 668:672] = np.broadcast_to(fox_b_f[l].reshape(1, 4), (128, 4))
        pb[l, :, 0:512] = gm_w_s[l].transpose(2, 0, 1).reshape(128, 512)
        pb[l, :, 512:1024] = dsa_w_uk[l].transpose(2, 0, 1).reshape(128, 512)
        pb[l, :, 1024:1536] = dsa_w_uv[l].transpose(1, 0, 2).reshape(128, 512)
    return pf, pb


_CACHE = {}


def _get_program(S, NL, final_norm, dbg):
    key = (S, NL, final_norm, dbg)
    if key not in _CACHE:
        _CACHE[key] = Builder(S, NL, final_norm, dbg).build()
    return _CACHE[key]


def run_layers(x, norm_g, w_in, gm_ln_g, gm_ln_b, gm_w_s, gm_b_s, dsa_kv_g, dsa_w_uk, dsa_w_uv,
               conv_w, fox_b_f, w_branch, w_out, final_g, final_norm=True, dbg=False):
    x = np.asarray(x, np.float32)
    B, S, _ = x.shape
    NL = int(np.asarray(w_in).shape[0])
    f = lambda a: np.ascontiguousarray(np.asarray(a, np.float32))
    nc = _get_program(S, NL, final_norm, dbg)
    cf32, cbf = _consts()
    pf, pb = _pack_params(NL, f(norm_g), f(gm_ln_g), f(gm_ln_b), f(gm_w_s), f(gm_b_s), f(dsa_kv_g), f(dsa_w_uk),
                          f(dsa_w_uv), f(conv_w), f(fox_b_f))
    fg = np.ascontiguousarray(np.broadcast_to(f(final_g).reshape(1, D), (128, D)))
    shared = dict(w_in=f(w_in), w_br=f(w_branch), w_out=f(w_out), pf32=pf, pbf=pb, cf32=cf32, cbf=cbf, fg=fg)
    in_maps = []
    for b in range(B):
        m = dict(shared)
        m["x"] = np.ascontiguousarray(x[b])
        in_maps.append(m)
    res = run_bass_kernel_spmd(nc, in_maps, core_ids=list(range(B)))
    out = np.stack([np.asarray(r["out"], np.float32) for r in res.results], axis=0)
    if dbg:
        return out, np.stack([np.asarray(r["dbg"], np.float32) for r in res.results], axis=0)
    return out


def kernel(x, norm_g, w_in, gm_ln_g, gm_ln_b, gm_w_s, gm_b_s, dsa_kv_g, dsa_w_uk, dsa_w_uv,
           conv_w, fox_b_f, w_branch, w_out, final_g):
    return run_layers(x, norm_g, w_in, gm_ln_g, gm_ln_b, gm_w_s, gm_b_s, dsa_kv_g, dsa_w_uk, dsa_w_uv,
                      conv_w, fox_b_f, w_branch, w_out, final_g)
```

```python
import numpy as np
import concourse.bass as bass
import concourse.mybir as mybir
from concourse.bass_utils import run_bass_kernel_spmd

F32 = mybir.dt.float32
BF16 = mybir.dt.bfloat16
U8 = mybir.dt.uint8
ALU = mybir.AluOpType
AF = mybir.ActivationFunctionType

D = 1024
W = 512
INW = 11208
OFF = dict(a_u=0, a_v=512, a_z=1024, b_q=1536, b_misc=2048, b_z=2500,
           c_b=3012, c_c=3524, c_x=4036, c_z=4548,
           d_q=5060, d_k=5572, d_vf=6084, d_z=6600, gates=7112)
EPS = 1e-6
TOPK = 256
NITER = 13
NEG = -1.0e30
NP32 = 672
NPBF = 1536
NC32 = 648
NCBF = 512
SLOT = 4160
NWSLOT = 4
GRAN = 1024


class Buf:
    def __init__(self, ap, toks):
        self.ap = ap
        self.toks = toks


class Sched:
    COMPUTE = ('pe', 'act', 'dve', 'pool')

    def __init__(self):
        self.ops = []
        self.tok = {}
        self.ndma = {}

    def _flat(self, lst):
        out = []
        for x in lst:
            if x is None:
                continue
            if isinstance(x, Buf):
                out.extend(x.toks)
            elif isinstance(x, (list, tuple)) and len(x) > 0 and isinstance(x[0], (Buf, list)):
                out.extend(self._flat(x))
            else:
                out.append(x)
        return out

    def op(self, eng, fn, reads=(), writes=(), dma=False):
        idx = len(self.ops)
        reads = self._flat(reads)
        writes = self._flat(writes)
        key = ('dma', idx) if dma else eng
        deps = set()
        for t in reads:
            st = self.tok.setdefault(t, [dict(), dict()])
            for e, i in st[0].items():
                deps.add(i)
        for t in writes:
            st = self.tok.setdefault(t, [dict(), dict()])
            for e, i in st[0].items():
                if e == key and not dma:
                    continue
                deps.add(i)
            for e, i in st[1].items():
                if e == key and not dma:
                    continue
                deps.add(i)
        for t in reads:
            self.tok[t][1][key] = idx
        for t in writes:
            self.tok[t][0] = {key: idx}
            self.tok[t][1] = {}
        deps.discard(idx)
        self.ops.append(dict(eng=eng, fn=fn, deps=deps, dma=dma, sig=False))
        return idx

    def emit(self, nc):
        ops = self.ops
        for o in ops:
            for d in o['deps']:
                ops[d]['sig'] = True
        tick = {e: 0 for e in self.COMPUTE}
        sems = {e: nc.alloc_semaphore(name="s_" + e) for e in self.COMPUTE}
        NR = 20
        rings = {q: [nc.alloc_semaphore(name="d_%s_%d" % (q, i)) for i in range(NR)] for q in ('sp', 'pool')}
        dcount = {'sp': 0, 'pool': 0}
        for o in ops:
            if o['dma']:
                q = o['eng']
                n = dcount[q]
                dcount[q] += 1
                o['dsem'] = rings[q][n % NR]
                o['dval'] = 16 * (n // NR + 1)
                o['dprev'] = 16 * (n // NR)
            elif o['sig']:
                tick[o['eng']] += 1
                o['tick'] = tick[o['eng']]
        by_eng = {e: [] for e in ('pe', 'act', 'dve', 'pool', 'sp')}
        for i, o in enumerate(ops):
            by_eng[o['eng']].append(i)

        def run(ename, e):
            seen = {}
            for i in by_eng[ename]:
                o = ops[i]
                waits = {}
                for d in o['deps']:
                    p = ops[d]
                    if p['dma']:
                        s, v = p['dsem'], p['dval']
                    else:
                        s, v = sems[p['eng']], p['tick']
                    k = id(s)
                    if seen.get(k, 0) >= v:
                        continue
                    if k not in waits or waits[k][1] < v:
                        waits[k] = (s, v)
                if o['dma'] and o['dprev'] > 0:
                    s, v = o['dsem'], o['dprev']
                    k = id(s)
                    if seen.get(k, 0) < v and (k not in waits or waits[k][1] < v):
                        waits[k] = (s, v)
                for k, (s, v) in waits.items():
                    e.wait_ge(s, v)
                    seen[k] = v
                if o['fn'] is None:
                    continue
                ins = o['fn'](e)
                if o['dma']:
                    ins.then_inc(o['dsem'], 16)
                elif o['sig']:
                    ins.then_inc(sems[ename], 1)

        with nc.Block() as block:
            @block.tensor
            def _(e):
                run('pe', e)

            @block.scalar
            def _(e):
                run('act', e)

            @block.vector
            def _(e):
                run('dve', e)

            @block.gpsimd
            def _(e):
                run('pool', e)

            @block.sync
            def _(e):
                run('sp', e)


class Builder:
    def __init__(self, S, NL, final_norm=True, dbg=False):
        self.S, self.NL, self.final_norm, self.dbg = S, NL, final_norm, dbg
        self.NT = S // 512
        self.NCH = S // 128
        self.nc = bass.Bass("TRN2", target_bir_lowering=False)
        self.s = Sched()
        self.wplan = None
        self.wpos = 0
        self.ps_rr = 0
        self.ps_ring = [0, 1, 2, 3]
        self._alloc()

    def _alloc(self):
        nc, S, NL = self.nc, self.S, self.NL
        dt = nc.dram_tensor
        self.x_in = dt("x", [S, D], F32, kind="ExternalInput").ap()
        self.w_in = dt("w_in", [NL, D, INW], F32, kind="ExternalInput").ap()
        self.w_br = dt("w_br", [NL, 4, W, D], F32, kind="ExternalInput").ap()
        self.w_out = dt("w_out", [NL, D, D], F32, kind="ExternalInput").ap()
        self.pf32_d = dt("pf32", [NL, 128, NP32], F32, kind="ExternalInput").ap()
        self.pbf_d = dt("pbf", [NL, 128, NPBF], F32, kind="ExternalInput").ap()
        self.cf32_d = dt("cf32", [128, NC32], F32, kind="ExternalInput").ap()
        self.cbf_d = dt("cbf", [128, NCBF], F32, kind="ExternalInput").ap()
        self.fg_d = dt("fg", [128, D], F32, kind="ExternalInput").ap()
        self.out_d = dt("out", [S, D], F32, kind="ExternalOutput").ap()
        self.xs_d = [dt("xs%d" % i, [S, D], F32).ap() for i in range(2)]
        if self.dbg:
            self.dbg_d = dt("dbg", [7, S, D], F32, kind="ExternalOutput").ap()

        def sb(name, shape, dtype):
            t = nc.alloc_sbuf_tensor(name, shape, dtype)
            return Buf(t.ap() if hasattr(t, 'ap') else t[:], [name])
        self.sb = sb
        NCH = self.NCH
        self.kT = sb("kT", [128, 4, S], BF16)
        self.vC = sb("vC", [128, NCH, W], BF16)
        self.cTok = sb("cTok", [128, NCH, 128], BF16)
        self.cT = sb("cT", [128, S], BF16)
        self.kiT = sb("kiT", [128, S], BF16)
        self.cumneg = sb("cumneg", [128, NCH, 4], F32)
        self.xc = [sb("xc0", [128, D], F32)] * 2
        self.hT = sb("hT", [128, 8, 512], BF16)
        self.mg = sb("mg", [128, 8, 512], BF16)
        self.yT = sb("yT", [128, 4, 512], BF16)
        self.wslot = [sb("wslot%d" % i, [128, SLOT], BF16) for i in range(NWSLOT)]
        self.pf32 = sb("pf32s", [128, NP32], F32)
        self.pbf = sb("pbfs", [128, NPBF], BF16)
        self.cf32 = sb("cf32s", [128, NC32], F32)
        self.cbf = sb("cbfs", [128, NCBF], BF16)
        self.biasm = sb("biasm", [128, 4, 128], F32)
        self.wTm = sb("wTm", [128, 4, 128], BF16)
        self.halo = sb("halo", [128, 4, 2], F32)
        self.small = sb("small", [128, 64], F32)
        self.bis = sb("bis", [128, 32], F32)
        self.wstat = sb("wstat", [128, 4, 8], F32)
        self.NSCR = 49 * 1024 + (16 * 1024 if self.dbg else 0)
        self.scr_t = nc.alloc_sbuf_tensor("scr", [128, self.NSCR // 2], BF16)
        self.scr_off = 0
        self.ps = []
        for i in range(8):
            t = nc.alloc_psum_tensor("ps%d" % i, [128, 512], F32)
            self.ps.append(Buf(t.ap() if hasattr(t, 'ap') else t[:], ["ps%d" % i]))

    def scr_reset(self):
        self.scr_off = 0

    def scr(self, shape, dtype, align=64):
        n = 1
        for d_ in shape[1:]:
            n *= d_
        nbytes = n * (4 if dtype == F32 else (1 if dtype == U8 else 2))
        off = (self.scr_off + align - 1) // align * align
        assert off + nbytes <= self.NSCR, ("scratch overflow", off, nbytes)
        self.scr_off = off + nbytes
        full = self.scr_t.ap() if hasattr(self.scr_t, 'ap') else self.scr_t[:]
        ap = full[:, off // 2:(off + nbytes) // 2]
        if dtype == F32:
            ap = ap.bitcast(F32)
        elif dtype == U8:
            ap = ap.bitcast(U8)
        if len(shape) == 3:
            ap = ap.rearrange("p (a b) -> p a b", a=shape[1])
        toks = [("scr", g) for g in range(off // GRAN, (off + nbytes - 1) // GRAN + 1)]
        return Buf(ap, toks)

    def psum(self):
        ring = self.ps_ring
        b = self.ps[ring[self.ps_rr % len(ring)]]
        self.ps_rr += 1
        return b

    def interleave(self, *gens):
        gens = list(gens)
        while gens:
            for g in list(gens):
                try:
                    next(g)
                except StopIteration:
                    gens.remove(g)

    def wnext(self, desc):
        if self.wplan is None:
            self.wrec.append(desc)
            return self.wslot[(len(self.wrec) - 1) % NWSLOT]
        j = self.wpos
        assert self.wplan[j] == desc, (self.wplan[j], desc)
        self.wpos += 1
        nxt = j + NWSLOT - 2
        if nxt < len(self.wplan):
            self._wload(nxt)
        return self.wslot[j % NWSLOT]

    def _wload(self, j):
        kind, l, a, b = self.wplan[j]
        slot = self.wslot[j % NWSLOT]
        if kind == 'in':
            src = self.w_in[l, :, a:a + b].rearrange("(kc p) n -> p kc n", p=128)
            dst = slot.ap[:, 0:8 * b].rearrange("p (kc n) -> p kc n", kc=8)
        elif kind == 'br':
            src = self.w_br[l, a][:, b * 512:(b + 1) * 512].rearrange("(wc p) n -> p wc n", p=128)
            dst = slot.ap[:, 0:2048].rearrange("p (wc n) -> p wc n", wc=4)
        else:
            src = self.w_out[l, :, a:a + b].rearrange("(kc p) n -> p kc n", p=128)
            dst = slot.ap[:, 0:8 * b].rearrange("p (kc n) -> p kc n", kc=8)
        self.s.op('pool', lambda e, dst=dst, src=src: e.dma_start(out=dst, in_=src),
                  reads=[], writes=[slot], dma=True)

    def mm(self, out, lhsT, rhs, start, stop, reads, writes):
        self.s.op('pe', lambda e: e.matmul(out, lhsT, rhs, start=start, stop=stop, skip_group_check=True),
                  reads=reads, writes=writes)

    def act(self, out, in_, func, reads, writes, bias=None, scale=None, accum_out=None):
        kw = {}
        if bias is not None:
            kw['bias'] = bias
        if scale is not None:
            kw['scale'] = scale
        if accum_out is not None:
            kw['accum_out'] = accum_out
        self.s.op('act', lambda e: e.activation(out, in_, func, **kw), reads=reads, writes=writes)

    def tt(self, eng, out, in0, in1, op, reads, writes):
        self.s.op(eng, lambda e: e.tensor_tensor(out, in0, in1, op), reads=reads, writes=writes)

    def ts(self, eng, out, in0, s1, s2, op0, op1, reads, writes, accum_out=None):
        if op1 is None:
            self.s.op(eng, lambda e: e.tensor_scalar(out, in0, s1, None, op0), reads=reads, writes=writes)
        elif accum_out is None:
            self.s.op(eng, lambda e: e.tensor_scalar(out, in0, s1, s2, op0, op1), reads=reads, writes=writes)
        else:
            self.s.op(eng, lambda e: e.tensor_scalar(out, in0, s1, s2, op0, op1, accum_out=accum_out),
                      reads=reads, writes=writes)

    def stt(self, eng, out, in0, scalar, in1, op0, op1, reads, writes, accum_out=None):
        if accum_out is None:
            self.s.op(eng, lambda e: e.scalar_tensor_tensor(out, in0, scalar, in1, op0, op1),
                      reads=reads, writes=writes)
        else:
            self.s.op(eng, lambda e: e.scalar_tensor_tensor(out, in0, scalar, in1, op0, op1, accum_out=accum_out),
                      reads=reads, writes=writes)

    def rsqrt(self, out, in_, scale, eps, reads, writes):
        self.act(out, in_, AF.Sqrt, reads=reads, writes=writes, bias=self.epsc, scale=scale)
        self.s.op('dve', lambda e: e.reciprocal(out, out), reads=writes, writes=writes)

    def dma(self, out, in_, reads, writes, q='sp'):
        self.s.op(q, lambda e: e.dma_start(out=out, in_=in_), reads=reads, writes=writes, dma=True)

    def proj_fm_sub(self, wb, sub, ncols=512):
        ps = self.psum()
        wv = wb.ap[:, 0:8 * ncols].rearrange("p (kc n) -> p kc n", kc=8)
        for kc in range(8):
            self.mm(ps.ap, wv[:, kc, sub * 128:(sub + 1) * 128], self.hT.ap[:, kc, :], kc == 0, kc == 7,
                    reads=[wb, self.hT], writes=[ps])
        return ps

    def proj_tm(self, wb, c, col0, ncols, blkcols, ps=None, first=True):
        if ps is None:
            ps = self.psum()
        wv = wb.ap[:, 0:8 * blkcols].rearrange("p (kc n) -> p kc n", kc=8)
        for kc in range(8):
            self.mm(ps.ap[:, 0:ncols], self.hT.ap[:, kc, c * 128:(c + 1) * 128], wv[:, kc, col0:col0 + ncols],
                    first and kc == 0, kc == 7, reads=[wb, self.hT], writes=[ps])
        return ps

    def build(self):
        self.wplan = None
        self.wrec = []
        saved = self.s
        self.s = Sched()
        self.program()
        plan = self.wrec
        self.s = saved
        self.wplan = plan
        self.wpos = 0
        self.ps_rr = 0
        for j in range(min(NWSLOT - 2, len(plan))):
            self._wload(j)
        self.program()
        assert self.wpos == len(plan)
        self.s.emit(self.nc)
        return self.nc

    def program(self):
        S, NL = self.S, self.NL
        self.dma(self.cf32.ap, self.cf32_d, [], [self.cf32])
        self.dma(self.cbf.ap, self.cbf_d, [], [self.cbf], q='pool')
        self.identf = self.cf32.ap[:, 0:128]
        self.Uf = self.cf32.ap[:, 128:256]
        self.E64 = self.cf32.ap[:, 256:384]
        self.E127 = self.cf32.ap[:, 384:512]
        self.onesf = self.cf32.ap[:, 512:640]
        self.epsc = self.cf32.ap[:, 640:641]
        self.onec = self.cf32.ap[:, 641:642]
        self.ident = self.cbf.ap[:, 0:128]
        self.tri01 = self.cbf.ap[:, 128:256]
        self.maskneg = self.cbf.ap[:, 256:384]
        self.ones = self.cbf.ap[:, 384:512]
        out_tokens = []
        for l in range(NL):
            src = self.x_in if l == 0 else self.xs_d[(l - 1) % 2]
            srct = "x_in" if l == 0 else "xs%d" % ((l - 1) % 2)
            last = (l == NL - 1)
            dst = self.out_d if last else self.xs_d[l % 2]
            dstt = "out" if last else "xs%d" % (l % 2)
            self.layer(l, src, srct, dst, dstt, last and self.final_norm)
        toks = [("out", t, c) for t in range(self.NT) for c in range(4)]
        if self.dbg:
            toks.append("dbg")
        self.s.op('sp', None, reads=toks, writes=[])

    def layer(self, l, src, srct, dst, dstt, do_final):
        s = self.s
        self.dma(self.pf32.ap, self.pf32_d[l], [], [self.pf32])
        self.dma(self.pbf.ap, self.pbf_d[l], [], [self.pbf], q='pool')
        pf = self.pf32.ap
        self.gT = pf[:, 0:8]
        self.GT = pf[:, 8:12]
        self.BT = pf[:, 12:16]
        self.bsbc = pf[:, 16:528].rearrange("p (g t) -> p g t", g=4)
        self.kvg = pf[:, 528:656]
        self.cw = pf[:, 656:668].rearrange("p (a b) -> p a b", a=4)
        self.bfbc = pf[:, 668:672]
        pb = self.pbf.ap
        wT = pb[:, 0:512].rearrange("p (g t) -> p g t", g=4)
        self.wuk = pb[:, 512:1024].rearrange("p (h l) -> p h l", h=4)
        self.wuv = pb[:, 1024:1536].rearrange("p (h d) -> p h d", h=4)
        self.tt('dve', self.wTm.ap, wT, self.tri01.unsqueeze(1).to_broadcast([128, 4, 128]), ALU.mult,
                reads=[self.pbf, self.cbf], writes=[self.wTm])
        ps = self.psum()
        self.mm(ps.ap, self.ones, self.wTm.ap.rearrange("p g t -> p (g t)"), True, True,
                reads=[self.cbf, self.wTm], writes=[ps])
        self.tt('dve', self.biasm.ap, ps.ap.rearrange("p (g t) -> p g t", g=4),
                self.BT.unsqueeze(2).to_broadcast([128, 4, 128]), ALU.mult,
                reads=[ps, self.pf32], writes=[self.biasm])
        self.tt('dve', self.biasm.ap, self.biasm.ap, self.bsbc, ALU.add,
                reads=[self.biasm, self.pf32], writes=[self.biasm])
        s.op('dve', lambda e: e.memset(self.halo.ap, 0.0), reads=[], writes=[self.halo])
        for t in range(self.NT):
            self.tile(l, t, src, srct, dst, dstt, do_final)

    def tile(self, l, t, src, srct, dst, dstt, do_final):
        s = self.s
        self.scr_reset()
        sm = self.small
        xc2 = [self.xc[0], self.scr([128, D], F32)]
        xsbs = [self.scr([128, D], BF16) for _ in range(2)]
        junk = self.scr([128, D], BF16)
        for c in range(4):
            gc = 4 * t + c
            xb = xc2[c % 2]
            r0 = gc * 128
            self.dma(xb.ap, src[r0:r0 + 128, :], [(srct, t, c)], [xb])
            xsb = xsbs[c % 2]
            ss = sm.ap[:, c:c + 1]
            s.op('dve', lambda e, ss=ss: e.memset(ss, 0.0), reads=[], writes=[("sm", c)])
            self.act(junk.ap, xb.ap, AF.Square, reads=[xb], writes=[junk, ("sm", c)], accum_out=ss)
            rs = sm.ap[:, 8 + c:9 + c]
            self.rsqrt(rs, ss, 1.0 / D, EPS, reads=[("sm", c)], writes=[("sm", 8 + c)])
            self.ts('dve', xsb.ap, xb.ap, rs, None, ALU.mult, None, reads=[xb, ("sm", 8 + c)], writes=[xsb])
            for half in range(2):
                ps = self.psum()
                for k4 in range(4):
                    kc = half * 4 + k4
                    self.mm(ps.ap[:, k4 * 128:(k4 + 1) * 128], xsb.ap[:, kc * 128:(kc + 1) * 128], self.ident,
                            k4 == 0, k4 == 3, reads=[xsb, self.cbf], writes=[ps])
                self.tt('dve', self.hT.ap[:, half * 4:half * 4 + 4, c * 128:(c + 1) * 128],
                        ps.ap.rearrange("p (k n) -> p k n", k=4),
                        self.gT[:, half * 4:half * 4 + 4].unsqueeze(2).to_broadcast([128, 4, 128]), ALU.mult,
                        reads=[ps, self.pf32], writes=[self.hT])
            self.scr_off -= 0
        self.scr_reset()
        self.branch_A(l, t)
        self.scr_reset()
        self.lift(l, t, 0)
        self.scr_reset()
        self.branch_C(l, t)
        self.scr_reset()
        self.lift(l, t, 2)
        self.scr_reset()
        self.branch_D(l, t)
        self.scr_reset()
        self.lift(l, t, 3)
        self.scr_reset()
        self.branch_B(l, t)
        self.scr_reset()
        self.lift(l, t, 1)
        self.scr_reset()
        self.outproj(l, t, src, srct, dst, dstt, do_final)

    def dbg_dump(self, idx, t):
        if not self.dbg:
            return
        tmp = self.scr([128, 4, 512], F32)
        self.s.op('dve', lambda e: e.tensor_copy(tmp.ap, self.yT.ap), reads=[self.yT], writes=[tmp])
        for wc in range(4):
            dstap = self.dbg_d[idx, t * 512:(t + 1) * 512, wc * 128:(wc + 1) * 128].rearrange("t w -> w t")
            self.s.op('sp', lambda e, dstap=dstap, wc=wc: e.dma_start(out=dstap, in_=tmp.ap[:, wc, :],
                                                                         allow_slow_non_contiguous=True),
                      reads=[tmp], writes=["dbg"], dma=True)

    def branch_A(self, l, t):
        s = self.s
        vn = self.scr([128, 4, W], BF16)
        t1 = self.scr([128, 4, 512], F32)
        st = self.scr([128, 4, 8], F32)
        mv = self.scr([128, 4, 2], F32)
        wb = self.wnext(('in', l, OFF['a_v'], 512))
        for c in range(4):
            ps = self.proj_tm(wb, c, 0, 512, 512)
            s.op('dve', lambda e, c=c, ps=ps: e.bn_stats(st.ap[:, c, 0:6], ps.ap), reads=[ps], writes=[("Ast", c)])
            s.op('dve', lambda e, c=c: e.bn_aggr(mv.ap[:, c, :], st.ap[:, c, 0:6]), reads=[("Ast", c)],
                 writes=[("Amv", c)])
            rstd = st.ap[:, c, 6:7]
            self.rsqrt(rstd, mv.ap[:, c, 1:2], 1.0, EPS, reads=[("Amv", c)], writes=[("Ars", c)])
            self.ts('dve', vn.ap[:, c, :], ps.ap, mv.ap[:, c, 0:1], rstd, ALU.subtract, ALU.mult,
                    reads=[ps, ("Amv", c), ("Ars", c)], writes=[vn])
        for g in range(4):
            ps = self.psum()
            for c in range(4):
                self.mm(ps.ap[:, c * 128:(c + 1) * 128], vn.ap[:, c, g * 128:(g + 1) * 128], self.wTm.ap[:, g, :],
                        c == 0, c == 3, reads=[vn, self.wTm], writes=[ps])
            self.stt('dve', t1.ap[:, g, :].rearrange("p (c n) -> p c n", c=4),
                     ps.ap.rearrange("p (c n) -> p c n", c=4), self.GT[:, g:g + 1],
                     self.biasm.ap[:, g, :].unsqueeze(1).to_broadcast([128, 4, 128]), ALU.mult, ALU.add,
                     reads=[ps, self.pf32, self.biasm], writes=[t1])
        wb = self.wnext(('in', l, OFF['a_u'], 512))
        for g in range(4):
            ps = self.proj_fm_sub(wb, g)
            self.tt('dve', t1.ap[:, g, :], ps.ap, t1.ap[:, g, :], ALU.mult, reads=[ps, t1], writes=[t1])
        wb = self.wnext(('in', l, OFF['a_z'], 512))
        szs = [self.scr([128, 512], F32) for _ in range(2)]
        for g in range(4):
            ps = self.proj_fm_sub(wb, g)
            sz = szs[g % 2]
            self.act(sz.ap, ps.ap, AF.Silu, reads=[ps], writes=[sz])
            self.tt('dve', self.yT.ap[:, g, :], t1.ap[:, g, :], sz.ap, ALU.mult, reads=[t1, sz], writes=[self.yT])
        self.dbg_dump(0, t)

    def lift(self, l, t, b):
        first = (b == 0)
        sgs = [self.scr([128, 512], F32) for _ in range(2)]
        tmps = [self.scr([128, 512], F32) for _ in range(2)]
        for half in range(2):
            wg = self.wnext(('in', l, OFF['gates'] + b * D + half * 512, 512))
            wbr = self.wnext(('br', l, b, half))
            wbv = wbr.ap[:, 0:2048].rearrange("p (wc n) -> p wc n", wc=4)
            for sub in range(4):
                ncn = half * 4 + sub
                psg = self.proj_fm_sub(wg, sub)
                sg = sgs[ncn % 2]
                self.act(sg.ap, psg.ap, AF.Sigmoid, reads=[psg], writes=[sg])
                psb = self.psum()
                for wc in range(4):
                    self.mm(psb.ap, wbv[:, wc, sub * 128:(sub + 1) * 128], self.yT.ap[:, wc, :], wc == 0, wc == 3,
                            reads=[wbr, self.yT], writes=[psb])
                if first:
                    self.tt('dve', self.mg.ap[:, ncn, :], psb.ap, sg.ap, ALU.mult, reads=[psb, sg],
                            writes=[("mg", ncn)])
                else:
                    tmp = tmps[ncn % 2]
                    self.tt('dve', tmp.ap, psb.ap, sg.ap, ALU.mult, reads=[psb, sg], writes=[tmp])
                    self.tt('pool', self.mg.ap[:, ncn, :], self.mg.ap[:, ncn, :], tmp.ap, ALU.add,
                            reads=[tmp, ("mg", ncn)], writes=[("mg", ncn)])

    def branch_C(self, l, t):
        s = self.s
        yc = self.scr([128, 4, 514], F32)
        cv = self.scr([128, 4, 512], F32)
        ccs = [self.scr([128, 512], F32) for _ in range(2)]
        wbc = self.wnext(('in', l, OFF['c_c'], 512))
        wbx = self.wnext(('in', l, OFF['c_x'], 512))
        s.op('dve', lambda e: e.tensor_copy(yc.ap[:, :, 0:2], self.halo.ap), reads=[self.halo], writes=[yc])
        for g in range(4):
            ps = self.proj_fm_sub(wbc, g)
            cc = ccs[g % 2]
            self.act(cc.ap, ps.ap, AF.Copy, reads=[ps], writes=[cc])
            ps2 = self.proj_fm_sub(wbx, g)
            self.tt('dve', yc.ap[:, g, 2:514], ps2.ap, cc.ap, ALU.mult, reads=[ps2, cc], writes=[yc])
            self.ts('dve', cv.ap[:, g, :], yc.ap[:, g, 0:512], self.cw[:, g, 0:1], None, ALU.mult, None,
                    reads=[yc, self.pf32], writes=[cv])
            self.stt('dve', cv.ap[:, g, :], yc.ap[:, g, 1:513], self.cw[:, g, 1:2], cv.ap[:, g, :], ALU.mult, ALU.add,
                     reads=[yc, self.pf32, cv], writes=[cv])
            self.stt('dve', cv.ap[:, g, :], yc.ap[:, g, 2:514], self.cw[:, g, 2:3], cv.ap[:, g, :], ALU.mult, ALU.add,
                     reads=[yc, self.pf32, cv], writes=[cv])
        s.op('dve', lambda e: e.tensor_copy(self.halo.ap, yc.ap[:, :, 512:514]), reads=[yc], writes=[self.halo])
        wbb = self.wnext(('in', l, OFF['c_b'], 512))
        for g in range(4):
            ps = self.proj_fm_sub(wbb, g)
            self.tt('dve', cv.ap[:, g, :], ps.ap, cv.ap[:, g, :], ALU.mult, reads=[ps, cv], writes=[cv])
        wbz = self.wnext(('in', l, OFF['c_z'], 512))
        for g in range(4):
            ps = self.proj_fm_sub(wbz, g)
            sz = ccs[g % 2]
            self.act(sz.ap, ps.ap, AF.Silu, reads=[ps], writes=[sz])
            self.tt('dve', self.yT.ap[:, g, :], cv.ap[:, g, :], sz.ap, ALU.mult, reads=[cv, sz], writes=[self.yT])
        self.dbg_dump(2, t)

    def branch_D(self, l, t):
        s = self.s
        NCH = self.NCH
        qTf = self.scr([128, 4, 512], BF16)
        szD = self.scr([128, 4, 512], BF16)
        biasall = self.scr([128, 4, NCH * 4], F32)
        pTs = [self.scr([128, 512], BF16) for _ in range(3)]
        rden = self.scr([128, 512], F32)
        tmpo = self.scr([128, 512], F32)
        lsb = self.scr([128, 4, 4], F32)
        crefs = self.scr([128, 4, 4], F32)
        wb = self.wnext(('in', l, OFF['d_q'], 512))
        for h in range(4):
            ps = self.proj_fm_sub(wb, h)
            self.act(qTf.ap[:, h, :], ps.ap, AF.Copy, reads=[ps], writes=[qTf], scale=float(128 ** -0.5))
        wb = self.wnext(('in', l, OFF['d_k'], 512))
        for h in range(4):
            ps = self.proj_fm_sub(wb, h)
            s.op('dve', lambda e, h=h, ps=ps: e.tensor_copy(self.kT.ap[:, h, t * 512:(t + 1) * 512], ps.ap),
                 reads=[ps], writes=[("kT", t)])
        wb = self.wnext(('in', l, OFF['d_vf'], 516))
        for c in range(4):
            gc = 4 * t + c
            ps = self.proj_tm(wb, c, 0, 512, 516)
            self.act(self.vC.ap[:, gc, :], ps.ap, AF.Copy, reads=[ps], writes=[("vC", gc)])
            psf = self.proj_tm(wb, c, 512, 4, 516)
            xf = lsb.ap[:, c, :]
            self.tt('dve', xf, psf.ap[:, 0:4], self.bfbc, ALU.add, reads=[psf, self.pf32], writes=[("lsb", c)])
            self.act(xf, xf, AF.Exp, reads=[("lsb", c)], writes=[("lsb", c)], scale=-1.0)
            self.act(xf, xf, AF.Ln, reads=[("lsb", c), self.cf32], writes=[("lsb", c)], bias=self.onec)
            psc = self.psum()
            self.mm(psc.ap[:, 0:4], self.Uf, xf, True, gc == 0, reads=[self.cf32, ("lsb", c)], writes=[psc])
            if gc > 0:
                self.mm(psc.ap[:, 0:4], self.E127, self.cumneg.ap[:, gc - 1, :], False, True,
                        reads=[self.cf32, ("cum", gc - 1)], writes=[psc])
            s.op('dve', lambda e, gc=gc, psc=psc: e.tensor_copy(self.cumneg.ap[:, gc, :], psc.ap[:, 0:4]),
                 reads=[psc], writes=[("cum", gc)])
            if c % 2 == 0:
                psr = self.psum()
                self.mm(psr.ap[:, 0:4], self.E127, self.cumneg.ap[:, gc, :], True, True,
                        reads=[self.cf32, ("cum", gc)], writes=[psr])
                s.op('dve', lambda e, c=c, psr=psr: e.tensor_copy(crefs.ap[:, c // 2, :], psr.ap[:, 0:4]),
                     reads=[psr], writes=[("cref", c // 2)])
            else:
                nk = gc + 1
                m = c // 2
                self.tt('dve', biasall.ap[:, m, 0:nk * 4].rearrange("p (i h) -> p i h", h=4),
                        self.cumneg.ap[:, 0:nk, :], crefs.ap[:, m, :].unsqueeze(1).to_broadcast([128, nk, 4]),
                        ALU.subtract, reads=[("cum", i) for i in range(nk)] + [("cref", m)], writes=[("bias", m)])
        wb = self.wnext(('in', l, OFF['d_z'], 512))
        for h in range(4):
            ps = self.proj_fm_sub(wb, h)
            self.act(szD.ap[:, h, :], ps.ap, AF.Silu, reads=[ps], writes=[szD])
        nkc = 4 * t + 4
        items = [(h, i) for h in range(4) for i in range(nkc)]
        pss_of = {}
        LA = 2

        def S1(k):
            h, i = items[k]
            c0 = max(i - 4 * t, 0) * 128
            pss = self.psum()
            pss_of[k] = pss
            self.mm(pss.ap[:, c0:512], self.kT.ap[:, h, i * 128:(i + 1) * 128], qTf.ap[:, h, c0:512], True, True,
                    reads=[("kT", i // 4), qTf], writes=[pss])

        def S23(k):
            h, i = items[k]
            ci = i - 4 * t
            j0 = max(ci, 0)
            c0 = j0 * 128
            acc_o = self.ps[4 + 2 * (h % 2)]
            acc_d = self.ps[5 + 2 * (h % 2)]
            pss = pss_of.pop(k)
            pT = pTs[k % 3]
            for m in range(j0 // 2, 2):
                a0 = max(c0, m * 256)
                self.act(pT.ap[:, a0:(m + 1) * 256], pss.ap[:, a0:(m + 1) * 256], AF.Exp,
                         reads=[pss, ("bias", m)], writes=[pT],
                         bias=biasall.ap[:, m, i * 4 + h:i * 4 + h + 1])
            if ci >= 0:
                self.tt('pool', pT.ap[:, c0:c0 + 128], pT.ap[:, c0:c0 + 128], self.tri01, ALU.mult,
                        reads=[pT, self.cbf], writes=[pT])
            self.mm(acc_o.ap[:, c0:512], self.vC.ap[:, i, h * 128:(h + 1) * 128], pT.ap[:, c0:512],
                    i == 0, i == nkc - 1, reads=[("vC", i), pT], writes=[acc_o])
            self.mm(acc_d.ap[:, c0:512], self.ones, pT.ap[:, c0:512], i == 0, i == nkc - 1,
                    reads=[self.cbf, pT], writes=[acc_d])
            if i == nkc - 1:
                s.op('dve', lambda e, acc_d=acc_d: e.reciprocal(rden.ap, acc_d.ap), reads=[acc_d], writes=[rden])
                self.tt('dve', tmpo.ap, acc_o.ap, rden.ap, ALU.mult, reads=[acc_o, rden], writes=[tmpo])
                self.tt('dve', self.yT.ap[:, h, :], tmpo.ap, szD.ap[:, h, :], ALU.mult, reads=[tmpo, szD],
                        writes=[self.yT])

        for k in range(min(LA, len(items))):
            S1(k)
        for k in range(len(items)):
            if k + LA < len(items):
                S1(k + LA)
            S23(k)
        self.dbg_dump(3, t)

    def branch_B(self, l, t):
        s = self.s
        S = self.S
        sm = self.small
        K1 = 1024
        qlatT = self.scr([128, 4, 512], BF16)
        szB = self.scr([128, 4, 512], BF16)
        qiT = self.scr([128, 2, 512], BF16)
        scores = [self.scr([128, S], BF16, align=K1) for _ in range(2)]
        junk = self.scr([128, S], U8)
        rhflat = self.scr([128, 2048], BF16, align=K1)
        rh = [Buf(rhflat.ap[:, h * 512:(h + 1) * 512], [rhflat.toks[h]]) for h in range(4)]
        st8 = Buf(rhflat.ap.bitcast(U8), rhflat.toks)
        pmall = self.scr([128, 4, 512], BF16, align=K1)
        pm = [Buf(pmall.ap[:, h, :], [pmall.toks[h]]) for h in range(4)]
        qTb = pmall
        pmT = [self.scr([128, 512], BF16, align=K1) for _ in range(2)]
        Dsg = self.scr([128, 4, 128], BF16, align=K1)
        Dg4 = self.scr([128, 4, 128], F32, align=K1)
        rdbc = self.scr([128, 512], F32, align=K1)
        olat = self.scr([128, 512], BF16, align=K1)
        ctmp = self.scr([128, 128], F32, align=K1)
        kid = self.scr([128, 128], BF16)
        qib = self.scr([128, 256], BF16)
        denp = self.scr([128, 4, 8], F32, align=K1)
        den = self.scr([128, 4], F32)
        wb = self.wnext(('in', l, OFF['b_q'], 512))
        for h in range(4):
            ps = self.proj_fm_sub(wb, h)
            self.act(qTb.ap[:, h, :], ps.ap, AF.Copy, reads=[ps], writes=[qTb])
        for h in range(4):
            ps = self.psum()
            self.mm(ps.ap, self.wuk[:, h, :], qTb.ap[:, h, :], True, True, reads=[self.pbf, qTb], writes=[ps])
            self.act(qlatT.ap[:, h, :], ps.ap, AF.Copy, reads=[ps], writes=[qlatT], scale=float(128 ** -0.5))
        wb = self.wnext(('in', l, OFF['b_misc'], 452))
        for c in range(4):
            gc = 4 * t + c
            ps = self.proj_tm(wb, c, 0, 452, 452)
            ss = sm.ap[:, 16 + c:17 + c]
            s.op('dve', lambda e, ss=ss: e.memset(ss, 0.0), reads=[], writes=[("sm", 16 + c)])
            self.act(ctmp.ap, ps.ap[:, 0:128], AF.Square, reads=[ps], writes=[ctmp, ("sm", 16 + c)], accum_out=ss)
            rs = sm.ap[:, 24 + c:25 + c]
            self.rsqrt(rs, ss, 1.0 / 128, EPS, reads=[("sm", 16 + c)], writes=[("sm", 24 + c)])
            self.stt('dve', self.cTok.ap[:, gc, :], ps.ap[:, 0:128], rs, self.kvg, ALU.mult, ALU.mult,
                     reads=[ps, ("sm", 24 + c), self.pf32], writes=[("cTok", gc)])
            pst = self.psum()
            self.mm(pst.ap[:, 0:128], self.cTok.ap[:, gc, :], self.ident, True, True,
                    reads=[("cTok", gc), self.cbf], writes=[pst])
            self.act(self.cT.ap[:, gc * 128:(gc + 1) * 128], pst.ap[:, 0:128], AF.Copy, reads=[pst],
                     writes=[("cT", gc)])
            s.op('dve', lambda e, ps=ps: e.tensor_copy(kid.ap.rearrange("p (a b) -> p a b", a=2),
                                                       ps.ap[:, 384:448].unsqueeze(1).to_broadcast([128, 2, 64])),
                 reads=[ps], writes=[kid])
            pst = self.psum()
            self.mm(pst.ap[:, 0:128], kid.ap, self.ident, True, True, reads=[kid, self.cbf], writes=[pst])
            self.act(self.kiT.ap[:, gc * 128:(gc + 1) * 128], pst.ap[:, 0:128], AF.Copy, reads=[pst],
                     writes=[("kiT", gc)])
            s.op('dve', lambda e, ps=ps: e.tensor_copy(qib.ap, ps.ap[:, 128:384]), reads=[ps], writes=[qib])
            pst = self.psum()
            for pr in range(2):
                self.mm(pst.ap[:, pr * 128:(pr + 1) * 128], qib.ap[:, pr * 128:(pr + 1) * 128], self.ident,
                        pr == 0, pr == 1, reads=[qib, self.cbf], writes=[pst])
            self.act(qiT.ap[:, :, c * 128:(c + 1) * 128], pst.ap[:, 0:256].rearrange("p (a b) -> p a b", a=2),
                     AF.Copy, reads=[pst], writes=[qiT])
            aw = self.wstat.ap[:, c, 0:4]
            sg = self.wstat.ap[:, c, 4:8]
            self.act(aw, ps.ap[:, 448:452], AF.Abs, reads=[ps], writes=[("aw", c)], scale=0.5 / 8.0)
            self.act(sg, ps.ap[:, 448:452], AF.Sign, reads=[ps], writes=[("sg", c)])
        wb = self.wnext(('in', l, OFF['b_z'], 512))
        for h in range(4):
            ps = self.proj_fm_sub(wb, h)
            self.act(szB.ap[:, h, :], ps.ap, AF.Silu, reads=[ps], writes=[szB])
        acc_o = self.ps[4]
        acc_r = self.ps[5]
        pso = self.ps[5]
        self.ps_ring = [0, 1, 2, 3, 6, 7]
        bis = self.bis.ap
        th = bis[:, 0:1]

        def SC(c):
            gc = 4 * t + c
            nk = gc + 1
            nkt = (nk + 3) // 4
            score = scores[c % 2]
            aw = self.wstat.ap[:, c, 0:4]
            sg = self.wstat.ap[:, c, 4:8]
            self.tt('dve', Dsg.ap, self.ident.unsqueeze(1).to_broadcast([128, 4, 128]),
                    sg.unsqueeze(2).to_broadcast([128, 4, 128]), ALU.mult, reads=[self.cbf, ("sg", c)], writes=[Dsg])
            for kt in range(nkt):
                wdt = min(512, nk * 128 - kt * 512)
                k0 = kt * 512
                for h in range(4):
                    ps = self.psum()
                    r0 = (h % 2) * 64
                    self.mm(ps.ap[:, 0:wdt], qiT.ap[r0:r0 + 64, h // 2, c * 128:(c + 1) * 128],
                            self.kiT.ap[r0:r0 + 64, k0:k0 + wdt], True, True,
                            reads=[qiT] + [("kiT", k0 // 128 + i) for i in range(wdt // 128)], writes=[ps])
                    self.act(rh[h].ap[:, 0:wdt], ps.ap[:, 0:wdt], AF.Relu, reads=[ps, ("aw", c)], writes=[rh[h]],
                             scale=aw[:, h:h + 1])
                pssc = self.psum()
                diag = (gc // 4 == kt)
                for h in range(4):
                    self.mm(pssc.ap[:, 0:wdt], Dsg.ap[:, h, :], rh[h].ap[:, 0:wdt], h == 0, (h == 3 and not diag),
                            reads=[Dsg, rh[h]], writes=[pssc])
                if diag:
                    jd = gc % 4
                    self.mm(pssc.ap[:, jd * 128:(jd + 1) * 128], self.ident, self.maskneg, False, True,
                            reads=[self.cbf], writes=[pssc])
                s.op('dve', lambda e, score=score, pssc=pssc, k0=k0, wdt=wdt:
                     e.tensor_copy(score.ap[:, k0:k0 + wdt], pssc.ap[:, 0:wdt]), reads=[pssc], writes=[score])
                yield

        def BI(c):
            gc = 4 * t + c
            nk = gc + 1
            n = nk * 128
            score = scores[c % 2]
            if gc < 2:
                s.op('dve', lambda e: e.memset(th, -1.0e29), reads=[], writes=[self.bis])
            else:
                s.op('dve', lambda e: e.memset(bis[:, 0:32], 0.0), reads=[], writes=[self.bis])
                s.op('dve', lambda e: e.memset(th, 1.0e-30), reads=[], writes=[self.bis])
                delta = 4.0
                for it in range(NITER):
                    cnt = bis[:, 2 + it:3 + it]
                    self.ts('dve', junk.ap[:, 0:n], score.ap[:, 0:n], th, 0.0, ALU.is_ge, ALU.add,
                            reads=[score, self.bis], writes=[self.bis], accum_out=cnt)
                    sel = bis[:, 1:2]
                    self.ts('dve', sel, cnt, TOPK - 0.5, 2.0 * delta, ALU.is_ge, ALU.mult, reads=[self.bis],
                            writes=[self.bis])
                    self.stt('dve', th, th, -delta, sel, ALU.add, ALU.add, reads=[self.bis], writes=[self.bis])
                    delta *= 0.5
                    yield
                self.ts('dve', th, th, -2.0 * delta, None, ALU.add, None, reads=[self.bis], writes=[self.bis])
                cpos = bis[:, 2:3]
                nz = bis[:, 16:17]
                t1 = bis[:, 17:18]
                t2 = bis[:, 18:19]
                tie = bis[:, 19:20]
                aa = bis[:, 20:21]
                ini = bis[:, 21:22]
                dd = bis[:, 22:23]
                B_ = [self.bis]
                J_ = [("junk",)]
                self.ts('dve', junk.ap[:, 0:n], score.ap[:, 0:n], 0.0, 0.0, ALU.is_equal, ALU.add,
                        reads=[score], writes=J_ + B_, accum_out=nz)
                self.ts('dve', t1, cpos, TOPK - 0.5, None, ALU.is_le, None, reads=B_, writes=B_)
                self.tt('dve', t2, cpos, nz, ALU.add, reads=B_, writes=B_)
                self.stt('dve', tie, t2, TOPK - 0.5, t1, ALU.is_ge, ALU.mult, reads=B_, writes=B_)
                self.ts('dve', aa, cpos, 2.0, -(2.0 * TOPK + 2.0), ALU.mult, ALU.add, reads=B_, writes=B_)
                self.ts('dve', ini, aa, tie, 4.0, ALU.mult, ALU.add, reads=B_, writes=B_)
                self.ts('dve', dd, tie, -1.0, 1.0, ALU.mult, ALU.add, reads=B_, writes=B_)
                self.tt('dve', th, th, dd, ALU.mult, reads=B_, writes=B_)
                self.stt('dve', th, tie, 1.0e-30, th, ALU.mult, ALU.add, reads=B_, writes=B_)
                s.op('dve', lambda e, n=n: e.tensor_tensor_scan(st8.ap[:, 0:n], junk.ap[:, 0:n], junk.ap[:, 0:n], ini,
                                                                ALU.add, ALU.add), reads=J_ + B_, writes=[st8])
                self.stt('dve', junk.ap[:, 0:n], st8.ap[:, 0:n], 2.5, junk.ap[:, 0:n], ALU.is_le, ALU.mult,
                         reads=[st8] + J_, writes=J_)
                self.stt('dve', score.ap[:, 0:n], junk.ap[:, 0:n], 1.0, score.ap[:, 0:n], ALU.mult, ALU.add,
                         reads=J_ + [score], writes=[score])
                yield
            if self.dbg:
                self.dma(self.dbg_d[5, gc * 128:(gc + 1) * 128, 0:32], self.bis.ap, [self.bis], ["dbg"])
                self.dma(self.dbg_d[6, gc * 128:(gc + 1) * 128, 0:nk * 128], score.ap[:, 0:nk * 128], [score],
                         ["dbg"], q='pool')
            self.ts('dve', score.ap[:, 0:n], score.ap[:, 0:n], th, -30000.0, ALU.is_lt, ALU.mult,
                    reads=[score, self.bis], writes=[score])
            yield

        def AT(c):
            gc = 4 * t + c
            nk = gc + 1
            nkt = (nk + 3) // 4
            score = scores[c % 2]
            s.op('dve', lambda e: e.memset(denp.ap, 0.0), reads=[], writes=[denp])
            for kt in range(nkt):
                wdt = min(512, nk * 128 - kt * 512)
                k0 = kt * 512
                for h in range(4):
                    ps = self.psum()
                    self.mm(ps.ap[:, 0:wdt], qlatT.ap[:, h, c * 128:(c + 1) * 128], self.cT.ap[:, k0:k0 + wdt],
                            True, False, reads=[qlatT] + [("cT", k0 // 128 + i) for i in range(wdt // 128)],
                            writes=[ps])
                    self.mm(ps.ap[:, 0:wdt], self.ident, score.ap[:, k0:k0 + wdt], False, True,
                            reads=[self.cbf, score], writes=[ps])
                    self.act(pm[h].ap[:, 0:wdt], ps.ap[:, 0:wdt], AF.Exp, reads=[ps], writes=[pm[h], denp],
                             accum_out=denp.ap[:, h, kt:kt + 1])
                yield
                njj = wdt // 128
                psts = {}

                def T(jj):
                    pst = self.psum()
                    psts[jj] = pst
                    for h in range(4):
                        self.mm(pst.ap[:, h * 128:(h + 1) * 128], pm[h].ap[:, jj * 128:(jj + 1) * 128], self.ident,
                                h == 0, h == 3, reads=[pm[h], self.cbf], writes=[pst])
                T(0)
                for jj in range(njj):
                    if jj + 1 < njj:
                        T(jj + 1)
                    kc = k0 // 128 + jj
                    pst = psts.pop(jj)
                    pT = pmT[kc % 2]
                    self.act(pT.ap, pst.ap, AF.Copy, reads=[pst], writes=[pT])
                    self.mm(acc_o.ap, self.cTok.ap[:, kc, :], pT.ap, kc == 0, kc == nk - 1,
                            reads=[("cTok", kc), pT], writes=[acc_o])
                    yield
            s.op('dve', lambda e: e.tensor_reduce(den.ap, denp.ap, mybir.AxisListType.X, ALU.add),
                 reads=[denp], writes=[den])
            s.op('dve', lambda e: e.reciprocal(den.ap, den.ap), reads=[den], writes=[den])
            self.tt('dve', Dg4.ap, self.identf.unsqueeze(1).to_broadcast([128, 4, 128]),
                    den.ap.unsqueeze(2).to_broadcast([128, 4, 128]), ALU.mult, reads=[self.cf32, den], writes=[Dg4])
            self.mm(acc_r.ap, self.onesf, Dg4.ap.rearrange("p h n -> p (h n)"), True, True,
                    reads=[self.cf32, Dg4], writes=[acc_r])
            self.act(rdbc.ap, acc_r.ap, AF.Copy, reads=[acc_r], writes=[rdbc])
            self.tt('dve', olat.ap, acc_o.ap, rdbc.ap, ALU.mult, reads=[acc_o, rdbc], writes=[olat])
            yield
            for h in range(4):
                self.mm(pso.ap[:, h * 128:(h + 1) * 128], self.wuv[:, h, :], olat.ap[:, h * 128:(h + 1) * 128],
                        h == 0, h == 3, reads=[self.pbf, olat], writes=[pso])
            self.tt('dve', self.yT.ap[:, :, c * 128:(c + 1) * 128], pso.ap.rearrange("p (h n) -> p h n", h=4),
                    szB.ap[:, :, c * 128:(c + 1) * 128], ALU.mult, reads=[pso, szB], writes=[self.yT])
            yield

        def chain(*gens):
            for g in gens:
                for _ in g:
                    yield

        self.interleave(SC(0))
        self.interleave(BI(0), SC(1))
        for c in range(4):
            first = chain(AT(c), SC(c + 2)) if c + 2 < 4 else AT(c)
            if c + 1 < 4:
                self.interleave(first, BI(c + 1))
            else:
                self.interleave(first)
        self.ps_ring = [0, 1, 2, 3]
        self.dbg_dump(1, t)

    def outproj(self, l, t, src, srct, dst, dstt, do_final):
        s = self.s
        sm = self.small
        mgb = Buf(self.mg.ap, [("mg", i) for i in range(8)])
        if self.dbg and l == 0:
            tmp = self.scr([128, 8, 512], F32)
            self.s.op('dve', lambda e: e.tensor_copy(tmp.ap, self.mg.ap), reads=[("mg", i) for i in range(8)],
                      writes=[tmp])
            for ncn in range(8):
                dstap = self.dbg_d[4, t * 512:(t + 1) * 512, ncn * 128:(ncn + 1) * 128].rearrange("t w -> w t")
                self.s.op('sp', lambda e, dstap=dstap, ncn=ncn: e.dma_start(out=dstap, in_=tmp.ap[:, ncn, :],
                                                                               allow_slow_non_contiguous=True),
                          reads=[tmp], writes=["dbg"], dma=True)
        wos = [self.wnext(('out', l, half * 512, 512)) for half in range(2)]
        if do_final:
            fgb = self.scr([128, D], F32)
            self.dma(fgb.ap, self.fg_d, [], [fgb])
            junk = self.scr([128, D], BF16)
        xc2 = [self.xc[0], self.scr([128, D], F32)]
        for c in range(4):
            gc = 4 * t + c
            r0 = gc * 128
            xb = xc2[c % 2]
            self.dma(xb.ap, src[r0:r0 + 128, :], [(srct, t, c)], [xb])
            for half in range(2):
                wv = wos[half].ap[:, 0:4096].rearrange("p (kc n) -> p kc n", kc=8)
                ps = self.psum()
                for ncn in range(8):
                    self.mm(ps.ap, mgb.ap[:, ncn, c * 128:(c + 1) * 128], wv[:, ncn, :], ncn == 0, ncn == 7,
                            reads=[mgb, wos[half]], writes=[ps])
                self.tt('dve', xb.ap[:, half * 512:(half + 1) * 512], ps.ap, xb.ap[:, half * 512:(half + 1) * 512],
                        ALU.add, reads=[ps, xb], writes=[xb])
            if do_final:
                ss = sm.ap[:, 32 + c:33 + c]
                s.op('dve', lambda e, ss=ss: e.memset(ss, 0.0), reads=[], writes=[("sm", 32 + c)])
                self.act(junk.ap, xb.ap, AF.Square, reads=[xb], writes=[junk, ("sm", 32 + c)], accum_out=ss)
                rs = sm.ap[:, 40 + c:41 + c]
                self.rsqrt(rs, ss, 1.0 / D, EPS, reads=[("sm", 32 + c)], writes=[("sm", 40 + c)])
                self.stt('dve', xb.ap, xb.ap, rs, fgb.ap, ALU.mult, ALU.mult, reads=[xb, ("sm", 40 + c), fgb],
                         writes=[xb])
            self.dma(dst[r0:r0 + 128, :], xb.ap, [xb], [(dstt, t, c)])


def _consts():
    i = np.arange(128)
    ident = np.eye(128, dtype=np.float32)
    tri = (i[:, None] <= i[None, :]).astype(np.float32)
    e64 = np.zeros((128, 128), np.float32)
    e64[64, :] = 1.0
    e127 = np.zeros((128, 128), np.float32)
    e127[127, :] = 1.0
    ones = np.ones((128, 128), np.float32)
    maskneg = np.where(i[None, :] <= i[:, None], 0.0, NEG).astype(np.float32)
    extra = np.zeros((128, 8), np.float32)
    extra[:, 0] = EPS
    extra[:, 1] = 1.0
    cf32 = np.concatenate([ident, tri, e64, e127, ones, extra], axis=1)
    cbf = np.concatenate([ident, tri, maskneg, ones], axis=1)
    return np.ascontiguousarray(cf32), np.ascontiguousarray(cbf)


def _pack_params(NL, norm_g, gm_ln_g, gm_ln_b, gm_w_s, gm_b_s, dsa_kv_g, dsa_w_uk, dsa_w_uv, conv_w, fox_b_f):
    pf = np.zeros((NL, 128, NP32), np.float32)
    pb = np.zeros((NL, 128, NPBF), np.float32)
    for l in range(NL):
        pf[l, :, 0:8] = norm_g[l].reshape(8, 128).T
        pf[l, :, 8:12] = gm_ln_g[l].reshape(4, 128).T
        pf[l, :, 12:16] = gm_ln_b[l].reshape(4, 128).T
        pf[l, :, 16:528] = np.broadcast_to(gm_b_s[l].reshape(1, 512), (128, 512))
        pf[l, :, 528:656] = np.broadcast_to(dsa_kv_g[l].reshape(1, 128), (128, 128))
        pf[l, :, 656:668] = conv_w[l].reshape(3, 4, 128).transpose(2, 1, 0).reshape(128, 12)
        pf[l, :, 668:672] = np.broadcast_to(fox_b_f[l].reshape(1, 4), (128, 4))
        pb[l, :, 0:512] = gm_w_s[l].transpose(2, 0, 1).reshape(128, 512)
        pb[l, :, 512:1024] = dsa_w_uk[l].transpose(2, 0, 1).reshape(128, 512)
        pb[l, :, 1024:1536] = dsa_w_uv[l].transpose(1, 0, 2).reshape(128, 512)
    return pf, pb


_CACHE = {}


def _get_program(S, NL, final_norm, dbg):
    key = (S, NL, final_norm, dbg)
    if key not in _CACHE:
        _CACHE[key] = Builder(S, NL, final_norm, dbg).build()
    return _CACHE[key]


def run_layers(x, norm_g, w_in, gm_ln_g, gm_ln_b, gm_w_s, gm_b_s, dsa_kv_g, dsa_w_uk, dsa_w_uv,
               conv_w, fox_b_f, w_branch, w_out, final_g, final_norm=True, dbg=False):
    x = np.asarray(x, np.float32)
    B, S, _ = x.shape
    NL = int(np.asarray(w_in).shape[0])
    f = lambda a: np.ascontiguousarray(np.asarray(a, np.float32))
    nc = _get_program(S, NL, final_norm, dbg)
    cf32, cbf = _consts()
    pf, pb = _pack_params(NL, f(norm_g), f(gm_ln_g), f(gm_ln_b), f(gm_w_s), f(gm_b_s), f(dsa_kv_g), f(dsa_w_uk),
                          f(dsa_w_uv), f(conv_w), f(fox_b_f))
    fg = np.ascontiguousarray(np.broadcast_to(f(final_g).reshape(1, D), (128, D)))
    shared = dict(w_in=f(w_in), w_br=f(w_branch), w_out=f(w_out), pf32=pf, pbf=pb, cf32=cf32, cbf=cbf, fg=fg)
    in_maps = []
    for b in range(B):
        m = dict(shared)
        m["x"] = np.ascontiguousarray(x[b])
        in_maps.append(m)
    res = run_bass_kernel_spmd(nc, in_maps, core_ids=list(range(B)))
    out = np.stack([np.asarray(r["out"], np.float32) for r in res.results], axis=0)
    if dbg:
        return out, np.stack([np.asarray(r["dbg"], np.float32) for r in res.results], axis=0)
    return out


def kernel(x, norm_g, w_in, gm_ln_g, gm_ln_b, gm_w_s, gm_b_s, dsa_kv_g, dsa_w_uk, dsa_w_uv,
           conv_w, fox_b_f, w_branch, w_out, final_g):
    return run_layers(x, norm_g, w_in, gm_ln_g, gm_ln_b, gm_w_s, gm_b_s, dsa_kv_g, dsa_w_uk, dsa_w_uv,
                      conv_w, fox_b_f, w_branch, w_out, final_g)
```

```python
import numpy as np
import concourse.bass as bass
import concourse.mybir as mybir
from concourse.bass_utils import run_bass_kernel_spmd

F32 = mybir.dt.float32
BF16 = mybir.dt.bfloat16
U8 = mybir.dt.uint8
ALU = mybir.AluOpType
AF = mybir.ActivationFunctionType

D = 1024
W = 512
INW = 11208
OFF = dict(a_u=0, a_v=512, a_z=1024, b_q=1536, b_misc=2048, b_z=2500,
           c_b=3012, c_c=3524, c_x=4036, c_z=4548,
           d_q=5060, d_k=5572, d_vf=6084, d_z=6600, gates=7112)
EPS = 1e-6
TOPK = 256
NITER = 13
NEG = -1.0e30
NP32 = 672
NPBF = 1536
NC32 = 648
NCBF = 512
SLOT = 4160
NWSLOT = 4
GRAN = 1024


class Buf:
    def __init__(self, ap, toks):
        self.ap = ap
        self.toks = toks


class Sched:
    COMPUTE = ('pe', 'act', 'dve', 'pool')

    def __init__(self):
        self.ops = []
        self.tok = {}
        self.ndma = {}

    def _flat(self, lst):
        out = []
        for x in lst:
            if x is None:
                continue
            if isinstance(x, Buf):
                out.extend(x.toks)
            elif isinstance(x, (list, tuple)) and len(x) > 0 and isinstance(x[0], (Buf, list)):
                out.extend(self._flat(x))
            else:
                out.append(x)
        return out

    def op(self, eng, fn, reads=(), writes=(), dma=False):
        idx = len(self.ops)
        reads = self._flat(reads)
        writes = self._flat(writes)
        key = ('dma', idx) if dma else eng
        deps = set()
        for t in reads:
            st = self.tok.setdefault(t, [dict(), dict()])
            for e, i in st[0].items():
                deps.add(i)
        for t in writes:
            st = self.tok.setdefault(t, [dict(), dict()])
            for e, i in st[0].items():
                if e == key and not dma:
                    continue
                deps.add(i)
            for e, i in st[1].items():
                if e == key and not dma:
                    continue
                deps.add(i)
        for t in reads:
            self.tok[t][1][key] = idx
        for t in writes:
            self.tok[t][0] = {key: idx}
            self.tok[t][1] = {}
        deps.discard(idx)
        self.ops.append(dict(eng=eng, fn=fn, deps=deps, dma=dma, sig=False))
        return idx

    def emit(self, nc):
        ops = self.ops
        for o in ops:
            for d in o['deps']:
                ops[d]['sig'] = True
        tick = {e: 0 for e in self.COMPUTE}
        sems = {e: nc.alloc_semaphore(name="s_" + e) for e in self.COMPUTE}
        NR = 20
        rings = {q: [nc.alloc_semaphore(name="d_%s_%d" % (q, i)) for i in range(NR)] for q in ('sp', 'pool')}
        dcount = {'sp': 0, 'pool': 0}
        for o in ops:
            if o['dma']:
                q = o['eng']
                n = dcount[q]
                dcount[q] += 1
                o['dsem'] = rings[q][n % NR]
                o['dval'] = 16 * (n // NR + 1)
                o['dprev'] = 16 * (n // NR)
            elif o['sig']:
                tick[o['eng']] += 1
                o['tick'] = tick[o['eng']]
        by_eng = {e: [] for e in ('pe', 'act', 'dve', 'pool', 'sp')}
        for i, o in enumerate(ops):
            by_eng[o['eng']].append(i)

        def run(ename, e):
            seen = {}
            for i in by_eng[ename]:
                o = ops[i]
                waits = {}
                for d in o['deps']:
                    p = ops[d]
                    if p['dma']:
                        s, v = p['dsem'], p['dval']
                    else:
                        s, v = sems[p['eng']], p['tick']
                    k = id(s)
                    if seen.get(k, 0) >= v:
                        continue
                    if k not in waits or waits[k][1] < v:
                        waits[k] = (s, v)
                if o['dma'] and o['dprev'] > 0:
                    s, v = o['dsem'], o['dprev']
                    k = id(s)
                    if seen.get(k, 0) < v and (k not in waits or waits[k][1] < v):
                        waits[k] = (s, v)
                for k, (s, v) in waits.items():
                    e.wait_ge(s, v)
                    seen[k] = v
                if o['fn'] is None:
                    continue
                ins = o['fn'](e)
                if o['dma']:
                    ins.then_inc(o['dsem'], 16)
                elif o['sig']:
                    ins.then_inc(sems[ename], 1)

        with nc.Block() as block:
            @block.tensor
            def _(e):
                run('pe', e)

            @block.scalar
            def _(e):
                run('act', e)

            @block.vector
            def _(e):
                run('dve', e)

            @block.gpsimd
            def _(e):
                run('pool', e)

            @block.sync
            def _(e):
                run('sp', e)


class Builder:
    def __init__(self, S, NL, final_norm=True, dbg=False):
        self.S, self.NL, self.final_norm, self.dbg = S, NL, final_norm, dbg
        self.NT = S // 512
        self.NCH = S // 128
        self.nc = bass.Bass("TRN2", target_bir_lowering=False)
        self.s = Sched()
        self.wplan = None
        self.wpos = 0
        self.ps_rr = 0
        self.ps_ring = [0, 1, 2, 3]
        self._alloc()

    def _alloc(self):
        nc, S, NL = self.nc, self.S, self.NL
        dt = nc.dram_tensor
        self.x_in = dt("x", [S, D], F32, kind="ExternalInput").ap()
        self.w_in = dt("w_in", [NL, D, INW], F32, kind="ExternalInput").ap()
        self.w_br = dt("w_br", [NL, 4, W, D], F32, kind="ExternalInput").ap()
        self.w_out = dt("w_out", [NL, D, D], F32, kind="ExternalInput").ap()
        self.pf32_d = dt("pf32", [NL, 128, NP32], F32, kind="ExternalInput").ap()
        self.pbf_d = dt("pbf", [NL, 128, NPBF], F32, kind="ExternalInput").ap()
        self.cf32_d = dt("cf32", [128, NC32], F32, kind="ExternalInput").ap()
        self.cbf_d = dt("cbf", [128, NCBF], F32, kind="ExternalInput").ap()
        self.fg_d = dt("fg", [128, D], F32, kind="ExternalInput").ap()
        self.out_d = dt("out", [S, D], F32, kind="ExternalOutput").ap()
        self.xs_d = [dt("xs%d" % i, [S, D], F32).ap() for i in range(2)]
        if self.dbg:
            self.dbg_d = dt("dbg", [7, S, D], F32, kind="ExternalOutput").ap()

        def sb(name, shape, dtype):
            t = nc.alloc_sbuf_tensor(name, shape, dtype)
            return Buf(t.ap() if hasattr(t, 'ap') else t[:], [name])
        self.sb = sb
        NCH = self.NCH
        self.kT = sb("kT", [128, 4, S], BF16)
        self.vC = sb("vC", [128, NCH, W], BF16)
        self.cTok = sb("cTok", [128, NCH, 128], BF16)
        self.cT = sb("cT", [128, S], BF16)
        self.kiT = sb("kiT", [128, S], BF16)
        self.cumneg = sb("cumneg", [128, NCH, 4], F32)
        self.xc = [sb("xc0", [128, D], F32)] * 2
        self.hT = sb("hT", [128, 8, 512], BF16)
        self.mg = sb("mg", [128, 8, 512], BF16)
        self.yT = sb("yT", [128, 4, 512], BF16)
        self.wslot = [sb("wslot%d" % i, [128, SLOT], BF16) for i in range(NWSLOT)]
        self.pf32 = sb("pf32s", [128, NP32], F32)
        self.pbf = sb("pbfs", [128, NPBF], BF16)
        self.cf32 = sb("cf32s", [128, NC32], F32)
        self.cbf = sb("cbfs", [128, NCBF], BF16)
        self.biasm = sb("biasm", [128, 4, 128], F32)
        self.wTm = sb("wTm", [128, 4, 128], BF16)
        self.halo = sb("halo", [128, 4, 2], F32)
        self.small = sb("small", [128, 64], F32)
        self.bis = sb("bis", [128, 32], F32)
        self.wstat = sb("wstat", [128, 4, 8], F32)
        self.NSCR = 49 * 1024 + (16 * 1024 if self.dbg else 0)
        self.scr_t = nc.alloc_sbuf_tensor("scr", [128, self.NSCR // 2], BF16)
        self.scr_off = 0
        self.ps = []
        for i in range(8):
            t = nc.alloc_psum_tensor("ps%d" % i, [128, 512], F32)
            self.ps.append(Buf(t.ap() if hasattr(t, 'ap') else t[:], ["ps%d" % i]))

    def scr_reset(self):
        self.scr_off = 0

    def scr(self, shape, dtype, align=64):
        n = 1
        for d_ in shape[1:]:
            n *= d_
        nbytes = n * (4 if dtype == F32 else (1 if dtype == U8 else 2))
        off = (self.scr_off + align - 1) // align * align
        assert off + nbytes <= self.NSCR, ("scratch overflow", off, nbytes)
        self.scr_off = off + nbytes
        full = self.scr_t.ap() if hasattr(self.scr_t, 'ap') else self.scr_t[:]
        ap = full[:, off // 2:(off + nbytes) // 2]
        if dtype == F32:
            ap = ap.bitcast(F32)
        elif dtype == U8:
            ap = ap.bitcast(U8)
        if len(shape) == 3:
            ap = ap.rearrange("p (a b) -> p a b", a=shape[1])
        toks = [("scr", g) for g in range(off // GRAN, (off + nbytes - 1) // GRAN + 1)]
        return Buf(ap, toks)

    def psum(self):
        ring = self.ps_ring
        b = self.ps[ring[self.ps_rr % len(ring)]]
        self.ps_rr += 1
        return b

    def interleave(self, *gens):
        gens = list(gens)
        while gens:
            for g in list(gens):
                try:
                    next(g)
                except StopIteration:
                    gens.remove(g)

    def wnext(self, desc):
        if self.wplan is None:
            self.wrec.append(desc)
            return self.wslot[(len(self.wrec) - 1) % NWSLOT]
        j = self.wpos
        assert self.wplan[j] == desc, (self.wplan[j], desc)
        self.wpos += 1
        nxt = j + NWSLOT - 2
        if nxt < len(self.wplan):
            self._wload(nxt)
        return self.wslot[j % NWSLOT]

    def _wload(self, j):
        kind, l, a, b = self.wplan[j]
        slot = self.wslot[j % NWSLOT]
        if kind == 'in':
            src = self.w_in[l, :, a:a + b].rearrange("(kc p) n -> p kc n", p=128)
            dst = slot.ap[:, 0:8 * b].rearrange("p (kc n) -> p kc n", kc=8)
        elif kind == 'br':
            src = self.w_br[l, a][:, b * 512:(b + 1) * 512].rearrange("(wc p) n -> p wc n", p=128)
            dst = slot.ap[:, 0:2048].rearrange("p (wc n) -> p wc n", wc=4)
        else:
            src = self.w_out[l, :, a:a + b].rearrange("(kc p) n -> p kc n", p=128)
            dst = slot.ap[:, 0:8 * b].rearrange("p (kc n) -> p kc n", kc=8)
        self.s.op('pool', lambda e, dst=dst, src=src: e.dma_start(out=dst, in_=src),
                  reads=[], writes=[slot], dma=True)

    def mm(self, out, lhsT, rhs, start, stop, reads, writes):
        self.s.op('pe', lambda e: e.matmul(out, lhsT, rhs, start=start, stop=stop, skip_group_check=True),
                  reads=reads, writes=writes)

    def act(self, out, in_, func, reads, writes, bias=None, scale=None, accum_out=None):
        kw = {}
        if bias is not None:
            kw['bias'] = bias
        if scale is not None:
            kw['scale'] = scale
        if accum_out is not None:
            kw['accum_out'] = accum_out
        self.s.op('act', lambda e: e.activation(out, in_, func, **kw), reads=reads, writes=writes)

    def tt(self, eng, out, in0, in1, op, reads, writes):
        self.s.op(eng, lambda e: e.tensor_tensor(out, in0, in1, op), reads=reads, writes=writes)

    def ts(self, eng, out, in0, s1, s2, op0, op1, reads, writes, accum_out=None):
        if op1 is None:
            self.s.op(eng, lambda e: e.tensor_scalar(out, in0, s1, None, op0), reads=reads, writes=writes)
        elif accum_out is None:
            self.s.op(eng, lambda e: e.tensor_scalar(out, in0, s1, s2, op0, op1), reads=reads, writes=writes)
        else:
            self.s.op(eng, lambda e: e.tensor_scalar(out, in0, s1, s2, op0, op1, accum_out=accum_out),
                      reads=reads, writes=writes)

    def stt(self, eng, out, in0, scalar, in1, op0, op1, reads, writes, accum_out=None):
        if accum_out is None:
            self.s.op(eng, lambda e: e.scalar_tensor_tensor(out, in0, scalar, in1, op0, op1),
                      reads=reads, writes=writes)
        else:
            self.s.op(eng, lambda e: e.scalar_tensor_tensor(out, in0, scalar, in1, op0, op1, accum_out=accum_out),
                      reads=reads, writes=writes)

    def rsqrt(self, out, in_, scale, eps, reads, writes):
        self.act(out, in_, AF.Sqrt, reads=reads, writes=writes, bias=self.epsc, scale=scale)
        self.s.op('dve', lambda e: e.reciprocal(out, out), reads=writes, writes=writes)

    def dma(self, out, in_, reads, writes, q='sp'):
        self.s.op(q, lambda e: e.dma_start(out=out, in_=in_), reads=reads, writes=writes, dma=True)

    def proj_fm_sub(self, wb, sub, ncols=512):
        ps = self.psum()
        wv = wb.ap[:, 0:8 * ncols].rearrange("p (kc n) -> p kc n", kc=8)
        for kc in range(8):
            self.mm(ps.ap, wv[:, kc, sub * 128:(sub + 1) * 128], self.hT.ap[:, kc, :], kc == 0, kc == 7,
                    reads=[wb, self.hT], writes=[ps])
        return ps

    def proj_tm(self, wb, c, col0, ncols, blkcols, ps=None, first=True):
        if ps is None:
            ps = self.psum()
        wv = wb.ap[:, 0:8 * blkcols].rearrange("p (kc n) -> p kc n", kc=8)
        for kc in range(8):
            self.mm(ps.ap[:, 0:ncols], self.hT.ap[:, kc, c * 128:(c + 1) * 128], wv[:, kc, col0:col0 + ncols],
                    first and kc == 0, kc == 7, reads=[wb, self.hT], writes=[ps])
        return ps

    def build(self):
        self.wplan = None
        self.wrec = []
        saved = self.s
        self.s = Sched()
        self.program()
        plan = self.wrec
        self.s = saved
        self.wplan = plan
        self.wpos = 0
        self.ps_rr = 0
        for j in range(min(NWSLOT - 2, len(plan))):
            self._wload(j)
        self.program()
        assert self.wpos == len(plan)
        self.s.emit(self.nc)
        return self.nc

    def program(self):
        S, NL = self.S, self.NL
        self.dma(self.cf32.ap, self.cf32_d, [], [self.cf32])
        self.dma(self.cbf.ap, self.cbf_d, [], [self.cbf], q='pool')
        self.identf = self.cf32.ap[:, 0:128]
        self.Uf = self.cf32.ap[:, 128:256]
        self.E64 = self.cf32.ap[:, 256:384]
        self.E127 = self.cf32.ap[:, 384:512]
        self.onesf = self.cf32.ap[:, 512:640]
        self.epsc = self.cf32.ap[:, 640:641]
        self.onec = self.cf32.ap[:, 641:642]
        self.ident = self.cbf.ap[:, 0:128]
        self.tri01 = self.cbf.ap[:, 128:256]
        self.maskneg = self.cbf.ap[:, 256:384]
        self.ones = self.cbf.ap[:, 384:512]
        out_tokens = []
        for l in range(NL):
            src = self.x_in if l == 0 else self.xs_d[(l - 1) % 2]
            srct = "x_in" if l == 0 else "xs%d" % ((l - 1) % 2)
            last = (l == NL - 1)
            dst = self.out_d if last else self.xs_d[l % 2]
            dstt = "out" if last else "xs%d" % (l % 2)
            self.layer(l, src, srct, dst, dstt, last and self.final_norm)
        toks = [("out", t, c) for t in range(self.NT) for c in range(4)]
        if self.dbg:
            toks.append("dbg")
        self.s.op('sp', None, reads=toks, writes=[])

    def layer(self, l, src, srct, dst, dstt, do_final):
        s = self.s
        self.dma(self.pf32.ap, self.pf32_d[l], [], [self.pf32])
        self.dma(self.pbf.ap, self.pbf_d[l], [], [self.pbf], q='pool')
        pf = self.pf32.ap
        self.gT = pf[:, 0:8]
        self.GT = pf[:, 8:12]
        self.BT = pf[:, 12:16]
        self.bsbc = pf[:, 16:528].rearrange("p (g t) -> p g t", g=4)
        self.kvg = pf[:, 528:656]
        self.cw = pf[:, 656:668].rearrange("p (a b) -> p a b", a=4)
        self.bfbc = pf[:, 668:672]
        pb = self.pbf.ap
        wT = pb[:, 0:512].rearrange("p (g t) -> p g t", g=4)
        self.wuk = pb[:, 512:1024].rearrange("p (h l) -> p h l", h=4)
        self.wuv = pb[:, 1024:1536].rearrange("p (h d) -> p h d", h=4)
        self.tt('dve', self.wTm.ap, wT, self.tri01.unsqueeze(1).to_broadcast([128, 4, 128]), ALU.mult,
                reads=[self.pbf, self.cbf], writes=[self.wTm])
        ps = self.psum()
        self.mm(ps.ap, self.ones, self.wTm.ap.rearrange("p g t -> p (g t)"), True, True,
                reads=[self.cbf, self.wTm], writes=[ps])
        self.tt('dve', self.biasm.ap, ps.ap.rearrange("p (g t) -> p g t", g=4),
                self.BT.unsqueeze(2).to_broadcast([128, 4, 128]), ALU.mult,
                reads=[ps, self.pf32], writes=[self.biasm])
        self.tt('dve', self.biasm.ap, self.biasm.ap, self.bsbc, ALU.add,
                reads=[self.biasm, self.pf32], writes=[self.biasm])
        s.op('dve', lambda e: e.memset(self.halo.ap, 0.0), reads=[], writes=[self.halo])
        for t in range(self.NT):
            self.tile(l, t, src, srct, dst, dstt, do_final)

    def tile(self, l, t, src, srct, dst, dstt, do_final):
        s = self.s
        self.scr_reset()
        sm = self.small
        xc2 = [self.xc[0], self.scr([128, D], F32)]
        xsbs = [self.scr([128, D], BF16) for _ in range(2)]
        junk = self.scr([128, D], BF16)
        for c in range(4):
            gc = 4 * t + c
            xb = xc2[c % 2]
            r0 = gc * 128
            self.dma(xb.ap, src[r0:r0 + 128, :], [(srct, t, c)], [xb])
            xsb = xsbs[c % 2]
            ss = sm.ap[:, c:c + 1]
            s.op('dve', lambda e, ss=ss: e.memset(ss, 0.0), reads=[], writes=[("sm", c)])
            self.act(junk.ap, xb.ap, AF.Square, reads=[xb], writes=[junk, ("sm", c)], accum_out=ss)
            rs = sm.ap[:, 8 + c:9 + c]
            self.rsqrt(rs, ss, 1.0 / D, EPS, reads=[("sm", c)], writes=[("sm", 8 + c)])
            self.ts('dve', xsb.ap, xb.ap, rs, None, ALU.mult, None, reads=[xb, ("sm", 8 + c)], writes=[xsb])
            for half in range(2):
                ps = self.psum()
                for k4 in range(4):
                    kc = half * 4 + k4
                    self.mm(ps.ap[:, k4 * 128:(k4 + 1) * 128], xsb.ap[:, kc * 128:(kc + 1) * 128], self.ident,
                            k4 == 0, k4 == 3, reads=[xsb, self.cbf], writes=[ps])
                self.tt('dve', self.hT.ap[:, half * 4:half * 4 + 4, c * 128:(c + 1) * 128],
                        ps.ap.rearrange("p (k n) -> p k n", k=4),
                        self.gT[:, half * 4:half * 4 + 4].unsqueeze(2).to_broadcast([128, 4, 128]), ALU.mult,
                        reads=[ps, self.pf32], writes=[self.hT])
            self.scr_off -= 0
        self.scr_reset()
        self.branch_A(l, t)
        self.scr_reset()
        self.lift(l, t, 0)
        self.scr_reset()
        self.branch_C(l, t)
        self.scr_reset()
        self.lift(l, t, 2)
        self.scr_reset()
        self.branch_D(l, t)
        self.scr_reset()
        self.lift(l, t, 3)
        self.scr_reset()
        self.branch_B(l, t)
        self.scr_reset()
        self.lift(l, t, 1)
        self.scr_reset()
        self.outproj(l, t, src, srct, dst, dstt, do_final)

    def dbg_dump(self, idx, t):
        if not self.dbg:
            return
        tmp = self.scr([128, 4, 512], F32)
        self.s.op('dve', lambda e: e.tensor_copy(tmp.ap, self.yT.ap), reads=[self.yT], writes=[tmp])
        for wc in range(4):
            dstap = self.dbg_d[idx, t * 512:(t + 1) * 512, wc * 128:(wc + 1) * 128].rearrange("t w -> w t")
            self.s.op('sp', lambda e, dstap=dstap, wc=wc: e.dma_start(out=dstap, in_=tmp.ap[:, wc, :],
                                                                         allow_slow_non_contiguous=True),
                      reads=[tmp], writes=["dbg"], dma=True)

    def branch_A(self, l, t):
        s = self.s
        vn = self.scr([128, 4, W], BF16)
        t1 = self.scr([128, 4, 512], F32)
        st = self.scr([128, 4, 8], F32)
        mv = self.scr([128, 4, 2], F32)
        wb = self.wnext(('in', l, OFF['a_v'], 512))
        for c in range(4):
            ps = self.proj_tm(wb, c, 0, 512, 512)
            s.op('dve', lambda e, c=c, ps=ps: e.bn_stats(st.ap[:, c, 0:6], ps.ap), reads=[ps], writes=[("Ast", c)])
            s.op('dve', lambda e, c=c: e.bn_aggr(mv.ap[:, c, :], st.ap[:, c, 0:6]), reads=[("Ast", c)],
                 writes=[("Amv", c)])
            rstd = st.ap[:, c, 6:7]
            self.rsqrt(rstd, mv.ap[:, c, 1:2], 1.0, EPS, reads=[("Amv", c)], writes=[("Ars", c)])
            self.ts('dve', vn.ap[:, c, :], ps.ap, mv.ap[:, c, 0:1], rstd, ALU.subtract, ALU.mult,
                    reads=[ps, ("Amv", c), ("Ars", c)], writes=[vn])
        for g in range(4):
            ps = self.psum()
            for c in range(4):
                self.mm(ps.ap[:, c * 128:(c + 1) * 128], vn.ap[:, c, g * 128:(g + 1) * 128], self.wTm.ap[:, g, :],
                        c == 0, c == 3, reads=[vn, self.wTm], writes=[ps])
            self.stt('dve', t1.ap[:, g, :].rearrange("p (c n) -> p c n", c=4),
                     ps.ap.rearrange("p (c n) -> p c n", c=4), self.GT[:, g:g + 1],
                     self.biasm.ap[:, g, :].unsqueeze(1).to_broadcast([128, 4, 128]), ALU.mult, ALU.add,
                     reads=[ps, self.pf32, self.biasm], writes=[t1])
        wb = self.wnext(('in', l, OFF['a_u'], 512))
        for g in range(4):
            ps = self.proj_fm_sub(wb, g)
            self.tt('dve', t1.ap[:, g, :], ps.ap, t1.ap[:, g, :], ALU.mult, reads=[ps, t1], writes=[t1])
        wb = self.wnext(('in', l, OFF['a_z'], 512))
        szs = [self.scr([128, 512], F32) for _ in range(2)]
        for g in range(4):
            ps = self.proj_fm_sub(wb, g)
            sz = szs[g % 2]
            self.act(sz.ap, ps.ap, AF.Silu, reads=[ps], writes=[sz])
            self.tt('dve', self.yT.ap[:, g, :], t1.ap[:, g, :], sz.ap, ALU.mult, reads=[t1, sz], writes=[self.yT])
        self.dbg_dump(0, t)

    def lift(self, l, t, b):
        first = (b == 0)
        sgs = [self.scr([128, 512], F32) for _ in range(2)]
        tmps = [self.scr([128, 512], F32) for _ in range(2)]
        for half in range(2):
            wg = self.wnext(('in', l, OFF['gates'] + b * D + half * 512, 512))
            wbr = self.wnext(('br', l, b, half))
            wbv = wbr.ap[:, 0:2048].rearrange("p (wc n) -> p wc n", wc=4)
            for sub in range(4):
                ncn = half * 4 + sub
                psg = self.proj_fm_sub(wg, sub)
                sg = sgs[ncn % 2]
                self.act(sg.ap, psg.ap, AF.Sigmoid, reads=[psg], writes=[sg])
                psb = self.psum()
                for wc in range(4):
                    self.mm(psb.ap, wbv[:, wc, sub * 128:(sub + 1) * 128], self.yT.ap[:, wc, :], wc == 0, wc == 3,
                            reads=[wbr, self.yT], writes=[psb])
                if first:
                    self.tt('dve', self.mg.ap[:, ncn, :], psb.ap, sg.ap, ALU.mult, reads=[psb, sg],
                            writes=[("mg", ncn)])
                else:
                    tmp = tmps[ncn % 2]
                    self.tt('dve', tmp.ap, psb.ap, sg.ap, ALU.mult, reads=[psb, sg], writes=[tmp])
                    self.tt('pool', self.mg.ap[:, ncn, :], self.mg.ap[:, ncn, :], tmp.ap, ALU.add,
                            reads=[tmp, ("mg", ncn)], writes=[("mg", ncn)])

    def branch_C(self, l, t):
        s = self.s
        yc = self.scr([128, 4, 514], F32)
        cv = self.scr([128, 4, 512], F32)
        ccs = [self.scr([128, 512], F32) for _ in range(2)]
        wbc = self.wnext(('in', l, OFF['c_c'], 512))
        wbx = self.wnext(('in', l, OFF['c_x'], 512))
        s.op('dve', lambda e: e.tensor_copy(yc.ap[:, :, 0:2], self.halo.ap), reads=[self.halo], writes=[yc])
        for g in range(4):
            ps = self.proj_fm_sub(wbc, g)
            cc = ccs[g % 2]
            self.act(cc.ap, ps.ap, AF.Copy, reads=[ps], writes=[cc])
            ps2 = self.proj_fm_sub(wbx, g)
            self.tt('dve', yc.ap[:, g, 2:514], ps2.ap, cc.ap, ALU.mult, reads=[ps2, cc], writes=[yc])
            self.ts('dve', cv.ap[:, g, :], yc.ap[:, g, 0:512], self.cw[:, g, 0:1], None, ALU.mult, None,
                    reads=[yc, self.pf32], writes=[cv])
            self.stt('dve', cv.ap[:, g, :], yc.ap[:, g, 1:513], self.cw[:, g, 1:2], cv.ap[:, g, :], ALU.mult, ALU.add,
                     reads=[yc, self.pf32, cv], writes=[cv])
            self.stt('dve', cv.ap[:, g, :], yc.ap[:, g, 2:514], self.cw[:, g, 2:3], cv.ap[:, g, :], ALU.mult, ALU.add,
                     reads=[yc, self.pf32, cv], writes=[cv])
        s.op('dve', lambda e: e.tensor_copy(self.halo.ap, yc.ap[:, :, 512:514]), reads=[yc], writes=[self.halo])
        wbb = self.wnext(('in', l, OFF['c_b'], 512))
        for g in range(4):
            ps = self.proj_fm_sub(wbb, g)
            self.tt('dve', cv.ap[:, g, :], ps.ap, cv.ap[:, g, :], ALU.mult, reads=[ps, cv], writes=[cv])
        wbz = self.wnext(('in', l, OFF['c_z'], 512))
        for g in range(4):
            ps = self.proj_fm_sub(wbz, g)
            sz = ccs[g % 2]
            self.act(sz.ap, ps.ap, AF.Silu, reads=[ps], writes=[sz])
            self.tt('dve', self.yT.ap[:, g, :], cv.ap[:, g, :], sz.ap, ALU.mult, reads=[cv, sz], writes=[self.yT])
        self.dbg_dump(2, t)

    def branch_D(self, l, t):
        s = self.s
        NCH = self.NCH
        qTf = self.scr([128, 4, 512], BF16)
        szD = self.scr([128, 4, 512], BF16)
        biasall = self.scr([128, 4, NCH * 4], F32)
        pTs = [self.scr([128, 512], BF16) for _ in range(3)]
        rden = self.scr([128, 512], F32)
        tmpo = self.scr([128, 512], F32)
        lsb = self.scr([128, 4, 4], F32)
        crefs = self.scr([128, 4, 4], F32)
        wb = self.wnext(('in', l, OFF['d_q'], 512))
        for h in range(4):
            ps = self.proj_fm_sub(wb, h)
            self.act(qTf.ap[:, h, :], ps.ap, AF.Copy, reads=[ps], writes=[qTf], scale=float(128 ** -0.5))
        wb = self.wnext(('in', l, OFF['d_k'], 512))
        for h in range(4):
            ps = self.proj_fm_sub(wb, h)
            s.op('dve', lambda e, h=h, ps=ps: e.tensor_copy(self.kT.ap[:, h, t * 512:(t + 1) * 512], ps.ap),
                 reads=[ps], writes=[("kT", t)])
        wb = self.wnext(('in', l, OFF['d_vf'], 516))
        for c in range(4):
            gc = 4 * t + c
            ps = self.proj_tm(wb, c, 0, 512, 516)
            self.act(self.vC.ap[:, gc, :], ps.ap, AF.Copy, reads=[ps], writes=[("vC", gc)])
            psf = self.proj_tm(wb, c, 512, 4, 516)
            xf = lsb.ap[:, c, :]
            self.tt('dve', xf, psf.ap[:, 0:4], self.bfbc, ALU.add, reads=[psf, self.pf32], writes=[("lsb", c)])
            self.act(xf, xf, AF.Exp, reads=[("lsb", c)], writes=[("lsb", c)], scale=-1.0)
            self.act(xf, xf, AF.Ln, reads=[("lsb", c), self.cf32], writes=[("lsb", c)], bias=self.onec)
            psc = self.psum()
            self.mm(psc.ap[:, 0:4], self.Uf, xf, True, gc == 0, reads=[self.cf32, ("lsb", c)], writes=[psc])
            if gc > 0:
                self.mm(psc.ap[:, 0:4], self.E127, self.cumneg.ap[:, gc - 1, :], False, True,
                        reads=[self.cf32, ("cum", gc - 1)], writes=[psc])
            s.op('dve', lambda e, gc=gc, psc=psc: e.tensor_copy(self.cumneg.ap[:, gc, :], psc.ap[:, 0:4]),
                 reads=[psc], writes=[("cum", gc)])
            if c % 2 == 0:
                psr = self.psum()
                self.mm(psr.ap[:, 0:4], self.E127, self.cumneg.ap[:, gc, :], True, True,
                        reads=[self.cf32, ("cum", gc)], writes=[psr])
                s.op('dve', lambda e, c=c, psr=psr: e.tensor_copy(crefs.ap[:, c // 2, :], psr.ap[:, 0:4]),
                     reads=[psr], writes=[("cref", c // 2)])
            else:
                nk = gc + 1
                m = c // 2
                self.tt('dve', biasall.ap[:, m, 0:nk * 4].rearrange("p (i h) -> p i h", h=4),
                        self.cumneg.ap[:, 0:nk, :], crefs.ap[:, m, :].unsqueeze(1).to_broadcast([128, nk, 4]),
                        ALU.subtract, reads=[("cum", i) for i in range(nk)] + [("cref", m)], writes=[("bias", m)])
        wb = self.wnext(('in', l, OFF['d_z'], 512))
        for h in range(4):
            ps = self.proj_fm_sub(wb, h)
            self.act(szD.ap[:, h, :], ps.ap, AF.Silu, reads=[ps], writes=[szD])
        nkc = 4 * t + 4
        items = [(h, i) for h in range(4) for i in range(nkc)]
        pss_of = {}
        LA = 2

        def S1(k):
            h, i = items[k]
            c0 = max(i - 4 * t, 0) * 128
            pss = self.psum()
            pss_of[k] = pss
            self.mm(pss.ap[:, c0:512], self.kT.ap[:, h, i * 128:(i + 1) * 128], qTf.ap[:, h, c0:512], True, True,
                    reads=[("kT", i // 4), qTf], writes=[pss])

        def S23(k):
            h, i = items[k]
            ci = i - 4 * t
            j0 = max(ci, 0)
            c0 = j0 * 128
            acc_o = self.ps[4 + 2 * (h % 2)]
            acc_d = self.ps[5 + 2 * (h % 2)]
            pss = pss_of.pop(k)
            pT = pTs[k % 3]
            for m in range(j0 // 2, 2):
                a0 = max(c0, m * 256)
                self.act(pT.ap[:, a0:(m + 1) * 256], pss.ap[:, a0:(m + 1) * 256], AF.Exp,
                         reads=[pss, ("bias", m)], writes=[pT],
                         bias=biasall.ap[:, m, i * 4 + h:i * 4 + h + 1])
            if ci >= 0:
                self.tt('pool', pT.ap[:, c0:c0 + 128], pT.ap[:, c0:c0 + 128], self.tri01, ALU.mult,
                        reads=[pT, self.cbf], writes=[pT])
            self.mm(acc_o.ap[:, c0:512], self.vC.ap[:, i, h * 128:(h + 1) * 128], pT.ap[:, c0:512],
                    i == 0, i == nkc - 1, reads=[("vC", i), pT], writes=[acc_o])
            self.mm(acc_d.ap[:, c0:512], self.ones, pT.ap[:, c0:512], i == 0, i == nkc - 1,
                    reads=[self.cbf, pT], writes=[acc_d])
            if i == nkc - 1:
                s.op('dve', lambda e, acc_d=acc_d: e.reciprocal(rden.ap, acc_d.ap), reads=[acc_d], writes=[rden])
                self.tt('dve', tmpo.ap, acc_o.ap, rden.ap, ALU.mult, reads=[acc_o, rden], writes=[tmpo])
                self.tt('dve', self.yT.ap[:, h, :], tmpo.ap, szD.ap[:, h, :], ALU.mult, reads=[tmpo, szD],
                        writes=[self.yT])

        for k in range(min(LA, len(items))):
            S1(k)
        for k in range(len(items)):
            if k + LA < len(items):
                S1(k + LA)
            S23(k)
        self.dbg_dump(3, t)

    def branch_B(self, l, t):
        s = self.s
        S = self.S
        sm = self.small
        K1 = 1024
        qlatT = self.scr([128, 4, 512], BF16)
        szB = self.scr([128, 4, 512], BF16)
        qiT = self.scr([128, 2, 512], BF16)
        scores = [self.scr([128, S], BF16, align=K1) for _ in range(2)]
        junk = self.scr([128, S], U8)
        rhflat = self.scr([128, 2048], BF16, align=K1)
        rh = [Buf(rhflat.ap[:, h * 512:(h + 1) * 512], [rhflat.toks[h]]) for h in range(4)]
        st8 = Buf(rhflat.ap.bitcast(U8), rhflat.toks)
        pmall = self.scr([128, 4, 512], BF16, align=K1)
        pm = [Buf(pmall.ap[:, h, :], [pmall.toks[h]]) for h in range(4)]
        qTb = pmall
        pmT = [self.scr([128, 512], BF16, align=K1) for _ in range(2)]
        Dsg = self.scr([128, 4, 128], BF16, align=K1)
        Dg4 = self.scr([128, 4, 128], F32, align=K1)
        rdbc = self.scr([128, 512], F32, align=K1)
        olat = self.scr([128, 512], BF16, align=K1)
        ctmp = self.scr([128, 128], F32, align=K1)
        kid = self.scr([128, 128], BF16)
        qib = self.scr([128, 256], BF16)
        denp = self.scr([128, 4, 8], F32, align=K1)
        den = self.scr([128, 4], F32)
        def prepQ():
            wb = self.wnext(('in', l, OFF['b_q'], 512))
            for h in range(4):
                ps = self.proj_fm_sub(wb, h)
                self.act(qTb.ap[:, h, :], ps.ap, AF.Copy, reads=[ps], writes=[qTb])
                yield
            for h in range(4):
                ps = self.psum()
                self.mm(ps.ap, self.wuk[:, h, :], qTb.ap[:, h, :], True, True, reads=[self.pbf, qTb], writes=[ps])
                self.act(qlatT.ap[:, h, :], ps.ap, AF.Copy, reads=[ps], writes=[qlatT], scale=float(128 ** -0.5))
                yield

        def prepZ():
            wb = self.wnext(('in', l, OFF['b_z'], 512))
            for h in range(4):
                ps = self.proj_fm_sub(wb, h)
                self.act(szB.ap[:, h, :], ps.ap, AF.Silu, reads=[ps], writes=[szB])
                yield

        wb = self.wnext(('in', l, OFF['b_misc'], 452))
        for c in range(4):
            gc = 4 * t + c
            ps = self.proj_tm(wb, c, 0, 452, 452)
            ss = sm.ap[:, 16 + c:17 + c]
            s.op('dve', lambda e, ss=ss: e.memset(ss, 0.0), reads=[], writes=[("sm", 16 + c)])
            self.act(ctmp.ap, ps.ap[:, 0:128], AF.Square, reads=[ps], writes=[ctmp, ("sm", 16 + c)], accum_out=ss)
            rs = sm.ap[:, 24 + c:25 + c]
            self.rsqrt(rs, ss, 1.0 / 128, EPS, reads=[("sm", 16 + c)], writes=[("sm", 24 + c)])
            self.stt('dve', self.cTok.ap[:, gc, :], ps.ap[:, 0:128], rs, self.kvg, ALU.mult, ALU.mult,
                     reads=[ps, ("sm", 24 + c), self.pf32], writes=[("cTok", gc)])
            pst = self.psum()
            self.mm(pst.ap[:, 0:128], self.cTok.ap[:, gc, :], self.ident, True, True,
                    reads=[("cTok", gc), self.cbf], writes=[pst])
            self.act(self.cT.ap[:, gc * 128:(gc + 1) * 128], pst.ap[:, 0:128], AF.Copy, reads=[pst],
                     writes=[("cT", gc)])
            s.op('dve', lambda e, ps=ps: e.tensor_copy(kid.ap.rearrange("p (a b) -> p a b", a=2),
                                                       ps.ap[:, 384:448].unsqueeze(1).to_broadcast([128, 2, 64])),
                 reads=[ps], writes=[kid])
            pst = self.psum()
            self.mm(pst.ap[:, 0:128], kid.ap, self.ident, True, True, reads=[kid, self.cbf], writes=[pst])
            self.act(self.kiT.ap[:, gc * 128:(gc + 1) * 128], pst.ap[:, 0:128], AF.Copy, reads=[pst],
                     writes=[("kiT", gc)])
            s.op('dve', lambda e, ps=ps: e.tensor_copy(qib.ap, ps.ap[:, 128:384]), reads=[ps], writes=[qib])
            pst = self.psum()
            for pr in range(2):
                self.mm(pst.ap[:, pr * 128:(pr + 1) * 128], qib.ap[:, pr * 128:(pr + 1) * 128], self.ident,
                        pr == 0, pr == 1, reads=[qib, self.cbf], writes=[pst])
            self.act(qiT.ap[:, :, c * 128:(c + 1) * 128], pst.ap[:, 0:256].rearrange("p (a b) -> p a b", a=2),
                     AF.Copy, reads=[pst], writes=[qiT])
            aw = self.wstat.ap[:, c, 0:4]
            sg = self.wstat.ap[:, c, 4:8]
            self.act(aw, ps.ap[:, 448:452], AF.Abs, reads=[ps], writes=[("aw", c)], scale=0.5 / 8.0)
            self.act(sg, ps.ap[:, 448:452], AF.Sign, reads=[ps], writes=[("sg", c)])
        acc_o = self.ps[4]
        acc_r = self.ps[5]
        pso = self.ps[5]
        self.ps_ring = [0, 1, 2, 3, 6, 7]
        bis = self.bis.ap
        th = bis[:, 0:1]

        def SC(c):
            gc = 4 * t + c
            nk = gc + 1
            nkt = (nk + 3) // 4
            score = scores[c % 2]
            aw = self.wstat.ap[:, c, 0:4]
            sg = self.wstat.ap[:, c, 4:8]
            self.tt('dve', Dsg.ap, self.ident.unsqueeze(1).to_broadcast([128, 4, 128]),
                    sg.unsqueeze(2).to_broadcast([128, 4, 128]), ALU.mult, reads=[self.cbf, ("sg", c)], writes=[Dsg])
            for kt in range(nkt):
                wdt = min(512, nk * 128 - kt * 512)
                k0 = kt * 512
                for h in range(4):
                    ps = self.psum()
                    r0 = (h % 2) * 64
                    self.mm(ps.ap[:, 0:wdt], qiT.ap[r0:r0 + 64, h // 2, c * 128:(c + 1) * 128],
                            self.kiT.ap[r0:r0 + 64, k0:k0 + wdt], True, True,
                            reads=[qiT] + [("kiT", k0 // 128 + i) for i in range(wdt // 128)], writes=[ps])
                    self.act(rh[h].ap[:, 0:wdt], ps.ap[:, 0:wdt], AF.Relu, reads=[ps, ("aw", c)], writes=[rh[h]],
                             scale=aw[:, h:h + 1])
                pssc = self.psum()
                diag = (gc // 4 == kt)
                for h in range(4):
                    self.mm(pssc.ap[:, 0:wdt], Dsg.ap[:, h, :], rh[h].ap[:, 0:wdt], h == 0, (h == 3 and not diag),
                            reads=[Dsg, rh[h]], writes=[pssc])
                if diag:
                    jd = gc % 4
                    self.mm(pssc.ap[:, jd * 128:(jd + 1) * 128], self.ident, self.maskneg, False, True,
                            reads=[self.cbf], writes=[pssc])
                s.op('dve', lambda e, score=score, pssc=pssc, k0=k0, wdt=wdt:
                     e.tensor_copy(score.ap[:, k0:k0 + wdt], pssc.ap[:, 0:wdt]), reads=[pssc], writes=[score])
                yield

        def BI(c):
            gc = 4 * t + c
            nk = gc + 1
            n = nk * 128
            score = scores[c % 2]
            if gc < 2:
                s.op('dve', lambda e: e.memset(th, -1.0e29), reads=[], writes=[self.bis])
            else:
                s.op('dve', lambda e: e.memset(bis[:, 0:32], 0.0), reads=[], writes=[self.bis])
                s.op('dve', lambda e: e.memset(th, 1.0e-30), reads=[], writes=[self.bis])
                delta = 4.0
                for it in range(NITER):
                    cnt = bis[:, 2 + it:3 + it]
                    self.ts('dve', junk.ap[:, 0:n], score.ap[:, 0:n], th, 0.0, ALU.is_ge, ALU.add,
                            reads=[score, self.bis], writes=[self.bis], accum_out=cnt)
                    sel = bis[:, 1:2]
                    self.ts('dve', sel, cnt, TOPK - 0.5, 2.0 * delta, ALU.is_ge, ALU.mult, reads=[self.bis],
                            writes=[self.bis])
                    self.stt('dve', th, th, -delta, sel, ALU.add, ALU.add, reads=[self.bis], writes=[self.bis])
                    delta *= 0.5
                    yield
                self.ts('dve', th, th, -2.0 * delta, None, ALU.add, None, reads=[self.bis], writes=[self.bis])
                cpos = bis[:, 2:3]
                nz = bis[:, 16:17]
                t1 = bis[:, 17:18]
                t2 = bis[:, 18:19]
                tie = bis[:, 19:20]
                aa = bis[:, 20:21]
                ini = bis[:, 21:22]
                dd = bis[:, 22:23]
                B_ = [self.bis]
                J_ = [("junk",)]
                self.ts('dve', junk.ap[:, 0:n], score.ap[:, 0:n], 0.0, 0.0, ALU.is_equal, ALU.add,
                        reads=[score], writes=J_ + B_, accum_out=nz)
                self.ts('dve', t1, cpos, TOPK - 0.5, None, ALU.is_le, None, reads=B_, writes=B_)
                self.tt('dve', t2, cpos, nz, ALU.add, reads=B_, writes=B_)
                self.stt('dve', tie, t2, TOPK - 0.5, t1, ALU.is_ge, ALU.mult, reads=B_, writes=B_)
                self.ts('dve', aa, cpos, 2.0, -(2.0 * TOPK + 2.0), ALU.mult, ALU.add, reads=B_, writes=B_)
                self.ts('dve', ini, aa, tie, 4.0, ALU.mult, ALU.add, reads=B_, writes=B_)
                self.ts('dve', dd, tie, -1.0, 1.0, ALU.mult, ALU.add, reads=B_, writes=B_)
                self.tt('dve', th, th, dd, ALU.mult, reads=B_, writes=B_)
                self.stt('dve', th, tie, 1.0e-30, th, ALU.mult, ALU.add, reads=B_, writes=B_)
                s.op('dve', lambda e, n=n: e.tensor_tensor_scan(st8.ap[:, 0:n], junk.ap[:, 0:n], junk.ap[:, 0:n], ini,
                                                                ALU.add, ALU.add), reads=J_ + B_, writes=[st8])
                self.stt('dve', junk.ap[:, 0:n], st8.ap[:, 0:n], 2.5, junk.ap[:, 0:n], ALU.is_le, ALU.mult,
                         reads=[st8] + J_, writes=J_)
                self.stt('dve', score.ap[:, 0:n], junk.ap[:, 0:n], 1.0, score.ap[:, 0:n], ALU.mult, ALU.add,
                         reads=J_ + [score], writes=[score])
                yield
            if self.dbg:
                self.dma(self.dbg_d[5, gc * 128:(gc + 1) * 128, 0:32], self.bis.ap, [self.bis], ["dbg"])
                self.dma(self.dbg_d[6, gc * 128:(gc + 1) * 128, 0:nk * 128], score.ap[:, 0:nk * 128], [score],
                         ["dbg"], q='pool')
            self.ts('dve', score.ap[:, 0:n], score.ap[:, 0:n], th, -30000.0, ALU.is_lt, ALU.mult,
                    reads=[score, self.bis], writes=[score])
            yield

        def AT(c):
            gc = 4 * t + c
            nk = gc + 1
            nkt = (nk + 3) // 4
            score = scores[c % 2]
            s.op('dve', lambda e: e.memset(denp.ap, 0.0), reads=[], writes=[denp])
            for kt in range(nkt):
                wdt = min(512, nk * 128 - kt * 512)
                k0 = kt * 512
                for h in range(4):
                    ps = self.psum()
                    self.mm(ps.ap[:, 0:wdt], qlatT.ap[:, h, c * 128:(c + 1) * 128], self.cT.ap[:, k0:k0 + wdt],
                            True, False, reads=[qlatT] + [("cT", k0 // 128 + i) for i in range(wdt // 128)],
                            writes=[ps])
                    self.mm(ps.ap[:, 0:wdt], self.ident, score.ap[:, k0:k0 + wdt], False, True,
                            reads=[self.cbf, score], writes=[ps])
                    self.act(pm[h].ap[:, 0:wdt], ps.ap[:, 0:wdt], AF.Exp, reads=[ps], writes=[pm[h], denp],
                             accum_out=denp.ap[:, h, kt:kt + 1])
                yield
                njj = wdt // 128
                psts = {}

                def T(jj):
                    pst = self.psum()
                    psts[jj] = pst
                    for h in range(4):
                        self.mm(pst.ap[:, h * 128:(h + 1) * 128], pm[h].ap[:, jj * 128:(jj + 1) * 128], self.ident,
                                h == 0, h == 3, reads=[pm[h], self.cbf], writes=[pst])
                T(0)
                for jj in range(njj):
                    if jj + 1 < njj:
                        T(jj + 1)
                    kc = k0 // 128 + jj
                    pst = psts.pop(jj)
                    pT = pmT[kc % 2]
                    self.act(pT.ap, pst.ap, AF.Copy, reads=[pst], writes=[pT])
                    self.mm(acc_o.ap, self.cTok.ap[:, kc, :], pT.ap, kc == 0, kc == nk - 1,
                            reads=[("cTok", kc), pT], writes=[acc_o])
                    yield
            s.op('dve', lambda e: e.tensor_reduce(den.ap, denp.ap, mybir.AxisListType.X, ALU.add),
                 reads=[denp], writes=[den])
            s.op('dve', lambda e: e.reciprocal(den.ap, den.ap), reads=[den], writes=[den])
            self.tt('dve', Dg4.ap, self.identf.unsqueeze(1).to_broadcast([128, 4, 128]),
                    den.ap.unsqueeze(2).to_broadcast([128, 4, 128]), ALU.mult, reads=[self.cf32, den], writes=[Dg4])
            self.mm(acc_r.ap, self.onesf, Dg4.ap.rearrange("p h n -> p (h n)"), True, True,
                    reads=[self.cf32, Dg4], writes=[acc_r])
            self.act(rdbc.ap, acc_r.ap, AF.Copy, reads=[acc_r], writes=[rdbc])
            self.tt('dve', olat.ap, acc_o.ap, rdbc.ap, ALU.mult, reads=[acc_o, rdbc], writes=[olat])
            yield
            for h in range(4):
                self.mm(pso.ap[:, h * 128:(h + 1) * 128], self.wuv[:, h, :], olat.ap[:, h * 128:(h + 1) * 128],
                        h == 0, h == 3, reads=[self.pbf, olat], writes=[pso])
            self.tt('dve', self.yT.ap[:, :, c * 128:(c + 1) * 128], pso.ap.rearrange("p (h n) -> p h n", h=4),
                    szB.ap[:, :, c * 128:(c + 1) * 128], ALU.mult, reads=[pso, szB], writes=[self.yT])
            yield

        def chain(*gens):
            for g in gens:
                for _ in g:
                    yield

        self.interleave(SC(0))
        self.interleave(BI(0), chain(prepQ(), prepZ(), SC(1)))
        for c in range(4):
            first = chain(AT(c), SC(c + 2)) if c + 2 < 4 else AT(c)
            if c + 1 < 4:
                self.interleave(first, BI(c + 1))
            else:
                self.interleave(first)
        self.ps_ring = [0, 1, 2, 3]
        self.dbg_dump(1, t)

    def outproj(self, l, t, src, srct, dst, dstt, do_final):
        s = self.s
        sm = self.small
        mgb = Buf(self.mg.ap, [("mg", i) for i in range(8)])
        if self.dbg and l == 0:
            tmp = self.scr([128, 8, 512], F32)
            self.s.op('dve', lambda e: e.tensor_copy(tmp.ap, self.mg.ap), reads=[("mg", i) for i in range(8)],
                      writes=[tmp])
            for ncn in range(8):
                dstap = self.dbg_d[4, t * 512:(t + 1) * 512, ncn * 128:(ncn + 1) * 128].rearrange("t w -> w t")
                self.s.op('sp', lambda e, dstap=dstap, ncn=ncn: e.dma_start(out=dstap, in_=tmp.ap[:, ncn, :],
                                                                               allow_slow_non_contiguous=True),
                          reads=[tmp], writes=["dbg"], dma=True)
        wos = [self.wnext(('out', l, half * 512, 512)) for half in range(2)]
        if do_final:
            fgb = self.scr([128, D], F32)
            self.dma(fgb.ap, self.fg_d, [], [fgb])
            junk = self.scr([128, D], BF16)
        xc2 = [self.xc[0], self.scr([128, D], F32)]
        for c in range(4):
            gc = 4 * t + c
            r0 = gc * 128
            xb = xc2[c % 2]
            self.dma(xb.ap, src[r0:r0 + 128, :], [(srct, t, c)], [xb])
            for half in range(2):
                wv = wos[half].ap[:, 0:4096].rearrange("p (kc n) -> p kc n", kc=8)
                ps = self.psum()
                for ncn in range(8):
                    self.mm(ps.ap, mgb.ap[:, ncn, c * 128:(c + 1) * 128], wv[:, ncn, :], ncn == 0, ncn == 7,
                            reads=[mgb, wos[half]], writes=[ps])
                self.tt('dve', xb.ap[:, half * 512:(half + 1) * 512], ps.ap, xb.ap[:, half * 512:(half + 1) * 512],
                        ALU.add, reads=[ps, xb], writes=[xb])
            if do_final:
                ss = sm.ap[:, 32 + c:33 + c]
                s.op('dve', lambda e, ss=ss: e.memset(ss, 0.0), reads=[], writes=[("sm", 32 + c)])
                self.act(junk.ap, xb.ap, AF.Square, reads=[xb], writes=[junk, ("sm", 32 + c)], accum_out=ss)
                rs = sm.ap[:, 40 + c:41 + c]
                self.rsqrt(rs, ss, 1.0 / D, EPS, reads=[("sm", 32 + c)], writes=[("sm", 40 + c)])
                self.stt('dve', xb.ap, xb.ap, rs, fgb.ap, ALU.mult, ALU.mult, reads=[xb, ("sm", 40 + c), fgb],
                         writes=[xb])
            self.dma(dst[r0:r0 + 128, :], xb.ap, [xb], [(dstt, t, c)])


def _consts():
    i = np.arange(128)
    ident = np.eye(128, dtype=np.float32)
    tri = (i[:, None] <= i[None, :]).astype(np.float32)
    e64 = np.zeros((128, 128), np.float32)
    e64[64, :] = 1.0
    e127 = np.zeros((128, 128), np.float32)
    e127[127, :] = 1.0
    ones = np.ones((128, 128), np.float32)
    maskneg = np.where(i[None, :] <= i[:, None], 0.0, NEG).astype(np.float32)
    extra = np.zeros((128, 8), np.float32)
    extra[:, 0] = EPS
    extra[:, 1] = 1.0
    cf32 = np.concatenate([ident, tri, e64, e127, ones, extra], axis=1)
    cbf = np.concatenate([ident, tri, maskneg, ones], axis=1)
    return np.ascontiguousarray(cf32), np.ascontiguousarray(cbf)


def _pack_params(NL, norm_g, gm_ln_g, gm_ln_b, gm_w_s, gm_b_s, dsa_kv_g, dsa_w_uk, dsa_w_uv, conv_w, fox_b_f):
    pf = np.zeros((NL, 128, NP32), np.float32)
    pb = np.zeros((NL, 128, NPBF), np.float32)
    for l in range(NL):
        pf[l, :, 0:8] = norm_g[l].reshape(8, 128).T
        pf[l, :, 8:12] = gm_ln_g[l].reshape(4, 128).T
        pf[l, :, 12:16] = gm_ln_b[l].reshape(4, 128).T
        pf[l, :, 16:528] = np.broadcast_to(gm_b_s[l].reshape(1, 512), (128, 512))
        pf[l, :, 528:656] = np.broadcast_to(dsa_kv_g[l].reshape(1, 128), (128, 128))
        pf[l, :, 656:668] = conv_w[l].reshape(3, 4, 128).transpose(2, 1, 0).reshape(128, 12)
        pf[l, :, 668:672] = np.broadcast_to(fox_b_f[l].reshape(1, 4), (128, 4))
        pb[l, :, 0:512] = gm_w_s[l].transpose(2, 0, 1).reshape(128, 512)
        pb[l, :, 512:1024] = dsa_w_uk[l].transpose(2, 0, 1).reshape(128, 512)
        pb[l, :, 1024:1536] = dsa_w_uv[l].transpose(1, 0, 2).reshape(128, 512)
    return pf, pb


_CACHE = {}


def _get_program(S, NL, final_norm, dbg):
    key = (S, NL, final_norm, dbg)
    if key not in _CACHE:
        _CACHE[key] = Builder(S, NL, final_norm, dbg).build()
    return _CACHE[key]


def run_layers(x, norm_g, w_in, gm_ln_g, gm_ln_b, gm_w_s, gm_b_s, dsa_kv_g, dsa_w_uk, dsa_w_uv,
               conv_w, fox_b_f, w_branch, w_out, final_g, final_norm=True, dbg=False):
    x = np.asarray(x, np.float32)
    B, S, _ = x.shape
    NL = int(np.asarray(w_in).shape[0])
    f = lambda a: np.ascontiguousarray(np.asarray(a, np.float32))
    nc = _get_program(S, NL, final_norm, dbg)
    cf32, cbf = _consts()
    pf, pb = _pack_params(NL, f(norm_g), f(gm_ln_g), f(gm_ln_b), f(gm_w_s), f(gm_b_s), f(dsa_kv_g), f(dsa_w_uk),
                          f(dsa_w_uv), f(conv_w), f(fox_b_f))
    fg = np.ascontiguousarray(np.broadcast_to(f(final_g).reshape(1, D), (128, D)))
    shared = dict(w_in=f(w_in), w_br=f(w_branch), w_out=f(w_out), pf32=pf, pbf=pb, cf32=cf32, cbf=cbf, fg=fg)
    in_maps = []
    for b in range(B):
        m = dict(shared)
        m["x"] = np.ascontiguousarray(x[b])
        in_maps.append(m)
    res = run_bass_kernel_spmd(nc, in_maps, core_ids=list(range(B)))
    out = np.stack([np.asarray(r["out"], np.float32) for r in res.results], axis=0)
    if dbg:
        return out, np.stack([np.asarray(r["dbg"], np.float32) for r in res.results], axis=0)
    return out


def kernel(x, norm_g, w_in, gm_ln_g, gm_ln_b, gm_w_s, gm_b_s, dsa_kv_g, dsa_w_uk, dsa_w_uv,
           conv_w, fox_b_f, w_branch, w_out, final_g):
    return run_layers(x, norm_g, w_in, gm_ln_g, gm_ln_b, gm_w_s, gm_b_s, dsa_kv_g, dsa_w_uk, dsa_w_uv,
                      conv_w, fox_b_f, w_branch, w_out, final_g)
```

```python
import numpy as np
import concourse.bass as bass
import concourse.mybir as mybir
from concourse.bass_utils import run_bass_kernel_spmd

F32 = mybir.dt.float32
BF16 = mybir.dt.bfloat16
U8 = mybir.dt.uint8
ALU = mybir.AluOpType
AF = mybir.ActivationFunctionType

D = 1024
W = 512
INW = 11208
OFF = dict(a_u=0, a_v=512, a_z=1024, b_q=1536, b_misc=2048, b_z=2500,
           c_b=3012, c_c=3524, c_x=4036, c_z=4548,
           d_q=5060, d_k=5572, d_vf=6084, d_z=6600, gates=7112)
EPS = 1e-6
TOPK = 256
NITER = 13
NEG = -1.0e30
NP32 = 672
NPBF = 1536
NC32 = 648
NCBF = 512
SLOT = 4160
NWSLOT = 4
GRAN = 1024


class Buf:
    def __init__(self, ap, toks):
        self.ap = ap
        self.toks = toks


class Sched:
    COMPUTE = ('pe', 'act', 'dve', 'pool')

    def __init__(self):
        self.ops = []
        self.tok = {}
        self.ndma = {}

    def _flat(self, lst):
        out = []
        for x in lst:
            if x is None:
                continue
            if isinstance(x, Buf):
                out.extend(x.toks)
            elif isinstance(x, (list, tuple)) and len(x) > 0 and isinstance(x[0], (Buf, list)):
                out.extend(self._flat(x))
            else:
                out.append(x)
        return out

    def op(self, eng, fn, reads=(), writes=(), dma=False):
        idx = len(self.ops)
        reads = self._flat(reads)
        writes = self._flat(writes)
        key = ('dma', idx) if dma else eng
        deps = set()
        for t in reads:
            st = self.tok.setdefault(t, [dict(), dict()])
            for e, i in st[0].items():
                deps.add(i)
        for t in writes:
            st = self.tok.setdefault(t, [dict(), dict()])
            for e, i in st[0].items():
                if e == key and not dma:
                    continue
                deps.add(i)
            for e, i in st[1].items():
                if e == key and not dma:
                    continue
                deps.add(i)
        for t in reads:
            self.tok[t][1][key] = idx
        for t in writes:
            self.tok[t][0] = {key: idx}
            self.tok[t][1] = {}
        deps.discard(idx)
        self.ops.append(dict(eng=eng, fn=fn, deps=deps, dma=dma, sig=False))
        return idx

    def emit(self, nc):
        ops = self.ops
        for o in ops:
            for d in o['deps']:
                ops[d]['sig'] = True
        tick = {e: 0 for e in self.COMPUTE}
        sems = {e: nc.alloc_semaphore(name="s_" + e) for e in self.COMPUTE}
        NR = 20
        rings = {q: [nc.alloc_semaphore(name="d_%s_%d" % (q, i)) for i in range(NR)] for q in ('sp', 'pool')}
        dcount = {'sp': 0, 'pool': 0}
        for o in ops:
            if o['dma']:
                q = o['eng']
                n = dcount[q]
                dcount[q] += 1
                o['dsem'] = rings[q][n % NR]
                o['dval'] = 16 * (n // NR + 1)
                o['dprev'] = 16 * (n // NR)
            elif o['sig']:
                tick[o['eng']] += 1
                o['tick'] = tick[o['eng']]
        by_eng = {e: [] for e in ('pe', 'act', 'dve', 'pool', 'sp')}
        for i, o in enumerate(ops):
            by_eng[o['eng']].append(i)

        def run(ename, e):
            seen = {}
            for i in by_eng[ename]:
                o = ops[i]
                waits = {}
                for d in o['deps']:
                    p = ops[d]
                    if p['dma']:
                        s, v = p['dsem'], p['dval']
                    else:
                        s, v = sems[p['eng']], p['tick']
                    k = id(s)
                    if seen.get(k, 0) >= v:
                        continue
                    if k not in waits or waits[k][1] < v:
                        waits[k] = (s, v)
                if o['dma'] and o['dprev'] > 0:
                    s, v = o['dsem'], o['dprev']
                    k = id(s)
                    if seen.get(k, 0) < v and (k not in waits or waits[k][1] < v):
                        waits[k] = (s, v)
                for k, (s, v) in waits.items():
                    e.wait_ge(s, v)
                    seen[k] = v
                if o['fn'] is None:
                    continue
                ins = o['fn'](e)
                if o['dma']:
                    ins.then_inc(o['dsem'], 16)
                elif o['sig']:
                    ins.then_inc(sems[ename], 1)

        with nc.Block() as block:
            @block.tensor
            def _(e):
                run('pe', e)

            @block.scalar
            def _(e):
                run('act', e)

            @block.vector
            def _(e):
                run('dve', e)

            @block.gpsimd
            def _(e):
                run('pool', e)

            @block.sync
            def _(e):
                run('sp', e)


class Builder:
    def __init__(self, S, NL, final_norm=True, dbg=False):
        self.S, self.NL, self.final_norm, self.dbg = S, NL, final_norm, dbg
        self.NT = S // 512
        self.NCH = S // 128
        self.nc = bass.Bass("TRN2", target_bir_lowering=False)
        self.s = Sched()
        self.wplan = None
        self.wpos = 0
        self.ps_rr = 0
        self.ps_ring = [0, 1, 2, 3]
        self._alloc()

    def _alloc(self):
        nc, S, NL = self.nc, self.S, self.NL
        dt = nc.dram_tensor
        self.x_in = dt("x", [S, D], F32, kind="ExternalInput").ap()
        self.w_in = dt("w_in", [NL, D, INW], F32, kind="ExternalInput").ap()
        self.w_br = dt("w_br", [NL, 4, W, D], F32, kind="ExternalInput").ap()
        self.w_out = dt("w_out", [NL, D, D], F32, kind="ExternalInput").ap()
        self.pf32_d = dt("pf32", [NL, 128, NP32], F32, kind="ExternalInput").ap()
        self.pbf_d = dt("pbf", [NL, 128, NPBF], F32, kind="ExternalInput").ap()
        self.cf32_d = dt("cf32", [128, NC32], F32, kind="ExternalInput").ap()
        self.cbf_d = dt("cbf", [128, NCBF], F32, kind="ExternalInput").ap()
        self.fg_d = dt("fg", [128, D], F32, kind="ExternalInput").ap()
        self.out_d = dt("out", [S, D], F32, kind="ExternalOutput").ap()
        self.xs_d = [dt("xs%d" % i, [S, D], F32).ap() for i in range(2)]
        if self.dbg:
            self.dbg_d = dt("dbg", [7, S, D], F32, kind="ExternalOutput").ap()

        def sb(name, shape, dtype):
            t = nc.alloc_sbuf_tensor(name, shape, dtype)
            return Buf(t.ap() if hasattr(t, 'ap') else t[:], [name])
        self.sb = sb
        NCH = self.NCH
        self.kT = sb("kT", [128, 4, S], BF16)
        self.vC = sb("vC", [128, NCH, W], BF16)
        self.cTok = sb("cTok", [128, NCH, 128], BF16)
        self.cT = sb("cT", [128, S], BF16)
        self.kiT = sb("kiT", [128, S], BF16)
        self.cumneg = sb("cumneg", [128, NCH, 4], F32)
        self.xc = [sb("xc0", [128, D], F32)] * 2
        self.hT = sb("hT", [128, 8, 512], BF16)
        self.mg = sb("mg", [128, 8, 512], BF16)
        self.yT = sb("yT", [128, 4, 512], BF16)
        self.wslot = [sb("wslot%d" % i, [128, SLOT], BF16) for i in range(NWSLOT)]
        self.pf32 = sb("pf32s", [128, NP32], F32)
        self.pbf = sb("pbfs", [128, NPBF], BF16)
        self.cf32 = sb("cf32s", [128, NC32], F32)
        self.cbf = sb("cbfs", [128, NCBF], BF16)
        self.biasm = sb("biasm", [128, 4, 128], F32)
        self.wTm = sb("wTm", [128, 4, 128], BF16)
        self.halo = sb("halo", [128, 4, 2], F32)
        self.small = sb("small", [128, 64], F32)
        self.bis = sb("bis", [128, 32], F32)
        self.wstat = sb("wstat", [128, 4, 8], F32)
        self.NSCR = 49 * 1024 + (16 * 1024 if self.dbg else 0)
        self.scr_t = nc.alloc_sbuf_tensor("scr", [128, self.NSCR // 2], BF16)
        self.scr_off = 0
        self.ps = []
        for i in range(8):
            t = nc.alloc_psum_tensor("ps%d" % i, [128, 512], F32)
            self.ps.append(Buf(t.ap() if hasattr(t, 'ap') else t[:], ["ps%d" % i]))

    def scr_reset(self):
        self.scr_off = 0

    def scr(self, shape, dtype, align=64):
        n = 1
        for d_ in shape[1:]:
            n *= d_
        nbytes = n * (4 if dtype == F32 else (1 if dtype == U8 else 2))
        off = (self.scr_off + align - 1) // align * align
        assert off + nbytes <= self.NSCR, ("scratch overflow", off, nbytes)
        self.scr_off = off + nbytes
        full = self.scr_t.ap() if hasattr(self.scr_t, 'ap') else self.scr_t[:]
        ap = full[:, off // 2:(off + nbytes) // 2]
        if dtype == F32:
            ap = ap.bitcast(F32)
        elif dtype == U8:
            ap = ap.bitcast(U8)
        if len(shape) == 3:
            ap = ap.rearrange("p (a b) -> p a b", a=shape[1])
        toks = [("scr", g) for g in range(off // GRAN, (off + nbytes - 1) // GRAN + 1)]
        return Buf(ap, toks)

    def psum(self):
        ring = self.ps_ring
        b = self.ps[ring[self.ps_rr % len(ring)]]
        self.ps_rr += 1
        return b

    def interleave(self, *gens):
        gens = list(gens)
        while gens:
            for g in list(gens):
                try:
                    next(g)
                except StopIteration:
                    gens.remove(g)

    def wnext(self, desc):
        if self.wplan is None:
            self.wrec.append(desc)
            return self.wslot[(len(self.wrec) - 1) % NWSLOT]
        j = self.wpos
        assert self.wplan[j] == desc, (self.wplan[j], desc)
        self.wpos += 1
        nxt = j + NWSLOT - 2
        if nxt < len(self.wplan):
            self._wload(nxt)
        return self.wslot[j % NWSLOT]

    def _wload(self, j):
        kind, l, a, b = self.wplan[j]
        slot = self.wslot[j % NWSLOT]
        if kind == 'in':
            src = self.w_in[l, :, a:a + b].rearrange("(kc p) n -> p kc n", p=128)
            dst = slot.ap[:, 0:8 * b].rearrange("p (kc n) -> p kc n", kc=8)
        elif kind == 'br':
            src = self.w_br[l, a][:, b * 512:(b + 1) * 512].rearrange("(wc p) n -> p wc n", p=128)
            dst = slot.ap[:, 0:2048].rearrange("p (wc n) -> p wc n", wc=4)
        else:
            src = self.w_out[l, :, a:a + b].rearrange("(kc p) n -> p kc n", p=128)
            dst = slot.ap[:, 0:8 * b].rearrange("p (kc n) -> p kc n", kc=8)
        self.s.op('pool', lambda e, dst=dst, src=src: e.dma_start(out=dst, in_=src),
                  reads=[], writes=[slot], dma=True)

    def mm(self, out, lhsT, rhs, start, stop, reads, writes):
        self.s.op('pe', lambda e: e.matmul(out, lhsT, rhs, start=start, stop=stop, skip_group_check=True),
                  reads=reads, writes=writes)

    def act(self, out, in_, func, reads, writes, bias=None, scale=None, accum_out=None):
        kw = {}
        if bias is not None:
            kw['bias'] = bias
        if scale is not None:
            kw['scale'] = scale
        if accum_out is not None:
            kw['accum_out'] = accum_out
        self.s.op('act', lambda e: e.activation(out, in_, func, **kw), reads=reads, writes=writes)

    def tt(self, eng, out, in0, in1, op, reads, writes):
        self.s.op(eng, lambda e: e.tensor_tensor(out, in0, in1, op), reads=reads, writes=writes)

    def ts(self, eng, out, in0, s1, s2, op0, op1, reads, writes, accum_out=None):
        if op1 is None:
            self.s.op(eng, lambda e: e.tensor_scalar(out, in0, s1, None, op0), reads=reads, writes=writes)
        elif accum_out is None:
            self.s.op(eng, lambda e: e.tensor_scalar(out, in0, s1, s2, op0, op1), reads=reads, writes=writes)
        else:
            self.s.op(eng, lambda e: e.tensor_scalar(out, in0, s1, s2, op0, op1, accum_out=accum_out),
                      reads=reads, writes=writes)

    def stt(self, eng, out, in0, scalar, in1, op0, op1, reads, writes, accum_out=None):
        if accum_out is None:
            self.s.op(eng, lambda e: e.scalar_tensor_tensor(out, in0, scalar, in1, op0, op1),
                      reads=reads, writes=writes)
        else:
            self.s.op(eng, lambda e: e.scalar_tensor_tensor(out, in0, scalar, in1, op0, op1, accum_out=accum_out),
                      reads=reads, writes=writes)

    def rsqrt(self, out, in_, scale, eps, reads, writes):
        self.act(out, in_, AF.Sqrt, reads=reads, writes=writes, bias=self.epsc, scale=scale)
        self.s.op('dve', lambda e: e.reciprocal(out, out), reads=writes, writes=writes)

    def dma(self, out, in_, reads, writes, q='sp'):
        self.s.op(q, lambda e: e.dma_start(out=out, in_=in_), reads=reads, writes=writes, dma=True)

    def proj_fm_sub(self, wb, sub, ncols=512):
        ps = self.psum()
        wv = wb.ap[:, 0:8 * ncols].rearrange("p (kc n) -> p kc n", kc=8)
        for kc in range(8):
            self.mm(ps.ap, wv[:, kc, sub * 128:(sub + 1) * 128], self.hT.ap[:, kc, :], kc == 0, kc == 7,
                    reads=[wb, self.hT], writes=[ps])
        return ps

    def proj_tm(self, wb, c, col0, ncols, blkcols, ps=None, first=True):
        if ps is None:
            ps = self.psum()
        wv = wb.ap[:, 0:8 * blkcols].rearrange("p (kc n) -> p kc n", kc=8)
        for kc in range(8):
            self.mm(ps.ap[:, 0:ncols], self.hT.ap[:, kc, c * 128:(c + 1) * 128], wv[:, kc, col0:col0 + ncols],
                    first and kc == 0, kc == 7, reads=[wb, self.hT], writes=[ps])
        return ps

    def build(self):
        self.wplan = None
        self.wrec = []
        saved = self.s
        self.s = Sched()
        self.program()
        plan = self.wrec
        self.s = saved
        self.wplan = plan
        self.wpos = 0
        self.ps_rr = 0
        for j in range(min(NWSLOT - 2, len(plan))):
            self._wload(j)
        self.program()
        assert self.wpos == len(plan)
        self.s.emit(self.nc)
        return self.nc

    def program(self):
        S, NL = self.S, self.NL
        self.dma(self.cf32.ap, self.cf32_d, [], [self.cf32])
        self.dma(self.cbf.ap, self.cbf_d, [], [self.cbf], q='pool')
        self.identf = self.cf32.ap[:, 0:128]
        self.Uf = self.cf32.ap[:, 128:256]
        self.E64 = self.cf32.ap[:, 256:384]
        self.E127 = self.cf32.ap[:, 384:512]
        self.onesf = self.cf32.ap[:, 512:640]
        self.epsc = self.cf32.ap[:, 640:641]
        self.onec = self.cf32.ap[:, 641:642]
        self.ident = self.cbf.ap[:, 0:128]
        self.tri01 = self.cbf.ap[:, 128:256]
        self.maskneg = self.cbf.ap[:, 256:384]
        self.ones = self.cbf.ap[:, 384:512]
        out_tokens = []
        for l in range(NL):
            src = self.x_in if l == 0 else self.xs_d[(l - 1) % 2]
            srct = "x_in" if l == 0 else "xs%d" % ((l - 1) % 2)
            last = (l == NL - 1)
            dst = self.out_d if last else self.xs_d[l % 2]
            dstt = "out" if last else "xs%d" % (l % 2)
            self.layer(l, src, srct, dst, dstt, last and self.final_norm)
        toks = [("out", t, c) for t in range(self.NT) for c in range(4)]
        if self.dbg:
            toks.append("dbg")
        self.s.op('sp', None, reads=toks, writes=[])

    def layer(self, l, src, srct, dst, dstt, do_final):
        s = self.s
        self.dma(self.pf32.ap, self.pf32_d[l], [], [self.pf32])
        self.dma(self.pbf.ap, self.pbf_d[l], [], [self.pbf], q='pool')
        pf = self.pf32.ap
        self.gT = pf[:, 0:8]
        self.GT = pf[:, 8:12]
        self.BT = pf[:, 12:16]
        self.bsbc = pf[:, 16:528].rearrange("p (g t) -> p g t", g=4)
        self.kvg = pf[:, 528:656]
        self.cw = pf[:, 656:668].rearrange("p (a b) -> p a b", a=4)
        self.bfbc = pf[:, 668:672]
        pb = self.pbf.ap
        wT = pb[:, 0:512].rearrange("p (g t) -> p g t", g=4)
        self.wuk = pb[:, 512:1024].rearrange("p (h l) -> p h l", h=4)
        self.wuv = pb[:, 1024:1536].rearrange("p (h d) -> p h d", h=4)
        self.tt('dve', self.wTm.ap, wT, self.tri01.unsqueeze(1).to_broadcast([128, 4, 128]), ALU.mult,
                reads=[self.pbf, self.cbf], writes=[self.wTm])
        ps = self.psum()
        self.mm(ps.ap, self.ones, self.wTm.ap.rearrange("p g t -> p (g t)"), True, True,
                reads=[self.cbf, self.wTm], writes=[ps])
        self.tt('dve', self.biasm.ap, ps.ap.rearrange("p (g t) -> p g t", g=4),
                self.BT.unsqueeze(2).to_broadcast([128, 4, 128]), ALU.mult,
                reads=[ps, self.pf32], writes=[self.biasm])
        self.tt('dve', self.biasm.ap, self.biasm.ap, self.bsbc, ALU.add,
                reads=[self.biasm, self.pf32], writes=[self.biasm])
        s.op('dve', lambda e: e.memset(self.halo.ap, 0.0), reads=[], writes=[self.halo])
        for t in range(self.NT):
            self.tile(l, t, src, srct, dst, dstt, do_final)

    def tile(self, l, t, src, srct, dst, dstt, do_final):
        s = self.s
        self.scr_reset()
        sm = self.small
        xc2 = [self.xc[0], self.scr([128, D], F32)]
        xsbs = [self.scr([128, D], BF16) for _ in range(2)]
        junk = self.scr([128, D], BF16)
        for c in range(4):
            gc = 4 * t + c
            xb = xc2[c % 2]
            r0 = gc * 128
            self.dma(xb.ap, src[r0:r0 + 128, :], [(srct, t, c)], [xb])
            xsb = xsbs[c % 2]
            ss = sm.ap[:, c:c + 1]
            s.op('dve', lambda e, ss=ss: e.memset(ss, 0.0), reads=[], writes=[("sm", c)])
            self.act(junk.ap, xb.ap, AF.Square, reads=[xb], writes=[junk, ("sm", c)], accum_out=ss)
            rs = sm.ap[:, 8 + c:9 + c]
            self.rsqrt(rs, ss, 1.0 / D, EPS, reads=[("sm", c)], writes=[("sm", 8 + c)])
            self.ts('dve', xsb.ap, xb.ap, rs, None, ALU.mult, None, reads=[xb, ("sm", 8 + c)], writes=[xsb])
            for half in range(2):
                ps = self.psum()
                for k4 in range(4):
                    kc = half * 4 + k4
                    self.mm(ps.ap[:, k4 * 128:(k4 + 1) * 128], xsb.ap[:, kc * 128:(kc + 1) * 128], self.ident,
                            k4 == 0, k4 == 3, reads=[xsb, self.cbf], writes=[ps])
                self.tt('dve', self.hT.ap[:, half * 4:half * 4 + 4, c * 128:(c + 1) * 128],
                        ps.ap.rearrange("p (k n) -> p k n", k=4),
                        self.gT[:, half * 4:half * 4 + 4].unsqueeze(2).to_broadcast([128, 4, 128]), ALU.mult,
                        reads=[ps, self.pf32], writes=[self.hT])
            self.scr_off -= 0
        self.scr_reset()
        self.ps_ring = [0, 1, 2, 3, 4, 5, 6, 7]
        self.branch_A(l, t)
        self.scr_reset()
        self.lift(l, t, 0)
        self.scr_reset()
        self.branch_C(l, t)
        self.scr_reset()
        self.lift(l, t, 2)
        self.scr_reset()
        self.ps_ring = [0, 1, 2, 3]
        self.branch_D(l, t)
        self.scr_reset()
        self.lift(l, t, 3)
        self.scr_reset()
        self.branch_B(l, t)
        self.scr_reset()
        self.lift(l, t, 1)
        self.scr_reset()
        self.outproj(l, t, src, srct, dst, dstt, do_final)

    def dbg_dump(self, idx, t):
        if not self.dbg:
            return
        tmp = self.scr([128, 4, 512], F32)
        self.s.op('dve', lambda e: e.tensor_copy(tmp.ap, self.yT.ap), reads=[self.yT], writes=[tmp])
        for wc in range(4):
            dstap = self.dbg_d[idx, t * 512:(t + 1) * 512, wc * 128:(wc + 1) * 128].rearrange("t w -> w t")
            self.s.op('sp', lambda e, dstap=dstap, wc=wc: e.dma_start(out=dstap, in_=tmp.ap[:, wc, :],
                                                                         allow_slow_non_contiguous=True),
                      reads=[tmp], writes=["dbg"], dma=True)

    def branch_A(self, l, t):
        s = self.s
        vn = self.scr([128, 4, W], BF16)
        t1 = self.scr([128, 4, 512], F32)
        st = self.scr([128, 4, 8], F32)
        mv = self.scr([128, 4, 2], F32)
        wb = self.wnext(('in', l, OFF['a_v'], 512))
        for c in range(4):
            ps = self.proj_tm(wb, c, 0, 512, 512)
            s.op('dve', lambda e, c=c, ps=ps: e.bn_stats(st.ap[:, c, 0:6], ps.ap), reads=[ps], writes=[("Ast", c)])
            s.op('dve', lambda e, c=c: e.bn_aggr(mv.ap[:, c, :], st.ap[:, c, 0:6]), reads=[("Ast", c)],
                 writes=[("Amv", c)])
            rstd = st.ap[:, c, 6:7]
            self.rsqrt(rstd, mv.ap[:, c, 1:2], 1.0, EPS, reads=[("Amv", c)], writes=[("Ars", c)])
            self.ts('dve', vn.ap[:, c, :], ps.ap, mv.ap[:, c, 0:1], rstd, ALU.subtract, ALU.mult,
                    reads=[ps, ("Amv", c), ("Ars", c)], writes=[vn])
        for g in range(4):
            ps = self.psum()
            for c in range(4):
                self.mm(ps.ap[:, c * 128:(c + 1) * 128], vn.ap[:, c, g * 128:(g + 1) * 128], self.wTm.ap[:, g, :],
                        c == 0, c == 3, reads=[vn, self.wTm], writes=[ps])
            self.stt('dve', t1.ap[:, g, :].rearrange("p (c n) -> p c n", c=4),
                     ps.ap.rearrange("p (c n) -> p c n", c=4), self.GT[:, g:g + 1],
                     self.biasm.ap[:, g, :].unsqueeze(1).to_broadcast([128, 4, 128]), ALU.mult, ALU.add,
                     reads=[ps, self.pf32, self.biasm], writes=[t1])
        wb = self.wnext(('in', l, OFF['a_u'], 512))
        for g in range(4):
            ps = self.proj_fm_sub(wb, g)
            self.tt('dve', t1.ap[:, g, :], ps.ap, t1.ap[:, g, :], ALU.mult, reads=[ps, t1], writes=[t1])
        wb = self.wnext(('in', l, OFF['a_z'], 512))
        szs = [self.scr([128, 512], F32) for _ in range(2)]
        for g in range(4):
            ps = self.proj_fm_sub(wb, g)
            sz = szs[g % 2]
            self.act(sz.ap, ps.ap, AF.Silu, reads=[ps], writes=[sz])
            self.tt('dve', self.yT.ap[:, g, :], t1.ap[:, g, :], sz.ap, ALU.mult, reads=[t1, sz], writes=[self.yT])
        self.dbg_dump(0, t)

    def lift(self, l, t, b):
        first = (b == 0)
        sgs = [self.scr([128, 512], F32) for _ in range(2)]
        tmps = [self.scr([128, 512], F32) for _ in range(2)]
        for half in range(2):
            wg = self.wnext(('in', l, OFF['gates'] + b * D + half * 512, 512))
            wbr = self.wnext(('br', l, b, half))
            wbv = wbr.ap[:, 0:2048].rearrange("p (wc n) -> p wc n", wc=4)
            for sub in range(4):
                ncn = half * 4 + sub
                psg = self.proj_fm_sub(wg, sub)
                sg = sgs[ncn % 2]
                self.act(sg.ap, psg.ap, AF.Sigmoid, reads=[psg], writes=[sg])
                psb = self.psum()
                for wc in range(4):
                    self.mm(psb.ap, wbv[:, wc, sub * 128:(sub + 1) * 128], self.yT.ap[:, wc, :], wc == 0, wc == 3,
                            reads=[wbr, self.yT], writes=[psb])
                if first:
                    self.tt('dve', self.mg.ap[:, ncn, :], psb.ap, sg.ap, ALU.mult, reads=[psb, sg],
                            writes=[("mg", ncn)])
                else:
                    tmp = tmps[ncn % 2]
                    self.tt('dve', tmp.ap, psb.ap, sg.ap, ALU.mult, reads=[psb, sg], writes=[tmp])
                    self.tt('pool', self.mg.ap[:, ncn, :], self.mg.ap[:, ncn, :], tmp.ap, ALU.add,
                            reads=[tmp, ("mg", ncn)], writes=[("mg", ncn)])

    def branch_C(self, l, t):
        s = self.s
        yc = self.scr([128, 4, 514], F32)
        cv = self.scr([128, 4, 512], F32)
        ccs = [self.scr([128, 512], F32) for _ in range(2)]
        wbc = self.wnext(('in', l, OFF['c_c'], 512))
        wbx = self.wnext(('in', l, OFF['c_x'], 512))
        s.op('dve', lambda e: e.tensor_copy(yc.ap[:, :, 0:2], self.halo.ap), reads=[self.halo], writes=[yc])
        for g in range(4):
            ps = self.proj_fm_sub(wbc, g)
            cc = ccs[g % 2]
            self.act(cc.ap, ps.ap, AF.Copy, reads=[ps], writes=[cc])
            ps2 = self.proj_fm_sub(wbx, g)
            self.tt('dve', yc.ap[:, g, 2:514], ps2.ap, cc.ap, ALU.mult, reads=[ps2, cc], writes=[yc])
            self.ts('dve', cv.ap[:, g, :], yc.ap[:, g, 0:512], self.cw[:, g, 0:1], None, ALU.mult, None,
                    reads=[yc, self.pf32], writes=[cv])
            self.stt('dve', cv.ap[:, g, :], yc.ap[:, g, 1:513], self.cw[:, g, 1:2], cv.ap[:, g, :], ALU.mult, ALU.add,
                     reads=[yc, self.pf32, cv], writes=[cv])
            self.stt('dve', cv.ap[:, g, :], yc.ap[:, g, 2:514], self.cw[:, g, 2:3], cv.ap[:, g, :], ALU.mult, ALU.add,
                     reads=[yc, self.pf32, cv], writes=[cv])
        s.op('dve', lambda e: e.tensor_copy(self.halo.ap, yc.ap[:, :, 512:514]), reads=[yc], writes=[self.halo])
        wbb = self.wnext(('in', l, OFF['c_b'], 512))
        for g in range(4):
            ps = self.proj_fm_sub(wbb, g)
            self.tt('dve', cv.ap[:, g, :], ps.ap, cv.ap[:, g, :], ALU.mult, reads=[ps, cv], writes=[cv])
        wbz = self.wnext(('in', l, OFF['c_z'], 512))
        for g in range(4):
            ps = self.proj_fm_sub(wbz, g)
            sz = ccs[g % 2]
            self.act(sz.ap, ps.ap, AF.Silu, reads=[ps], writes=[sz])
            self.tt('dve', self.yT.ap[:, g, :], cv.ap[:, g, :], sz.ap, ALU.mult, reads=[cv, sz], writes=[self.yT])
        self.dbg_dump(2, t)

    def branch_D(self, l, t):
        s = self.s
        NCH = self.NCH
        qTf = self.scr([128, 4, 512], BF16)
        szD = self.scr([128, 4, 512], BF16)
        biasall = self.scr([128, 4, NCH * 4], F32)
        pTs = [self.scr([128, 512], BF16) for _ in range(3)]
        rden = self.scr([128, 512], F32)
        tmpo = self.scr([128, 512], F32)
        lsb = self.scr([128, 4, 4], F32)
        crefs = self.scr([128, 4, 4], F32)
        wb = self.wnext(('in', l, OFF['d_q'], 512))
        for h in range(4):
            ps = self.proj_fm_sub(wb, h)
            self.act(qTf.ap[:, h, :], ps.ap, AF.Copy, reads=[ps], writes=[qTf], scale=float(128 ** -0.5))
        wb = self.wnext(('in', l, OFF['d_k'], 512))
        for h in range(4):
            ps = self.proj_fm_sub(wb, h)
            s.op('dve', lambda e, h=h, ps=ps: e.tensor_copy(self.kT.ap[:, h, t * 512:(t + 1) * 512], ps.ap),
                 reads=[ps], writes=[("kT", t)])
        wb = self.wnext(('in', l, OFF['d_vf'], 516))
        for c in range(4):
            gc = 4 * t + c
            ps = self.proj_tm(wb, c, 0, 512, 516)
            self.act(self.vC.ap[:, gc, :], ps.ap, AF.Copy, reads=[ps], writes=[("vC", gc)])
            psf = self.proj_tm(wb, c, 512, 4, 516)
            xf = lsb.ap[:, c, :]
            self.tt('dve', xf, psf.ap[:, 0:4], self.bfbc, ALU.add, reads=[psf, self.pf32], writes=[("lsb", c)])
            self.act(xf, xf, AF.Exp, reads=[("lsb", c)], writes=[("lsb", c)], scale=-1.0)
            self.act(xf, xf, AF.Ln, reads=[("lsb", c), self.cf32], writes=[("lsb", c)], bias=self.onec)
            psc = self.psum()
            self.mm(psc.ap[:, 0:4], self.Uf, xf, True, gc == 0, reads=[self.cf32, ("lsb", c)], writes=[psc])
            if gc > 0:
                self.mm(psc.ap[:, 0:4], self.E127, self.cumneg.ap[:, gc - 1, :], False, True,
                        reads=[self.cf32, ("cum", gc - 1)], writes=[psc])
            s.op('dve', lambda e, gc=gc, psc=psc: e.tensor_copy(self.cumneg.ap[:, gc, :], psc.ap[:, 0:4]),
                 reads=[psc], writes=[("cum", gc)])
            if c % 2 == 0:
                psr = self.psum()
                self.mm(psr.ap[:, 0:4], self.E127, self.cumneg.ap[:, gc, :], True, True,
                        reads=[self.cf32, ("cum", gc)], writes=[psr])
                s.op('dve', lambda e, c=c, psr=psr: e.tensor_copy(crefs.ap[:, c // 2, :], psr.ap[:, 0:4]),
                     reads=[psr], writes=[("cref", c // 2)])
            else:
                nk = gc + 1
                m = c // 2
                self.tt('dve', biasall.ap[:, m, 0:nk * 4].rearrange("p (i h) -> p i h", h=4),
                        self.cumneg.ap[:, 0:nk, :], crefs.ap[:, m, :].unsqueeze(1).to_broadcast([128, nk, 4]),
                        ALU.subtract, reads=[("cum", i) for i in range(nk)] + [("cref", m)], writes=[("bias", m)])
        wb = self.wnext(('in', l, OFF['d_z'], 512))
        for h in range(4):
            ps = self.proj_fm_sub(wb, h)
            self.act(szD.ap[:, h, :], ps.ap, AF.Silu, reads=[ps], writes=[szD])
        nkc = 4 * t + 4
        items = [(h, i) for h in range(4) for i in range(nkc)]
        pss_of = {}
        LA = 2

        def S1(k):
            h, i = items[k]
            c0 = max(i - 4 * t, 0) * 128
            pss = self.psum()
            pss_of[k] = pss
            self.mm(pss.ap[:, c0:512], self.kT.ap[:, h, i * 128:(i + 1) * 128], qTf.ap[:, h, c0:512], True, True,
                    reads=[("kT", i // 4), qTf], writes=[pss])

        def S23(k):
            h, i = items[k]
            ci = i - 4 * t
            j0 = max(ci, 0)
            c0 = j0 * 128
            acc_o = self.ps[4 + 2 * (h % 2)]
            acc_d = self.ps[5 + 2 * (h % 2)]
            pss = pss_of.pop(k)
            pT = pTs[k % 3]
            for m in range(j0 // 2, 2):
                a0 = max(c0, m * 256)
                self.act(pT.ap[:, a0:(m + 1) * 256], pss.ap[:, a0:(m + 1) * 256], AF.Exp,
                         reads=[pss, ("bias", m)], writes=[pT],
                         bias=biasall.ap[:, m, i * 4 + h:i * 4 + h + 1])
            if ci >= 0:
                self.tt('pool', pT.ap[:, c0:c0 + 128], pT.ap[:, c0:c0 + 128], self.tri01, ALU.mult,
                        reads=[pT, self.cbf], writes=[pT])
            self.mm(acc_o.ap[:, c0:512], self.vC.ap[:, i, h * 128:(h + 1) * 128], pT.ap[:, c0:512],
                    i == 0, i == nkc - 1, reads=[("vC", i), pT], writes=[acc_o])
            self.mm(acc_d.ap[:, c0:512], self.ones, pT.ap[:, c0:512], i == 0, i == nkc - 1,
                    reads=[self.cbf, pT], writes=[acc_d])
            if i == nkc - 1:
                s.op('dve', lambda e, acc_d=acc_d: e.reciprocal(rden.ap, acc_d.ap), reads=[acc_d], writes=[rden])
                self.tt('dve', tmpo.ap, acc_o.ap, rden.ap, ALU.mult, reads=[acc_o, rden], writes=[tmpo])
                self.tt('dve', self.yT.ap[:, h, :], tmpo.ap, szD.ap[:, h, :], ALU.mult, reads=[tmpo, szD],
                        writes=[self.yT])

        for k in range(min(LA, len(items))):
            S1(k)
        for k in range(len(items)):
            if k + LA < len(items):
                S1(k + LA)
            S23(k)
        self.dbg_dump(3, t)

    def branch_B(self, l, t):
        s = self.s
        S = self.S
        sm = self.small
        K1 = 1024
        qlatT = self.scr([128, 4, 512], BF16)
        szB = self.scr([128, 4, 512], BF16)
        qiT = self.scr([128, 2, 512], BF16)
        scores = [self.scr([128, S], BF16, align=K1) for _ in range(2)]
        junk = self.scr([128, S], U8)
        rhflat = self.scr([128, 2048], BF16, align=K1)
        rh = [Buf(rhflat.ap[:, h * 512:(h + 1) * 512], [rhflat.toks[h]]) for h in range(4)]
        st8 = Buf(rhflat.ap.bitcast(U8), rhflat.toks)
        pmall = self.scr([128, 4, 512], BF16, align=K1)
        pm = [Buf(pmall.ap[:, h, :], [pmall.toks[h]]) for h in range(4)]
        qTb = pmall
        pmT = [self.scr([128, 512], BF16, align=K1) for _ in range(2)]
        Dsg = self.scr([128, 4, 128], BF16, align=K1)
        Dg4 = self.scr([128, 4, 128], F32, align=K1)
        rdbc = self.scr([128, 512], F32, align=K1)
        olat = self.scr([128, 512], BF16, align=K1)
        ctmp = self.scr([128, 128], F32, align=K1)
        kid = self.scr([128, 128], BF16)
        qib = self.scr([128, 256], BF16)
        denp = self.scr([128, 4, 8], F32, align=K1)
        den = self.scr([128, 4], F32)
        def prepQ():
            wb = self.wnext(('in', l, OFF['b_q'], 512))
            for h in range(4):
                ps = self.proj_fm_sub(wb, h)
                self.act(qTb.ap[:, h, :], ps.ap, AF.Copy, reads=[ps], writes=[qTb])
                yield
            for h in range(4):
                ps = self.psum()
                self.mm(ps.ap, self.wuk[:, h, :], qTb.ap[:, h, :], True, True, reads=[self.pbf, qTb], writes=[ps])
                self.act(qlatT.ap[:, h, :], ps.ap, AF.Copy, reads=[ps], writes=[qlatT], scale=float(128 ** -0.5))
                yield

        def prepZ():
            wb = self.wnext(('in', l, OFF['b_z'], 512))
            for h in range(4):
                ps = self.proj_fm_sub(wb, h)
                self.act(szB.ap[:, h, :], ps.ap, AF.Silu, reads=[ps], writes=[szB])
                yield

        wb = self.wnext(('in', l, OFF['b_misc'], 452))
        for c in range(4):
            gc = 4 * t + c
            ps = self.proj_tm(wb, c, 0, 452, 452)
            ss = sm.ap[:, 16 + c:17 + c]
            s.op('dve', lambda e, ss=ss: e.memset(ss, 0.0), reads=[], writes=[("sm", 16 + c)])
            self.act(ctmp.ap, ps.ap[:, 0:128], AF.Square, reads=[ps], writes=[ctmp, ("sm", 16 + c)], accum_out=ss)
            rs = sm.ap[:, 24 + c:25 + c]
            self.rsqrt(rs, ss, 1.0 / 128, EPS, reads=[("sm", 16 + c)], writes=[("sm", 24 + c)])
            self.stt('dve', self.cTok.ap[:, gc, :], ps.ap[:, 0:128], rs, self.kvg, ALU.mult, ALU.mult,
                     reads=[ps, ("sm", 24 + c), self.pf32], writes=[("cTok", gc)])
            pst = self.psum()
            self.mm(pst.ap[:, 0:128], self.cTok.ap[:, gc, :], self.ident, True, True,
                    reads=[("cTok", gc), self.cbf], writes=[pst])
            self.act(self.cT.ap[:, gc * 128:(gc + 1) * 128], pst.ap[:, 0:128], AF.Copy, reads=[pst],
                     writes=[("cT", gc)])
            s.op('dve', lambda e, ps=ps: e.tensor_copy(kid.ap.rearrange("p (a b) -> p a b", a=2),
                                                       ps.ap[:, 384:448].unsqueeze(1).to_broadcast([128, 2, 64])),
                 reads=[ps], writes=[kid])
            pst = self.psum()
            self.mm(pst.ap[:, 0:128], kid.ap, self.ident, True, True, reads=[kid, self.cbf], writes=[pst])
            self.act(self.kiT.ap[:, gc * 128:(gc + 1) * 128], pst.ap[:, 0:128], AF.Copy, reads=[pst],
                     writes=[("kiT", gc)])
            s.op('dve', lambda e, ps=ps: e.tensor_copy(qib.ap, ps.ap[:, 128:384]), reads=[ps], writes=[qib])
            pst = self.psum()
            for pr in range(2):
                self.mm(pst.ap[:, pr * 128:(pr + 1) * 128], qib.ap[:, pr * 128:(pr + 1) * 128], self.ident,
                        pr == 0, pr == 1, reads=[qib, self.cbf], writes=[pst])
            self.act(qiT.ap[:, :, c * 128:(c + 1) * 128], pst.ap[:, 0:256].rearrange("p (a b) -> p a b", a=2),
                     AF.Copy, reads=[pst], writes=[qiT])
            aw = self.wstat.ap[:, c, 0:4]
            sg = self.wstat.ap[:, c, 4:8]
            self.act(aw, ps.ap[:, 448:452], AF.Abs, reads=[ps], writes=[("aw", c)], scale=0.5 / 8.0)
            self.act(sg, ps.ap[:, 448:452], AF.Sign, reads=[ps], writes=[("sg", c)])
        acc_o = self.ps[4]
        acc_r = self.ps[5]
        pso = self.ps[5]
        self.ps_ring = [0, 1, 2, 3, 6, 7]
        bis = self.bis.ap
        th = bis[:, 0:1]

        def SC(c):
            gc = 4 * t + c
            nk = gc + 1
            nkt = (nk + 3) // 4
            score = scores[c % 2]
            aw = self.wstat.ap[:, c, 0:4]
            sg = self.wstat.ap[:, c, 4:8]
            self.tt('dve', Dsg.ap, self.ident.unsqueeze(1).to_broadcast([128, 4, 128]),
                    sg.unsqueeze(2).to_broadcast([128, 4, 128]), ALU.mult, reads=[self.cbf, ("sg", c)], writes=[Dsg])
            for kt in range(nkt):
                wdt = min(512, nk * 128 - kt * 512)
                k0 = kt * 512
                for h in range(4):
                    ps = self.psum()
                    r0 = (h % 2) * 64
                    self.mm(ps.ap[:, 0:wdt], qiT.ap[r0:r0 + 64, h // 2, c * 128:(c + 1) * 128],
                            self.kiT.ap[r0:r0 + 64, k0:k0 + wdt], True, True,
                            reads=[qiT] + [("kiT", k0 // 128 + i) for i in range(wdt // 128)], writes=[ps])
                    self.act(rh[h].ap[:, 0:wdt], ps.ap[:, 0:wdt], AF.Relu, reads=[ps, ("aw", c)], writes=[rh[h]],
                             scale=aw[:, h:h + 1])
                pssc = self.psum()
                diag = (gc // 4 == kt)
                for h in range(4):
                    self.mm(pssc.ap[:, 0:wdt], Dsg.ap[:, h, :], rh[h].ap[:, 0:wdt], h == 0, (h == 3 and not diag),
                            reads=[Dsg, rh[h]], writes=[pssc])
                if diag:
                    jd = gc % 4
                    self.mm(pssc.ap[:, jd * 128:(jd + 1) * 128], self.ident, self.maskneg, False, True,
                            reads=[self.cbf], writes=[pssc])
                s.op('dve', lambda e, score=score, pssc=pssc, k0=k0, wdt=wdt:
                     e.tensor_copy(score.ap[:, k0:k0 + wdt], pssc.ap[:, 0:wdt]), reads=[pssc], writes=[score])
                yield

        def BI(c):
            gc = 4 * t + c
            nk = gc + 1
            n = nk * 128
            score = scores[c % 2]
            if gc < 2:
                s.op('dve', lambda e: e.memset(th, -1.0e29), reads=[], writes=[self.bis])
            else:
                s.op('dve', lambda e: e.memset(bis[:, 0:32], 0.0), reads=[], writes=[self.bis])
                s.op('dve', lambda e: e.memset(th, 1.0e-30), reads=[], writes=[self.bis])
                delta = 4.0
                for it in range(NITER):
                    cnt = bis[:, 2 + it:3 + it]
                    self.ts('dve', junk.ap[:, 0:n], score.ap[:, 0:n], th, 0.0, ALU.is_ge, ALU.add,
                            reads=[score, self.bis], writes=[self.bis], accum_out=cnt)
                    sel = bis[:, 1:2]
                    self.ts('dve', sel, cnt, TOPK - 0.5, 2.0 * delta, ALU.is_ge, ALU.mult, reads=[self.bis],
                            writes=[self.bis])
                    self.stt('dve', th, th, -delta, sel, ALU.add, ALU.add, reads=[self.bis], writes=[self.bis])
                    delta *= 0.5
                    yield
                self.ts('dve', th, th, -2.0 * delta, None, ALU.add, None, reads=[self.bis], writes=[self.bis])
                cpos = bis[:, 2:3]
                nz = bis[:, 16:17]
                t1 = bis[:, 17:18]
                t2 = bis[:, 18:19]
                tie = bis[:, 19:20]
                aa = bis[:, 20:21]
                ini = bis[:, 21:22]
                dd = bis[:, 22:23]
                B_ = [self.bis]
                J_ = [("junk",)]
                self.ts('dve', junk.ap[:, 0:n], score.ap[:, 0:n], 0.0, 0.0, ALU.is_equal, ALU.add,
                        reads=[score], writes=J_ + B_, accum_out=nz)
                self.ts('dve', t1, cpos, TOPK - 0.5, None, ALU.is_le, None, reads=B_, writes=B_)
                self.tt('dve', t2, cpos, nz, ALU.add, reads=B_, writes=B_)
                self.stt('dve', tie, t2, TOPK - 0.5, t1, ALU.is_ge, ALU.mult, reads=B_, writes=B_)
                self.ts('dve', aa, cpos, 2.0, -(2.0 * TOPK + 2.0), ALU.mult, ALU.add, reads=B_, writes=B_)
                self.ts('dve', ini, aa, tie, 4.0, ALU.mult, ALU.add, reads=B_, writes=B_)
                self.ts('dve', dd, tie, -1.0, 1.0, ALU.mult, ALU.add, reads=B_, writes=B_)
                self.tt('dve', th, th, dd, ALU.mult, reads=B_, writes=B_)
                self.stt('dve', th, tie, 1.0e-30, th, ALU.mult, ALU.add, reads=B_, writes=B_)
                s.op('dve', lambda e, n=n: e.tensor_tensor_scan(st8.ap[:, 0:n], junk.ap[:, 0:n], junk.ap[:, 0:n], ini,
                                                                ALU.add, ALU.add), reads=J_ + B_, writes=[st8])
                self.stt('dve', junk.ap[:, 0:n], st8.ap[:, 0:n], 2.5, junk.ap[:, 0:n], ALU.is_le, ALU.mult,
                         reads=[st8] + J_, writes=J_)
                self.stt('dve', score.ap[:, 0:n], junk.ap[:, 0:n], 1.0, score.ap[:, 0:n], ALU.mult, ALU.add,
                         reads=J_ + [score], writes=[score])
                yield
            if self.dbg:
                self.dma(self.dbg_d[5, gc * 128:(gc + 1) * 128, 0:32], self.bis.ap, [self.bis], ["dbg"])
                self.dma(self.dbg_d[6, gc * 128:(gc + 1) * 128, 0:nk * 128], score.ap[:, 0:nk * 128], [score],
                         ["dbg"], q='pool')
            self.ts('dve', score.ap[:, 0:n], score.ap[:, 0:n], th, -30000.0, ALU.is_lt, ALU.mult,
                    reads=[score, self.bis], writes=[score])
            yield

        def AT(c):
            gc = 4 * t + c
            nk = gc + 1
            nkt = (nk + 3) // 4
            score = scores[c % 2]
            s.op('dve', lambda e: e.memset(denp.ap, 0.0), reads=[], writes=[denp])
            for kt in range(nkt):
                wdt = min(512, nk * 128 - kt * 512)
                k0 = kt * 512
                for h in range(4):
                    ps = self.psum()
                    self.mm(ps.ap[:, 0:wdt], qlatT.ap[:, h, c * 128:(c + 1) * 128], self.cT.ap[:, k0:k0 + wdt],
                            True, False, reads=[qlatT] + [("cT", k0 // 128 + i) for i in range(wdt // 128)],
                            writes=[ps])
                    self.mm(ps.ap[:, 0:wdt], self.ident, score.ap[:, k0:k0 + wdt], False, True,
                            reads=[self.cbf, score], writes=[ps])
                    self.act(pm[h].ap[:, 0:wdt], ps.ap[:, 0:wdt], AF.Exp, reads=[ps], writes=[pm[h], denp],
                             accum_out=denp.ap[:, h, kt:kt + 1])
                yield
                njj = wdt // 128
                psts = {}

                def T(jj):
                    pst = self.psum()
                    psts[jj] = pst
                    for h in range(4):
                        self.mm(pst.ap[:, h * 128:(h + 1) * 128], pm[h].ap[:, jj * 128:(jj + 1) * 128], self.ident,
                                h == 0, h == 3, reads=[pm[h], self.cbf], writes=[pst])
                T(0)
                for jj in range(njj):
                    if jj + 1 < njj:
                        T(jj + 1)
                    kc = k0 // 128 + jj
                    pst = psts.pop(jj)
                    pT = pmT[kc % 2]
                    self.act(pT.ap, pst.ap, AF.Copy, reads=[pst], writes=[pT])
                    self.mm(acc_o.ap, self.cTok.ap[:, kc, :], pT.ap, kc == 0, kc == nk - 1,
                            reads=[("cTok", kc), pT], writes=[acc_o])
                    yield
            s.op('dve', lambda e: e.tensor_reduce(den.ap, denp.ap, mybir.AxisListType.X, ALU.add),
                 reads=[denp], writes=[den])
            s.op('dve', lambda e: e.reciprocal(den.ap, den.ap), reads=[den], writes=[den])
            self.tt('dve', Dg4.ap, self.identf.unsqueeze(1).to_broadcast([128, 4, 128]),
                    den.ap.unsqueeze(2).to_broadcast([128, 4, 128]), ALU.mult, reads=[self.cf32, den], writes=[Dg4])
            self.mm(acc_r.ap, self.onesf, Dg4.ap.rearrange("p h n -> p (h n)"), True, True,
                    reads=[self.cf32, Dg4], writes=[acc_r])
            self.act(rdbc.ap, acc_r.ap, AF.Copy, reads=[acc_r], writes=[rdbc])
            self.tt('dve', olat.ap, acc_o.ap, rdbc.ap, ALU.mult, reads=[acc_o, rdbc], writes=[olat])
            yield
            for h in range(4):
                self.mm(pso.ap[:, h * 128:(h + 1) * 128], self.wuv[:, h, :], olat.ap[:, h * 128:(h + 1) * 128],
                        h == 0, h == 3, reads=[self.pbf, olat], writes=[pso])
            self.tt('dve', self.yT.ap[:, :, c * 128:(c + 1) * 128], pso.ap.rearrange("p (h n) -> p h n", h=4),
                    szB.ap[:, :, c * 128:(c + 1) * 128], ALU.mult, reads=[pso, szB], writes=[self.yT])
            yield

        def chain(*gens):
            for g in gens:
                for _ in g:
                    yield

        self.interleave(SC(0))
        self.interleave(BI(0), chain(prepQ(), prepZ(), SC(1)))
        for c in range(4):
            first = chain(AT(c), SC(c + 2)) if c + 2 < 4 else AT(c)
            if c + 1 < 4:
                self.interleave(first, BI(c + 1))
            else:
                self.interleave(first)
        self.ps_ring = [0, 1, 2, 3]
        self.dbg_dump(1, t)

    def outproj(self, l, t, src, srct, dst, dstt, do_final):
        s = self.s
        sm = self.small
        mgb = Buf(self.mg.ap, [("mg", i) for i in range(8)])
        if self.dbg and l == 0:
            tmp = self.scr([128, 8, 512], F32)
            self.s.op('dve', lambda e: e.tensor_copy(tmp.ap, self.mg.ap), reads=[("mg", i) for i in range(8)],
                      writes=[tmp])
            for ncn in range(8):
                dstap = self.dbg_d[4, t * 512:(t + 1) * 512, ncn * 128:(ncn + 1) * 128].rearrange("t w -> w t")
                self.s.op('sp', lambda e, dstap=dstap, ncn=ncn: e.dma_start(out=dstap, in_=tmp.ap[:, ncn, :],
                                                                               allow_slow_non_contiguous=True),
                          reads=[tmp], writes=["dbg"], dma=True)
        wos = [self.wnext(('out', l, half * 512, 512)) for half in range(2)]
        if do_final:
            fgb = self.scr([128, D], F32)
            self.dma(fgb.ap, self.fg_d, [], [fgb])
            junk = self.scr([128, D], BF16)
        xc2 = [self.xc[0], self.scr([128, D], F32)]
        for c in range(4):
            gc = 4 * t + c
            r0 = gc * 128
            xb = xc2[c % 2]
            self.dma(xb.ap, src[r0:r0 + 128, :], [(srct, t, c)], [xb])
            for half in range(2):
                wv = wos[half].ap[:, 0:4096].rearrange("p (kc n) -> p kc n", kc=8)
                ps = self.psum()
                for ncn in range(8):
                    self.mm(ps.ap, mgb.ap[:, ncn, c * 128:(c + 1) * 128], wv[:, ncn, :], ncn == 0, ncn == 7,
                            reads=[mgb, wos[half]], writes=[ps])
                self.tt('dve', xb.ap[:, half * 512:(half + 1) * 512], ps.ap, xb.ap[:, half * 512:(half + 1) * 512],
                        ALU.add, reads=[ps, xb], writes=[xb])
            if do_final:
                ss = sm.ap[:, 32 + c:33 + c]
                s.op('dve', lambda e, ss=ss: e.memset(ss, 0.0), reads=[], writes=[("sm", 32 + c)])
                self.act(junk.ap, xb.ap, AF.Square, reads=[xb], writes=[junk, ("sm", 32 + c)], accum_out=ss)
                rs = sm.ap[:, 40 + c:41 + c]
                self.rsqrt(rs, ss, 1.0 / D, EPS, reads=[("sm", 32 + c)], writes=[("sm", 40 + c)])
                self.stt('dve', xb.ap, xb.ap, rs, fgb.ap, ALU.mult, ALU.mult, reads=[xb, ("sm", 40 + c), fgb],
                         writes=[xb])
            self.dma(dst[r0:r0 + 128, :], xb.ap, [xb], [(dstt, t, c)])


def _consts():
    i = np.arange(128)
    ident = np.eye(128, dtype=np.float32)
    tri = (i[:, None] <= i[None, :]).astype(np.float32)
    e64 = np.zeros((128, 128), np.float32)
    e64[64, :] = 1.0
    e127 = np.zeros((128, 128), np.float32)
    e127[127, :] = 1.0
    ones = np.ones((128, 128), np.float32)
    maskneg = np.where(i[None, :] <= i[:, None], 0.0, NEG).astype(np.float32)
    extra = np.zeros((128, 8), np.float32)
    extra[:, 0] = EPS
    extra[:, 1] = 1.0
    cf32 = np.concatenate([ident, tri, e64, e127, ones, extra], axis=1)
    cbf = np.concatenate([ident, tri, maskneg, ones], axis=1)
    return np.ascontiguousarray(cf32), np.ascontiguousarray(cbf)


def _pack_params(NL, norm_g, gm_ln_g, gm_ln_b, gm_w_s, gm_b_s, dsa_kv_g, dsa_w_uk, dsa_w_uv, conv_w, fox_b_f):
    pf = np.zeros((NL, 128, NP32), np.float32)
    pb = np.zeros((NL, 128, NPBF), np.float32)
    for l in range(NL):
        pf[l, :, 0:8] = norm_g[l].reshape(8, 128).T
        pf[l, :, 8:12] = gm_ln_g[l].reshape(4, 128).T
        pf[l, :, 12:16] = gm_ln_b[l].reshape(4, 128).T
        pf[l, :, 16:528] = np.broadcast_to(gm_b_s[l].reshape(1, 512), (128, 512))
        pf[l, :, 528:656] = np.broadcast_to(dsa_kv_g[l].reshape(1, 128), (128, 128))
        pf[l, :, 656:668] = conv_w[l].reshape(3, 4, 128).transpose(2, 1, 0).reshape(128, 12)
        pf[l, :, 668:672] = np.broadcast_to(fox_b_f[l].reshape(1, 4), (128, 4))
        pb[l, :, 0:512] = gm_w_s[l].transpose(2, 0, 1).reshape(128, 512)
        pb[l, :, 512:1024] = dsa_w_uk[l].transpose(2, 0, 1).reshape(128, 512)
        pb[l, :, 1024:1536] = dsa_w_uv[l].transpose(1, 0, 2).reshape(128, 512)
    return pf, pb


_CACHE = {}


def _get_program(S, NL, final_norm, dbg):
    key = (S, NL, final_norm, dbg)
    if key not in _CACHE:
        _CACHE[key] = Builder(S, NL, final_norm, dbg).build()
    return _CACHE[key]


def run_layers(x, norm_g, w_in, gm_ln_g, gm_ln_b, gm_w_s, gm_b_s, dsa_kv_g, dsa_w_uk, dsa_w_uv,
               conv_w, fox_b_f, w_branch, w_out, final_g, final_norm=True, dbg=False):
    x = np.asarray(x, np.float32)
    B, S, _ = x.shape
    NL = int(np.asarray(w_in).shape[0])
    f = lambda a: np.ascontiguousarray(np.asarray(a, np.float32))
    nc = _get_program(S, NL, final_norm, dbg)
    cf32, cbf = _consts()
    pf, pb = _pack_params(NL, f(norm_g), f(gm_ln_g), f(gm_ln_b), f(gm_w_s), f(gm_b_s), f(dsa_kv_g), f(dsa_w_uk),
                          f(dsa_w_uv), f(conv_w), f(fox_b_f))
    fg = np.ascontiguousarray(np.broadcast_to(f(final_g).reshape(1, D), (128, D)))
    shared = dict(w_in=f(w_in), w_br=f(w_branch), w_out=f(w_out), pf32=pf, pbf=pb, cf32=cf32, cbf=cbf, fg=fg)
    in_maps = []
    for b in range(B):
        m = dict(shared)
        m["x"] = np.ascontiguousarray(x[b])
        in_maps.append(m)
    res = run_bass_kernel_spmd(nc, in_maps, core_ids=list(range(B)))
    out = np.stack([np.asarray(r["out"], np.float32) for r in res.results], axis=0)
    if dbg:
        return out, np.stack([np.asarray(r["dbg"], np.float32) for r in res.results], axis=0)
    return out


def kernel(x, norm_g, w_in, gm_ln_g, gm_ln_b, gm_w_s, gm_b_s, dsa_kv_g, dsa_w_uk, dsa_w_uv,
           conv_w, fox_b_f, w_branch, w_out, final_g):
    return run_layers(x, norm_g, w_in, gm_ln_g, gm_ln_b, gm_w_s, gm_b_s, dsa_kv_g, dsa_w_uk, dsa_w_uv,
                      conv_w, fox_b_f, w_branch, w_out, final_g)
```
